# Optimizing a Trainium2 kernel written in Bass

```python
import math
import numpy as np
import jax
import jax.numpy as jnp
from jax import lax

D_MODEL = 1024
BATCH = 16
SEQ = 2048
DEPTH = 2

CTX_LEN = 256
GRID_W = 64
HEAD_DIM = 64
ROPE_HALF = HEAD_DIM // 2
ROPE_THETA = 10000.0
A_HEADS = 8
A_KV_HEADS = 2
B_HEADS = 4
C_HEADS = 16
C_KV_HEADS = 4
WINDOW = 128
Q_BLOCK = 128
D_FF = 2816
N_SUB = 3
HALF_STEP = 0.5
EPS = 1e-6
N_EVEN = (DEPTH + 1) // 2
N_ODD = DEPTH // 2
AB_WIDTHS = (A_HEADS * HEAD_DIM, A_KV_HEADS * HEAD_DIM, A_KV_HEADS * HEAD_DIM,
             B_HEADS * 2 * HEAD_DIM, B_HEADS * 2 * HEAD_DIM, B_HEADS * 2 * HEAD_DIM)
AB_IN = sum(AB_WIDTHS)
AB_OUT = A_HEADS * HEAD_DIM + B_HEADS * 2 * HEAD_DIM
C_WIDTHS = (C_HEADS * HEAD_DIM, C_KV_HEADS * HEAD_DIM, C_KV_HEADS * HEAD_DIM)
C_IN = sum(C_WIDTHS)
C_OUT = C_HEADS * HEAD_DIM

kernel_name = 'hybrid_dit_gqa_diffattn_swa_macaron'


def _rms(x, g):
    xf = x.astype(jnp.float32)
    y = xf * lax.rsqrt(jnp.mean(xf * xf, axis=-1, keepdims=True) + EPS)
    return y.astype(x.dtype) * g


def _pre(x, g, shift, scale):
    return _rms(x, g) * (1 + scale) + shift


def _residual(x, y, g_post, gate, res_w):
    return x + res_w * gate * _rms(y, g_post)


def _swiglu(h, wg, wu, wd):
    return (jax.nn.silu(h @ wg) * (h @ wu)) @ wd


def _split(p, widths):
    out, start = [], 0
    for w in widths:
        out.append(p[..., start:start + w])
        start += w
    return out


def _axial_rope(L):
    rows = L // GRID_W
    row = jnp.repeat(jnp.arange(rows, dtype=jnp.float32), GRID_W)
    col = jnp.tile(jnp.arange(GRID_W, dtype=jnp.float32), rows)
    inv = ROPE_THETA ** (-jnp.arange(0, ROPE_HALF, 2, dtype=jnp.float32) / ROPE_HALF)
    ang = jnp.concatenate([row[:, None] * inv, col[:, None] * inv], axis=-1)
    return jnp.cos(ang), jnp.sin(ang)


def _rope(x, cos, sin):
    shape = (1, x.shape[1]) + (1,) * (x.ndim - 3) + (ROPE_HALF,)
    cs, sn = cos.reshape(shape), sin.reshape(shape)
    x1, x2 = x[..., :ROPE_HALF], x[..., ROPE_HALF:]
    return jnp.concatenate([x1 * cs - x2 * sn, x2 * cs + x1 * sn], axis=-1).astype(x.dtype)


def _softmax_with_sink(s, sink):
    sk = sink[None, :, :, None, None]
    m = jnp.maximum(jnp.max(s, axis=-1, keepdims=True), sk)
    p = jnp.exp(s - m)
    return p / (jnp.sum(p, axis=-1, keepdims=True) + jnp.exp(sk - m))


def _gqa_dense(q, k, v, sink=None):
    B, L, H, dh = q.shape
    Hkv = k.shape[2]
    G = H // Hkv
    nb = L // Q_BLOCK
    scale = dh ** -0.5
    qb = q.reshape(B, nb, Q_BLOCK, Hkv, G, dh).swapaxes(0, 1)

    def one(qblk):
        s = jnp.einsum('bqkgd,bskd->bkgqs', qblk, k).astype(jnp.float32) * scale
        if sink is None:
            p = jax.nn.softmax(s, axis=-1)
        else:
            p = _softmax_with_sink(s, sink.reshape(Hkv, G).astype(jnp.float32))
        return jnp.einsum('bkgqs,bskd->bqkgd', p.astype(v.dtype), v)

    o = lax.map(one, qb)
    return o.swapaxes(0, 1).reshape(B, L, H * dh)


def _diff_attn(q, k, v, lam, sub_gain, lam_init):
    B, L, H, _, dh = q.shape
    nb = L // Q_BLOCK
    scale = dh ** -0.5
    qb = q.reshape(B, nb, Q_BLOCK, H, 2, dh).swapaxes(0, 1)

    def one(qblk):
        s = jnp.einsum('bqhcd,bshcd->bhcqs', qblk, k).astype(jnp.float32) * scale
        p = jax.nn.softmax(s, axis=-1)
        a = p[:, :, 0] - lam * p[:, :, 1]
        return jnp.einsum('bhqs,bshe->bqhe', a.astype(v.dtype), v)

    o = lax.map(one, qb).swapaxes(0, 1).reshape(B, L, H, 2 * dh)
    o = _rms(o, sub_gain) * (1.0 - lam_init)
    return o.reshape(B, L, H * 2 * dh)


def _window_attn(q, k, v, k_ctx, v_ctx, sink):
    B, L, H, dh = q.shape
    Hkv = k.shape[2]
    G = H // Hkv
    C = k_ctx.shape[1]
    nb = L // Q_BLOCK
    band = Q_BLOCK + 2 * WINDOW
    scale = dh ** -0.5
    pad = ((0, 0), (WINDOW, WINDOW), (0, 0), (0, 0))
    k_pad, v_pad = jnp.pad(k, pad), jnp.pad(v, pad)
    sk = sink.reshape(Hkv, G).astype(jnp.float32)

    def one(blk):
        start = blk * Q_BLOCK
        qblk = lax.dynamic_slice_in_dim(q, start, Q_BLOCK, axis=1).reshape(B, Q_BLOCK, Hkv, G, dh)
        kb = lax.dynamic_slice_in_dim(k_pad, start, band, axis=1)
        vb = lax.dynamic_slice_in_dim(v_pad, start, band, axis=1)
        qpos = start + jnp.arange(Q_BLOCK)
        kpos = start - WINDOW + jnp.arange(band)
        valid = (jnp.abs(kpos[None, :] - qpos[:, None]) <= WINDOW) & (kpos >= 0) & (kpos < L)
        s_loc = jnp.einsum('bqkgd,bskd->bkgqs', qblk, kb).astype(jnp.float32) * scale
        s_loc = jnp.where(valid, s_loc, -jnp.inf)
        s_ctx = jnp.einsum('bqkgd,bckd->bkgqc', qblk, k_ctx).astype(jnp.float32) * scale
        p = _softmax_with_sink(jnp.concatenate([s_ctx, s_loc], axis=-1), sk)
        o = (jnp.einsum('bkgqc,bckd->bqkgd', p[..., :C].astype(v.dtype), v_ctx)
             + jnp.einsum('bkgqs,bskd->bqkgd', p[..., C:].astype(v.dtype), vb))
        return o.reshape(B, Q_BLOCK, H * dh)

    o = lax.map(one, jnp.arange(nb))
    return o.swapaxes(0, 1).reshape(B, L, H * dh)


def _split_ab(p):
    B, S, _ = p.shape
    aq, ak, av, bq, bk, bv = _split(p, AB_WIDTHS)
    return (aq.reshape(B, S, A_HEADS, HEAD_DIM), ak.reshape(B, S, A_KV_HEADS, HEAD_DIM),
            av.reshape(B, S, A_KV_HEADS, HEAD_DIM), bq.reshape(B, S, B_HEADS, 2, HEAD_DIM),
            bk.reshape(B, S, B_HEADS, 2, HEAD_DIM), bv.reshape(B, S, B_HEADS, 2 * HEAD_DIM))


def _split_c(p):
    B, S, _ = p.shape
    q, k, v = _split(p, C_WIDTHS)
    return (q.reshape(B, S, C_HEADS, HEAD_DIM), k.reshape(B, S, C_KV_HEADS, HEAD_DIM),
            v.reshape(B, S, C_KV_HEADS, HEAD_DIM))


def _mixer_ab(h_lat, h_ctx, w_in, w_out, q_gain, k_gain, lq1, lk1, lq2, lk2, sub_gain, lam_init,
              cos, sin, ctx_out):
    aq_l, ak_l, av_l, bq_l, bk_l, bv_l = _split_ab(h_lat @ w_in)
    aq_c, ak_c, av_c, bq_c, bk_c, bv_c = _split_ab(h_ctx @ w_in)
    aq_l = _rope(_rms(aq_l, q_gain), cos, sin)
    ak_l = _rope(_rms(ak_l, k_gain), cos, sin)
    ak_c = _rms(ak_c, k_gain)
    bq_l = _rope(bq_l, cos, sin)
    bk_l = _rope(bk_l, cos, sin)
    lam = (jnp.exp(jnp.sum(lq1.astype(jnp.float32) * lk1.astype(jnp.float32)))
           - jnp.exp(jnp.sum(lq2.astype(jnp.float32) * lk2.astype(jnp.float32))) + lam_init)
    a_lat = _gqa_dense(aq_l, jnp.concatenate([ak_c, ak_l], axis=1), jnp.concatenate([av_c, av_l], axis=1))
    b_lat = _diff_attn(bq_l, jnp.concatenate([bk_c, bk_l], axis=1), jnp.concatenate([bv_c, bv_l], axis=1),
                       lam, sub_gain, lam_init)
    y_lat = jnp.concatenate([a_lat, b_lat], axis=-1) @ w_out
    y_ctx = None
    if ctx_out:
        a_ctx = _gqa_dense(_rms(aq_c, q_gain), ak_c, av_c)
        b_ctx = _diff_attn(bq_c, bk_c, bv_c, lam, sub_gain, lam_init)
        y_ctx = jnp.concatenate([a_ctx, b_ctx], axis=-1) @ w_out
    return y_lat, y_ctx


def _mixer_c(h_lat, h_ctx, w_in, w_out, sink, cos, sin, ctx_out):
    q_l, k_l, v_l = _split_c(h_lat @ w_in)
    q_c, k_c, v_c = _split_c(h_ctx @ w_in)
    q_l = _rope(q_l, cos, sin)
    k_l = _rope(k_l, cos, sin)
    y_lat = _window_attn(q_l, k_l, v_l, k_c, v_c, sink) @ w_out
    y_ctx = None
    if ctx_out:
        y_ctx = _gqa_dense(q_c, k_c, v_c, sink) @ w_out
    return y_lat, y_ctx


def setup_inputs(seed: int = 0) -> dict:
    key = jax.random.key(seed)
    ks = jax.random.split(key, 23)
    f32 = jnp.float32

    def nrm(k, shape, scale):
        return jax.random.normal(k, shape, f32) * scale

    def gain(k, shape):
        return 1.0 + nrm(k, shape, 0.02)

    D = D_MODEL
    return {
        'x': nrm(ks[0], (BATCH, SEQ, D), 1.0),
        'c': nrm(ks[1], (BATCH, D), 1.0),
        'ctx': nrm(ks[2], (BATCH, CTX_LEN, D), 1.0),
        'c_ctx': nrm(ks[3], (D,), 1.0),
        'w_mod': nrm(ks[4], (DEPTH, D, N_SUB * 3 * D), 0.5 * D ** -0.5),
        'b_mod': nrm(ks[5], (DEPTH, N_SUB * 3 * D), 0.02),
        'g_pre': gain(ks[6], (DEPTH, N_SUB, D)),
        'g_post': gain(ks[7], (DEPTH, N_SUB, D)),
        'w_ffn_gate': nrm(ks[8], (DEPTH, 2, D, D_FF), D ** -0.5),
        'w_ffn_up': nrm(ks[9], (DEPTH, 2, D, D_FF), D ** -0.5),
        'w_ffn_down': nrm(ks[10], (DEPTH, 2, D_FF, D), D_FF ** -0.5),
        'w_in_ab': nrm(ks[11], (N_EVEN, D, AB_IN), D ** -0.5),
        'w_out_ab': nrm(ks[12], (N_EVEN, AB_OUT, D), AB_OUT ** -0.5),
        'q_gain_a': gain(ks[13], (N_EVEN, HEAD_DIM)),
        'k_gain_a': gain(ks[14], (N_EVEN, HEAD_DIM)),
        'lam_q1': nrm(ks[15], (N_EVEN, HEAD_DIM), 0.1),
        'lam_k1': nrm(ks[16], (N_EVEN, HEAD_DIM), 0.1),
        'lam_q2': nrm(ks[17], (N_EVEN, HEAD_DIM), 0.1),
        'lam_k2': nrm(ks[18], (N_EVEN, HEAD_DIM), 0.1),
        'sub_gain_b': gain(ks[19], (N_EVEN, 2 * HEAD_DIM)),
        'w_in_c': nrm(ks[20], (N_ODD, D, C_IN), D ** -0.5),
        'w_out_c': nrm(ks[21], (N_ODD, C_OUT, D), C_OUT ** -0.5),
        'sink_c': nrm(ks[22], (N_ODD, C_HEADS), 0.5),
    }


def reference(x, c, ctx, c_ctx, w_mod, b_mod, g_pre, g_post, w_ffn_gate, w_ffn_up, w_ffn_down,
              w_in_ab, w_out_ab, q_gain_a, k_gain_a, lam_q1, lam_k1, lam_q2, lam_k2, sub_gain_b,
              w_in_c, w_out_c, sink_c):
    B, L, D = x.shape
    cos, sin = _axial_rope(L)
    x_lat, x_ctx = x, ctx
    for i in range(DEPTH):
        last = i == DEPTH - 1
        mod_l = (jax.nn.silu(c) @ w_mod[i] + b_mod[i]).reshape(B, N_SUB, 3, 1, D)
        mod_c = (jax.nn.silu(c_ctx) @ w_mod[i] + b_mod[i]).reshape(N_SUB, 3, D)

        wg, wu, wd = w_ffn_gate[i, 0], w_ffn_up[i, 0], w_ffn_down[i, 0]
        x_lat = _residual(x_lat, _swiglu(_pre(x_lat, g_pre[i, 0], mod_l[:, 0, 0], mod_l[:, 0, 1]), wg, wu, wd),
                          g_post[i, 0], mod_l[:, 0, 2], HALF_STEP)
        x_ctx = _residual(x_ctx, _swiglu(_pre(x_ctx, g_pre[i, 0], mod_c[0, 0], mod_c[0, 1]), wg, wu, wd),
                          g_post[i, 0], mod_c[0, 2], HALF_STEP)

        h_lat = _pre(x_lat, g_pre[i, 1], mod_l[:, 1, 0], mod_l[:, 1, 1])
        h_ctx = _pre(x_ctx, g_pre[i, 1], mod_c[1, 0], mod_c[1, 1])
        if i % 2 == 0:
            e = i // 2
            lam_init = 0.8 - 0.6 * math.exp(-0.3 * i)
            y_lat, y_ctx = _mixer_ab(h_lat, h_ctx, w_in_ab[e], w_out_ab[e], q_gain_a[e], k_gain_a[e],
                                     lam_q1[e], lam_k1[e], lam_q2[e], lam_k2[e], sub_gain_b[e], lam_init,
                                     cos, sin, not last)
        else:
            o = i // 2
            y_lat, y_ctx = _mixer_c(h_lat, h_ctx, w_in_c[o], w_out_c[o], sink_c[o], cos, sin, not last)
        x_lat = _residual(x_lat, y_lat, g_post[i, 1], mod_l[:, 1, 2], 1.0)
        if not last:
            x_ctx = _residual(x_ctx, y_ctx, g_post[i, 1], mod_c[1, 2], 1.0)

        wg, wu, wd = w_ffn_gate[i, 1], w_ffn_up[i, 1], w_ffn_down[i, 1]
        x_lat = _residual(x_lat, _swiglu(_pre(x_lat, g_pre[i, 2], mod_l[:, 2, 0], mod_l[:, 2, 1]), wg, wu, wd),
                          g_post[i, 2], mod_l[:, 2, 2], HALF_STEP)
        if not last:
            x_ctx = _residual(x_ctx, _swiglu(_pre(x_ctx, g_pre[i, 2], mod_c[2, 0], mod_c[2, 1]), wg, wu, wd),
                              g_post[i, 2], mod_c[2, 2], HALF_STEP)
    return x_lat
```

```python
import math
from contextlib import ExitStack
import numpy as np
import ml_dtypes
import concourse.bass as bass
import concourse.mybir as mybir
from concourse.bass_utils import run_bass_kernel_spmd

F32 = mybir.dt.float32
BF16 = mybir.dt.bfloat16
AF = mybir.ActivationFunctionType
ALU = mybir.AluOpType
AX = mybir.AxisListType

D = 1024
L = 2048
CT = 256
NT = 18
DFF = 2816
NJ = 22
EPS = 1e-6
NB = 2
STAGES = 99
SUB = 99
import os as _os
CT1 = int(_os.environ.get('CT1', '1'))
CT2 = int(_os.environ.get('CT2', '1'))
CT3 = int(_os.environ.get('CT3', '1'))
CT4 = int(_os.environ.get('CT4', '1'))
CT5 = int(_os.environ.get('CT5', '1'))
CT6 = int(_os.environ.get('CT6', '1'))
CT7 = int(_os.environ.get('CT7', '1'))


class Res:
    __slots__ = ("name", "w", "r", "excl")

    def __init__(self, name, excl=False):
        self.name = name
        self.w = None
        self.r = []
        self.excl = excl


class _Rec:
    def __getattr__(self, name):
        return lambda *a, **k: (name, a, k)


_REC = _Rec()


class Prog:
    CE = ("pe", "act", "dve", "pool")

    def __init__(self):
        self.q = {e: [] for e in ("pe", "act", "dve", "pool", "sp")}
        self.cnt = {e: 0 for e in self.CE}
        self.waited = {e: {} for e in self.q}
        self.dcnt = {}

    def _deps(self, eng, reads, writes):
        toks = []
        for r in reads:
            if r.w is not None:
                toks.append(r.w)
            if r.excl:
                for t in r.r:
                    if t[0] != eng:
                        toks.append(t)
        for w in writes:
            if w.w is not None:
                toks.append(w.w)
            for t in w.r:
                toks.append(t)
        need = {}
        for sk, v in toks:
            if sk == "pe" and eng == "pe":
                continue
            if self.waited[eng].get(sk, 0) >= v:
                continue
            need[sk] = max(need.get(sk, 0), v)
        for sk, v in need.items():
            self.waited[eng][sk] = v
            self.q[eng].append(("wait", sk, v))

    def op(self, eng, fn, reads=(), writes=()):
        self._deps(eng, reads, writes)
        self.cnt[eng] += 1
        tok = (eng, self.cnt[eng])
        self.q[eng].append(("op", fn(_REC)))
        for r in reads:
            r.r.append(tok)
        for w in writes:
            w.w = tok
            w.r = []
        return tok

    def dma(self, queue, semkey, fn, reads=(), writes=()):
        self._deps(queue, reads, writes)
        self.dcnt[semkey] = self.dcnt.get(semkey, 0) + 16
        tok = (semkey, self.dcnt[semkey])
        self.q[queue].append(("dma", fn(_REC), semkey))
        for r in reads:
            r.r.append(tok)
        for w in writes:
            w.w = tok
            w.r = []
        return tok

    def barrier(self):
        toks = [(e, c) for e, c in self.cnt.items() if c > 0] + [(k, v) for k, v in self.dcnt.items()]
        for eng in self.q:
            for sk, v in toks:
                if sk == eng:
                    continue
                if self.waited[eng].get(sk, 0) >= v:
                    continue
                self.waited[eng][sk] = v
                self.q[eng].append(("wait", sk, v))

    def emit(self, sems, block):
        def replay(key, eng):
            for it in self.q[key]:
                if it[0] == "wait":
                    eng.wait_ge(sems[it[1]], it[2])
                elif it[0] == "op":
                    nm, a, k = it[1]
                    getattr(eng, nm)(*a, **k).then_inc(sems[key], 1)
                else:
                    nm, a, k = it[1]
                    getattr(eng, nm)(*a, **k).then_inc(sems[it[2]], 16)

        @block.tensor
        def _(e):
            replay("pe", e)

        @block.scalar
        def _(e):
            replay("act", e)

        @block.vector
        def _(e):
            replay("dve", e)

        @block.gpsimd
        def _(e):
            replay("pool", e)

        @block.sync
        def _(e):
            replay("sp", e)


class Builder:
    def __init__(self):
        self.nc = bass.Bass("TRN2", target_bir_lowering=False)
        self.P = Prog()
        self.es = ExitStack()
        self.dsems = []

    def din(self, name, shape, dt=F32):
        return self.nc.dram_tensor(name, list(shape), dt, kind="ExternalInput").ap()

    def sb(self, name, shape, dt):
        return self.es.enter_context(self.nc.sbuf_tensor("s_" + name, list(shape), dt))

    def ps(self, name, shape, dt):
        return self.es.enter_context(self.nc.psum_tensor("p_" + name, list(shape), dt))

    def dsem(self, name):
        self.dsems.append(name)
        return name

    def carve(self, nbytes):
        off = self.ov_off
        assert off % 4 == 0
        self.ov_off += (nbytes + 3) // 4 * 4
        assert self.ov_off <= self.ov_bytes, (self.ov_off, self.ov_bytes)
        return off

    def ovv(self, shape, dt):
        esz = 2 if dt == BF16 else 4
        n = 1
        for s in shape[1:]:
            n *= s
        off = self.carve(n * esz)
        v = self.OV[0:shape[0], off // 2: off // 2 + n * esz // 2]
        if dt != BF16:
            v = v.bitcast(dt)
        if len(shape) == 3:
            v = v.rearrange("p (a b) -> p a b", a=shape[1])
        elif len(shape) == 4:
            v = v.rearrange("p (a b c) -> p a b c", a=shape[1], b=shape[2])
        return v

    @staticmethod
    def bcast(row_ap, n):
        ln = row_ap.shape[-1]
        return bass.AP(row_ap.tensor, row_ap.offset, [[0, n], [1, ln]])

    def build(self):
        nc, P = self.nc, self.P
        self.x_d = self.din("x", [NB, L, D])
        self.ctx_d = self.din("ctx", [NB, CT, D])
        self.cvec_d = self.din("cvec", [3, D])
        self.wmod_d = self.din("w_mod", [2, D, 9 * D])
        self.bmod_d = self.din("b_mod", [2, 9 * D])
        self.gpre_d = self.din("g_pre", [2, 3, D])
        self.gpost_d = self.din("g_post", [2, 3, D])
        self.wg_d = self.din("w_ffn_gate", [2, 2, D, DFF])
        self.wu_d = self.din("w_ffn_up", [2, 2, D, DFF])
        self.wd_d = self.din("w_ffn_down", [2, 2, DFF, D])
        self.winab_d = self.din("w_in_ab", [D, 2304])
        self.woutab_d = self.din("w_out_ab", [D, D])
        self.qg_d = self.din("q_gain_a", [1, 64])
        self.kg_d = self.din("k_gain_a", [1, 64])
        self.lam_d = self.din("lam4", [4, 64])
        self.sg_d = self.din("sub_gain_b", [128, 1])
        self.winc_d = self.din("w_in_c", [D, 1536])
        self.woutc_d = self.din("w_out_c", [D, D])
        self.sink_d = self.din("sink_c", [1, 16])
        self.identb_d = self.din("ident_bf", [128, 128], BF16)
        self.identf_d = self.din("ident_f", [128, 128], F32)
        self.cs_d = self.din("cs", [128, NT, 64])
        self.mask_d = self.din("maskb", [128, 2, 512], BF16)
        self.out_d = nc.dram_tensor("out", [NB, L, D], F32, kind="ExternalOutput").ap()
        self.modd = nc.dram_tensor("modd", [2, 3, 9 * D], F32).ap()

        self.X = self.sb("X", [128, NT, D], F32)
        self.rX = [Res(f"x{t}") for t in range(NT)]
        self.identb = self.sb("identb", [128, 128], BF16)
        self.onesb = self.sb("onesb", [128, 128], BF16)
        self.cs = self.sb("cs", [128, NT, 64], F32)
        self.rConst = Res("const")
        self.gm = self.sb("gm", [128, D], F32)
        self.sh = self.sb("sh", [128, D], F32)
        self.gg = self.sb("gg", [128, D], F32)
        self.rgm, self.rsh, self.rgg = Res("gm"), Res("sh"), Res("gg")
        self.st = self.sb("st", [128, 64], F32)
        self.rst = [Res(f"st{k}") for k in range(8)]
        self.lamt = self.sb("lamt", [128, 8], F32)
        self.rlam = Res("lam")
        self.sgc = self.sb("sgc", [128, 1], F32)
        self.qgt = self.sb("qgt", [128, 64], F32)
        self.kgt = self.sb("kgt", [128, 64], F32)
        self.skt = self.sb("skt", [128, 16], F32)
        self.ov_bytes = 116096 + 4096 + 512
        self.OV = self.sb("OV", [128, self.ov_bytes // 2], BF16)
        self.ov_off = 0

        self.PY = [self.ps("py0", [128, 1024], F32), self.ps("py1", [128, 1024], F32)]
        self.rPY = [[Res("py0a", True), Res("py0b", True)], [Res("py1a", True), Res("py1b", True)]]
        self.PT = self.ps("pt", [128, 8, 128], BF16)
        self.rPT = Res("pt", True)
        self.PA = self.ps("pa", [128, 512], F32)
        self.PB = self.ps("pb", [128, 512], F32)
        self.PC = self.ps("pc", [128, 512], F32)
        self.rPA, self.rPB, self.rPC = Res("pa", True), Res("pb", True), Res("pc", True)

        for k in ("ld0", "ld1", "ld2", "ld3", "ld4", "ld5", "st2", "st3", "bc0", "bc1", "bc2", "bc3", "w0", "w1", "w2", "w3", "wd", "wq", "wo",
                  "st0", "st1", "ms0", "ms1"):
            self.dsem(k)

        self.consts()
        self.stage_mod()
        P.barrier()
        for b in range(NB):
            self.load_x(b)
            if STAGES >= 1:
                self.ffn(0, 0, 0, b, list(range(NT)))
            P.barrier()
            if STAGES >= 2:
                self.mixer_ab(b)
                P.barrier()
            if STAGES >= 3:
                self.ffn(0, 1, 2, b, list(range(NT)))
                P.barrier()
            if STAGES >= 4:
                self.ffn(1, 0, 0, b, list(range(NT)))
                P.barrier()
            if STAGES >= 5:
                self.mixer_c(b)
                P.barrier()
            if STAGES >= 6:
                self.ffn(1, 1, 2, b, list(range(2, NT)))
                P.barrier()
            self.store_x(b)
        P.barrier()

        sems = {}
        for k in ("pe", "act", "dve", "pool"):
            sems[k] = self.es.enter_context(nc.semaphore(k))
        for k in self.dsems:
            sems[k] = self.es.enter_context(nc.semaphore(k))
        block = self.es.enter_context(nc.Block())
        P.emit(sems, block)
        self.es.close()
        return nc

    def consts(self):
        P = self.P
        rc = self.rConst
        P.dma("sp", "ld5", lambda e: e.dma_start(out=self.identb[:], in_=self.identb_d[:, :]), writes=[rc])
        P.dma("sp", "ld5", lambda e: e.dma_start(out=self.cs[:], in_=self.cs_d[:, :, :]), writes=[rc])
        P.dma("sp", "ld5", lambda e: e.dma_start(out=self.sgc[:], in_=self.sg_d[:, :]), writes=[rc])
        P.dma("sp", "ld5", lambda e: e.dma_start(out=self.qgt[:], in_=self.bcast(self.qg_d[0:1, :], 128)), writes=[rc])
        P.dma("sp", "ld5", lambda e: e.dma_start(out=self.kgt[:], in_=self.bcast(self.kg_d[0:1, :], 128)), writes=[rc])
        P.dma("sp", "ld5", lambda e: e.dma_start(out=self.skt[:], in_=self.bcast(self.sink_d[0:1, :], 128)), writes=[rc])
        P.op("dve", lambda e: e.memset(self.onesb[:], 1.0), writes=[rc])

    def stage_mod(self):
        P = self.P
        self.ov_off = 0
        NR = 6
        identf = self.ovv([128, 128], F32)
        cv = self.ovv([128, D], F32)
        sc = self.ovv([128, D], F32)
        scT = self.ovv([128, 8, 4], F32)
        wm = [self.ovv([128, 8, 512], F32) for _ in range(NR)]
        bm = [self.ovv([128, 512], F32) for _ in range(2)]
        ms = [self.ovv([128, 512], F32) for _ in range(2)]
        rcv, rsc, rscT, rid = Res("cv"), Res("sc"), Res("scT"), Res("identf")
        rwm = [Res(f"wm{k}") for k in range(NR)]
        rbm = [Res("bm0"), Res("bm1")]
        rms = [Res("ms0"), Res("ms1")]
        for k in range(NR):
            self.dsem(f"wm{k}")
        ptf = self.PT[:].rearrange("p a b -> p (a b)").bitcast(F32).rearrange("p (a b) -> p a b", a=8)
        P.dma("sp", "ld1", lambda e: e.dma_start(out=cv[0:3, :], in_=self.cvec_d[:, :]), writes=[rcv])
        P.dma("sp", "ld3", lambda e: e.dma_start(out=identf[:, :], in_=self.identf_d[:, :]), writes=[rid])
        slabs = [(i, s_) for i in range(2) for s_ in range(18)]

        def issue(n):
            i, s_ = slabs[n]
            k = n % NR
            wsrc = self.wmod_d[i, :, s_ * 512:(s_ + 1) * 512].rearrange("(kc p) f -> p kc f", p=128)
            P.dma("sp", f"wm{k}", lambda e: e.dma_start(out=wm[k][:], in_=wsrc), writes=[rwm[k]])
        for n in range(NR - 1):
            issue(n)
        P.op("act", lambda e: e.activation(out=sc[0:3, :], in_=cv[0:3, :], func=AF.Silu), reads=[rcv], writes=[rsc])
        for kc in range(8):
            P.op("pe", lambda e, kc=kc: e.transpose(out=ptf[:, kc, 0:3], in_=sc[0:3, kc * 128:(kc + 1) * 128],
                                                   identity=identf[0:3, 0:3]),
                 reads=[rsc, rid], writes=[self.rPT])
        P.op("dve", lambda e: e.tensor_copy(out=scT[:, :, 0:3], in_=ptf[:, :, 0:3]), reads=[self.rPT], writes=[rscT])
        for n, (i, s_) in enumerate(slabs):
            if n + NR - 1 < len(slabs):
                issue(n + NR - 1)
            k = n % 2
            kw = n % NR
            bsrc = self.bcast(self.bmod_d[i:i + 1, s_ * 512:(s_ + 1) * 512], 3)
            P.dma("sp", f"bc{k}", lambda e, k=k, bsrc=bsrc: e.dma_start(out=bm[k][0:3, :], in_=bsrc), writes=[rbm[k]])
            mp = self.PA if k == 0 else self.PB
            rmp = self.rPA if k == 0 else self.rPB
            for kc in range(8):
                P.op("pe", lambda e, kc=kc: e.matmul(mp[0:3, :], lhsT=scT[:, kc, 0:3], rhs=wm[kw][:, kc, :], start=(kc == 0), stop=(kc == 7)),
                     reads=[rscT, rwm[kw]], writes=[rmp])
            P.op("dve", lambda e: e.tensor_tensor(out=ms[k][0:3, :], in0=mp[0:3, :], in1=bm[k][0:3, :], op=ALU.add),
                 reads=[rmp, rbm[k]], writes=[rms[k]])
            dst = self.modd[i, :, s_ * 512:(s_ + 1) * 512]
            P.dma("sp", f"ms{k}", lambda e: e.dma_start(out=dst, in_=ms[k][0:3, :]), reads=[rms[k]])
        lt = self.ovv([128, 4, 64], F32)
        rlt = Res("lt")
        for r in range(4):
            P.dma("sp", "ld2", lambda e, r=r: e.dma_start(out=lt[:, r, :], in_=self.bcast(self.lam_d[r:r + 1, :], 128)), writes=[rlt])
        P.op("dve", lambda e: e.tensor_tensor(out=lt[:, 0, :], in0=lt[:, 0, :], in1=lt[:, 1, :], op=ALU.mult), reads=[rlt], writes=[rlt])
        P.op("dve", lambda e: e.tensor_tensor(out=lt[:, 2, :], in0=lt[:, 2, :], in1=lt[:, 3, :], op=ALU.mult), reads=[rlt], writes=[rlt])
        P.op("dve", lambda e: e.tensor_reduce(out=self.lamt[:, 0:1], in_=lt[:, 0, :], axis=AX.X, op=ALU.add), reads=[rlt], writes=[self.rlam])
        P.op("dve", lambda e: e.tensor_reduce(out=self.lamt[:, 1:2], in_=lt[:, 2, :], axis=AX.X, op=ALU.add), reads=[rlt], writes=[self.rlam])
        P.op("act", lambda e: e.activation(out=self.lamt[:, 2:4], in_=self.lamt[:, 0:2], func=AF.Exp), reads=[self.rlam], writes=[self.rlam])
        lam_init = 0.8 - 0.6 * math.exp(-0.3 * 0)
        P.op("dve", lambda e: e.scalar_tensor_tensor(out=self.lamt[:, 4:5], in0=self.lamt[:, 3:4], scalar=-lam_init, in1=self.lamt[:, 2:3],
                                                      op0=ALU.add, op1=ALU.subtract), reads=[self.rlam], writes=[self.rlam])
        P.op("dve", lambda e: e.tensor_scalar(out=self.sgc[:], in0=self.sgc[:], scalar1=1.0 - lam_init, scalar2=None, op0=ALU.mult),
             reads=[self.rConst], writes=[self.rConst])
        P.op("act", lambda e: e.activation(out=self.skt[:], in_=self.skt[:], func=AF.Exp), reads=[self.rConst], writes=[self.rConst])

    def load_x(self, b):
        P = self.P
        P.dma("sp", "ld4", lambda e: e.dma_start(out=self.X[:, 0:2, :], in_=self.ctx_d[b].rearrange("(t p) d -> p t d", p=128)),
              writes=self.rX[0:2])
        for q in range(4):
            t0 = 2 + 4 * q
            P.dma("sp", f"ld{q}", lambda e, q=q, t0=t0: e.dma_start(
                out=self.X[:, t0:t0 + 4, :], in_=self.x_d[b, q * 512:(q + 1) * 512, :].rearrange("(t p) d -> p t d", p=128)),
                writes=self.rX[t0:t0 + 4])

    def store_x(self, b):
        P = self.P
        for q in range(4):
            t0 = 2 + 4 * q
            P.dma("sp", f"st{q}", lambda e, q=q, t0=t0: e.dma_start(
                out=self.out_d[b, q * 512:(q + 1) * 512, :].rearrange("(t p) d -> p t d", p=128), in_=self.X[:, t0:t0 + 4, :]),
                reads=self.rX[t0:t0 + 4])

    def bc_row(self, dst, rdst, src_row, sem):
        self.P.dma("sp", sem, lambda e: e.dma_start(out=dst[:, :], in_=self.bcast(src_row, 128)), writes=[rdst])

    def prep_pre(self, i, slot, r):
        P = self.P
        base = slot * 3 * D
        self.bc_row(self.gm, self.rgm, self.modd[i, r:r + 1, base + D: base + 2 * D], "bc0")
        self.bc_row(self.m_tmpf, self.m_rtmpf, self.gpre_d[i, slot:slot + 1, :], "bc1")
        self.bc_row(self.sh, self.rsh, self.modd[i, r:r + 1, base: base + D], "bc2")
        P.op("dve", lambda e: e.scalar_tensor_tensor(out=self.gm[:], in0=self.gm[:], scalar=1.0, in1=self.m_tmpf[:], op0=ALU.add, op1=ALU.mult),
             reads=[self.rgm, self.m_rtmpf], writes=[self.rgm])

    def prep_post(self, i, slot, r, res_w):
        P = self.P
        base = slot * 3 * D
        self.bc_row(self.gg, self.rgg, self.modd[i, r:r + 1, base + 2 * D: base + 3 * D], "bc3")
        self.bc_row(self.m_tmpf, self.m_rtmpf, self.gpost_d[i, slot:slot + 1, :], "bc1")
        P.op("dve", lambda e: e.scalar_tensor_tensor(out=self.gg[:], in0=self.gg[:], scalar=float(res_w), in1=self.m_tmpf[:], op0=ALU.mult, op1=ALU.mult),
             reads=[self.rgg, self.m_rtmpf], writes=[self.rgg])

    def rstd_cols(self, srcs, col0, rk, inv_n):
        P = self.P
        n = len(srcs)
        for j, (ap, rr, junk, rjunk) in enumerate(srcs):
            P.op("act", lambda e, ap=ap, j=j, junk=junk: e.activation(out=junk, in_=ap, func=AF.Square, accum_out=self.st[:, col0 + j:col0 + j + 1]),
                 reads=rr, writes=[rjunk, rk])
        P.op("dve", lambda e: e.tensor_scalar(out=self.st[:, col0:col0 + n], in0=self.st[:, col0:col0 + n], scalar1=float(inv_n), scalar2=EPS,
                                               op0=ALU.mult, op1=ALU.add), reads=[rk], writes=[rk])
        P.op("act", lambda e: e.activation(out=self.st[:, col0:col0 + n], in_=self.st[:, col0:col0 + n], func=AF.Sqrt), reads=[rk], writes=[rk])
        P.op("dve", lambda e: e.reciprocal(out=self.st[:, col0:col0 + n], in_=self.st[:, col0:col0 + n]), reads=[rk], writes=[rk])

    def prenorm_tile(self, t, rcol, rk, tmpf, rtmpf, hb, rhb, dstT, rdstT):
        P = self.P
        P.op("dve", lambda e: e.scalar_tensor_tensor(out=tmpf, in0=self.X[:, t, :], scalar=rcol, in1=self.gm[:], op0=ALU.mult, op1=ALU.mult),
             reads=[self.rX[t], rk, self.rgm], writes=[rtmpf])
        P.op("dve", lambda e: e.tensor_tensor(out=hb, in0=tmpf, in1=self.sh[:], op=ALU.add), reads=[rtmpf, self.rsh], writes=[rhb])
        for kc in range(8):
            P.op("pe", lambda e, kc=kc: e.transpose(out=self.PT[:, kc, :], in_=hb[:, kc * 128:(kc + 1) * 128], identity=self.identb[:]),
                 reads=[rhb, self.rConst], writes=[self.rPT])
        P.op("act", lambda e: e.activation(out=dstT, in_=self.PT[:], func=AF.Copy), reads=[self.rPT], writes=[rdstT])

    def postnorm_tile(self, t, y_ap, ry, tmpf, rtmpf, col, rk):
        P = self.P
        junk = tmpf.bitcast(BF16)[:, 0:1024]
        self.rstd_cols([(y_ap, ry, junk, rtmpf)], col, rk, 1.0 / D)
        P.op("dve", lambda e: e.scalar_tensor_tensor(out=tmpf, in0=y_ap, scalar=self.st[:, col:col + 1], in1=self.gg[:], op0=ALU.mult, op1=ALU.mult),
             reads=ry + [rk, self.rgg], writes=[rtmpf])
        P.op("dve", lambda e: e.tensor_tensor(out=self.X[:, t, :], in0=self.X[:, t, :], in1=tmpf, op=ALU.add), reads=[rtmpf, self.rX[t]], writes=[self.rX[t]])

    def ffn(self, i, widx, slot, b, tiles):
        P = self.P
        self.ov_off = 0
        TP = 6
        hT = self.ovv([128, 8, TP * 128], BF16)
        aT = self.ovv([128, NJ, TP * 128], BF16)
        wdt = self.ovv([128, NJ, D], BF16)
        ring = [(self.ovv([128, 8, 256], BF16), self.ovv([128, 8, 256], BF16)) for _ in range(2)]
        tmpf = [self.ovv([128, D], F32), self.ovv([128, D], F32)]
        hb = [self.ovv([128, D], BF16)] * 2
        sgb = [self.ovv([128, 512], F32)] * 2
        rhT = [Res(f"hT{j}") for j in range(TP)]
        raT = [Res(f"aT{j}") for j in range(NJ)]
        rwd = Res("wd")
        rring = [Res("ring0"), Res("ring1")]
        rtmpf = [Res("tmpf0"), Res("tmpf1")]
        rhb = [Res("hb0")] * 2
        rsgb = [Res("sgb0")] * 2
        self.m_tmpf, self.m_rtmpf = tmpf[0], rtmpf[0]
        for q in range(4):
            j0, j1 = q * 6, min(NJ, q * 6 + 6)
            src = self.wd_d[i, widx, j0 * 128:j1 * 128, :].rearrange("(j p) d -> p j d", p=128)
            P.dma("pool", "wd", lambda e, j0=j0, j1=j1, src=src: e.dma_start(out=wdt[:, j0:j1, :], in_=src), writes=[rwd])
        passes = [tiles[k:k + TP] for k in range(0, len(tiles), TP)]
        self._cur_row = None
        slab_ctr = [0]

        def issue_slab(sidx, s_):
            k = sidx % 2
            gsrc = self.wg_d[i, widx, :, s_ * 256:(s_ + 1) * 256].rearrange("(kc p) f -> p kc f", p=128)
            usrc = self.wu_d[i, widx, :, s_ * 256:(s_ + 1) * 256].rearrange("(kc p) f -> p kc f", p=128)
            P.dma("pool", f"w{k}", lambda e: e.dma_start(out=ring[k][0][:], in_=gsrc), writes=[rring[k]])
            P.dma("pool", f"w{k}", lambda e: e.dma_start(out=ring[k][1][:], in_=usrc), writes=[rring[k]])

        def p1_stats(ptiles, pi):
            col0 = 0 if pi % 2 == 0 else 16
            rk = self.rst[0] if pi % 2 == 0 else self.rst[6]
            srcs = [(self.X[:, t, :], [self.rX[t]], hb[0], rhb[0]) for t in ptiles]
            self.rstd_cols(srcs, col0, rk, 1.0 / D)
            return col0, rk

        def p1_tile(j, t, col0, rk):
            row = 2 if t < 2 else b
            if row != self._cur_row:
                self.prep_pre(i, slot, row)
                self._cur_row = row
            self.prenorm_tile(t, self.st[:, col0 + j:col0 + j + 1], rk, tmpf[0], rtmpf[0], hb[0], rhb[0],
                              hT[:, :, j * 128:(j + 1) * 128], rhT[j])

        c0_, rk_ = p1_stats(passes[0], 0)
        for j, t in enumerate(passes[0]):
            p1_tile(j, t, c0_, rk_)
        for pi, ptiles in enumerate(passes):
            npt = len(ptiles)
            T = npt * 128
            nblk = 2 if T > 512 else 1
            bs = T // nblk
            issue_slab(slab_ctr[0], 0)
            for s_ in range(11):
                if s_ + 1 < 11:
                    issue_slab(slab_ctr[0] + 1, s_ + 1)
                k = slab_ctr[0] % 2
                for jj in range(2):
                    j = s_ * 2 + jj
                    for nb in range(nblk):
                        pk = (j * nblk + nb) % 2
                        g_ps = self.PY[pk][:, 0:bs]
                        u_ps = self.PY[pk][:, 512:512 + bs]
                        tl = list(range(nb * bs // 128, (nb + 1) * bs // 128))
                        rh = [rhT[q] for q in tl]
                        for kc in range(8):
                            P.op("pe", lambda e: e.matmul(g_ps, lhsT=ring[k][0][:, kc, jj * 128:(jj + 1) * 128], rhs=hT[:, kc, nb * bs:(nb + 1) * bs],
                                                          start=(kc == 0), stop=(kc == 7)), reads=rh + [rring[k]], writes=[self.rPY[pk][0]])
                        for kc in range(8):
                            P.op("pe", lambda e: e.matmul(u_ps, lhsT=ring[k][1][:, kc, jj * 128:(jj + 1) * 128], rhs=hT[:, kc, nb * bs:(nb + 1) * bs],
                                                          start=(kc == 0), stop=(kc == 7)), reads=rh + [rring[k]], writes=[self.rPY[pk][1]])
                        P.op("act", lambda e: e.activation(out=sgb[pk][:, 0:bs], in_=g_ps, func=AF.Silu), reads=[self.rPY[pk][0]], writes=[rsgb[pk]])
                        P.op("dve", lambda e: e.tensor_tensor(out=aT[:, j, nb * bs:(nb + 1) * bs], in0=sgb[pk][:, 0:bs], in1=u_ps, op=ALU.mult),
                             reads=[rsgb[pk], self.rPY[pk][1]], writes=[raT[j]])
                slab_ctr[0] += 1
            nxt = passes[pi + 1] if pi + 1 < len(passes) else None
            if nxt is not None:
                c0n, rkn = p1_stats(nxt, pi + 1)
            for j, t in enumerate(ptiles):
                row = 2 if t < 2 else b
                pk = j % 2
                y = self.PY[pk]
                for half in range(2):
                    for jf in range(NJ):
                        P.op("pe", lambda e: e.matmul(y[:, half * 512:(half + 1) * 512], lhsT=aT[:, jf, j * 128:(j + 1) * 128],
                                                      rhs=wdt[:, jf, half * 512:(half + 1) * 512], start=(jf == 0), stop=(jf == NJ - 1)),
                             reads=[raT[jf], rwd], writes=[self.rPY[pk][half]])
                if nxt is not None and j < len(nxt):
                    p1_tile(j, nxt[j], c0n, rkn)
                self._post_row(i, slot, row, 0.5)
                self.postnorm_tile(t, y[:, :], [self.rPY[pk][0], self.rPY[pk][1]], tmpf[1], rtmpf[1], 8 + pk, self.rst[1 + pk])
        self._post_state = None

    _post_state = None

    def _post_row(self, i, slot, row, res_w):
        key = (i, slot, row)
        if self._post_state != key:
            self.prep_post(i, slot, row, res_w)
            self._post_state = key

    def rsqrt_inplace(self, ap, res, inv_n):
        P = self.P
        P.op("dve", lambda e: e.tensor_scalar(out=ap, in0=ap, scalar1=float(inv_n), scalar2=EPS, op0=ALU.mult, op1=ALU.add), reads=[res], writes=[res])
        P.op("act", lambda e: e.activation(out=ap, in_=ap, func=AF.Sqrt), reads=[res], writes=[res])
        P.op("dve", lambda e: e.reciprocal(out=ap, in_=ap), reads=[res], writes=[res])

    def rope(self, src3, rsrc, t, nh, dst3, rdst, ra, rb, rra, rrb):
        P = self.P
        cos = self.cs[:, t, 0:32].unsqueeze(1).to_broadcast([128, nh, 32])
        sin = self.cs[:, t, 32:64].unsqueeze(1).to_broadcast([128, nh, 32])
        x1, x2 = src3[:, :, 0:32], src3[:, :, 32:64]
        a, b_ = ra[:, 0:nh, :], rb[:, 0:nh, :]
        P.op("dve", lambda e: e.tensor_tensor(out=a, in0=x1, in1=cos, op=ALU.mult), reads=rsrc + [self.rConst], writes=[rra])
        P.op("dve", lambda e: e.tensor_tensor(out=b_, in0=x2, in1=sin, op=ALU.mult), reads=rsrc + [self.rConst], writes=[rrb])
        P.op("pool", lambda e: e.tensor_tensor(out=dst3[:, :, 0:32], in0=a, in1=b_, op=ALU.subtract), reads=[rra, rrb], writes=[rdst])
        P.op("dve", lambda e: e.tensor_tensor(out=a, in0=x2, in1=cos, op=ALU.mult), reads=rsrc + [self.rConst], writes=[rra])
        P.op("dve", lambda e: e.tensor_tensor(out=b_, in0=x1, in1=sin, op=ALU.mult), reads=rsrc + [self.rConst], writes=[rrb])
        P.op("pool", lambda e: e.tensor_tensor(out=dst3[:, :, 32:64], in0=a, in1=b_, op=ALU.add), reads=[rra, rrb], writes=[rdst])

    def head_norm(self, ps2, rps, nh, gain_t, sq, rsq, col0, rk):
        P = self.P
        v3 = ps2.rearrange("p (h d) -> p h d", h=nh)
        sq2 = sq[:, 0:nh * 64]
        P.op("act", lambda e: e.activation(out=sq2, in_=ps2, func=AF.Square), reads=rps, writes=[rsq])
        P.op("dve", lambda e: e.tensor_reduce(out=self.st[:, col0:col0 + nh], in_=sq2.rearrange("p (h d) -> p h d", h=nh), axis=AX.X, op=ALU.add),
             reads=[rsq], writes=[rk])
        self.rsqrt_inplace(self.st[:, col0:col0 + nh], rk, 1.0 / 64)
        rbc = self.st[:, col0:col0 + nh].unsqueeze(2).to_broadcast([128, nh, 64])
        gbc = gain_t[:, :].unsqueeze(1).to_broadcast([128, nh, 64])
        P.op("dve", lambda e: e.tensor_tensor(out=v3, in0=v3, in1=rbc, op=ALU.mult), reads=rps + [rk], writes=rps)
        P.op("dve", lambda e: e.tensor_tensor(out=v3, in0=v3, in1=gbc, op=ALU.mult), reads=rps + [self.rConst], writes=rps)

    def attend(self, jobs, mode="ab", bg=None, bg_every=4):
        P = self.P
        steps = []
        for ji, jb in enumerate(jobs):
            for ki, kt in enumerate(jb["kts"]):
                steps.append((ji, ki, kt))
        ptf = self.PT[:].rearrange("p a b -> p (a b)").bitcast(F32)
        if mode == "ab":
            Sb = [(self.PA, self.rPA), (self.PB, self.rPB), (ptf, self.rPT)]
            Ob = [(self.PC, self.rPC), (self.PY[0][:, 0:512], self.rPY[0][0])]
            Rb = [(self.PY[1][:, 0:512], self.rPY[1][0]), (self.PY[0][:, 512:1024], self.rPY[0][1])]
        else:
            Sb = [(self.PA, self.rPA), (self.PB, self.rPB)]
            Ob = [(self.PC, self.rPC), (self.PY[1][:, 0:512], self.rPY[1][0])]
            Rb = [(None, None), (None, None)]
        NS = len(Sb)
        LA = NS - 1

        def emit_S(si):
            ji, ki, kt = steps[si]
            jb = jobs[ji]
            S, rS = Sb[si % NS]
            nq = jb["nq"]
            mk = jb["mask"](kt) if jb.get("mask") else None
            P.op("pe", lambda e: e.matmul(S[:, 0:nq], lhsT=jb["kT"](kt), rhs=jb["q"], start=True, stop=(mk is None)),
                 reads=jb["rk"] + jb["rq"], writes=[rS])
            if mk is not None:
                P.op("pe", lambda e: e.matmul(S[:, 0:nq], lhsT=self.identb[:], rhs=mk, start=False, stop=True),
                     reads=[self.rConst], writes=[rS])
            Pt, rPt = self.Pring[si % NS]
            P.op("act", lambda e: e.activation(out=Pt[:, 0:nq], in_=S[:, 0:nq], func=AF.Exp, scale=0.125), reads=[rS], writes=[rPt])

        for si in range(min(LA, len(steps))):
            emit_S(si)
        bg = list(bg) if bg else []
        for si, (ji, ki, kt) in enumerate(steps):
            if bg and si % bg_every == 2:
                bg.pop(0)()
            if si + LA < len(steps):
                emit_S(si + LA)
            jb = jobs[ji]
            nq = jb["nq"]
            O, rO = Ob[ji % 2]
            Rs, rR = Rb[ji % 2]
            Pt, rPt = self.Pring[si % NS]
            first, last = ki == 0, ki == len(jb["kts"]) - 1
            P.op("pe", lambda e: e.matmul(O[:, 0:nq], lhsT=jb["v"](kt), rhs=Pt[:, 0:nq], start=first, stop=last),
                 reads=[rPt] + jb["rv"], writes=[rO])
            if jb["rs_sep"]:
                P.op("pe", lambda e: e.matmul(Rs[:, 0:nq], lhsT=self.onesb[:, :], rhs=Pt[:, 0:nq], start=first, stop=last),
                     reads=[rPt, self.rConst], writes=[rR])
            if last:
                jb["fin"](O, rO, Rs, rR)
        while bg:
            bg.pop(0)()

    @staticmethod
    def skewed(n, stages):
        ns = len(stages)
        out = []
        for s_ in range(n + ns - 1):
            for k in reversed(range(ns)):
                j = s_ - k
                if 0 <= j < n:
                    out.append(lambda k=k, j=j: stages[k](j))
        return out

    def outproj_mm(self, j, OT, rOT, wo, rwo, nchunk, yk=1):
        P = self.P
        y = self.PY[yk]
        for half in range(2):
            for c in range(nchunk):
                P.op("pe", lambda e: e.matmul(y[:, half * 512:(half + 1) * 512], lhsT=OT[:, c, j * 128:(j + 1) * 128],
                                              rhs=wo[:, c, half * 512:(half + 1) * 512], start=(c == 0), stop=(c == nchunk - 1)),
                     reads=[rOT, rwo], writes=[self.rPY[yk][half]])

    def outproj_post(self, t, i, row, col, yk=1, tmp=None, rtmp=None):
        y = self.PY[yk]
        self._post_row(i, 1, row, 1.0)
        self.postnorm_tile(t, y[:, :], [self.rPY[yk][0], self.rPY[yk][1]], tmp if tmp is not None else self.m_tmpf,
                           rtmp if rtmp is not None else self.m_rtmpf, col, self.rst[3])

    def outproj_tile(self, t, j, OT, rOT, wo, rwo, nchunk, i, row, col):
        self.outproj_mm(j, OT, rOT, wo, rwo, nchunk)
        self.outproj_post(t, i, row, col)

    def mixer_ab(self, b):
        P = self.P
        self.ov_off = 0
        self._post_state = None
        T = NT * 128
        KTA = [self.ovv([128, T], BF16), self.ovv([128, T], BF16)]
        KTB = self.ovv([128, 4, T], BF16)
        VA = self.ovv([128, NT, 192], BF16)
        VB = self.ovv([128, NT, 512], BF16)
        wq = self.ovv([128, 8, 1024], BF16)
        wx = self.ovv([128, 8, 256], BF16)
        wo = self.ovv([128, 8, D], BF16)
        tmpf = self.ovv([128, D], F32)
        hb = self.ovv([128, D], BF16)
        hTt = self.ovv([128, 8, 128], BF16)
        ra = self.ovv([128, 16, 32], F32)
        rb = self.ovv([128, 16, 32], F32)
        QT = self.ovv([128, 8, 512], BF16)
        OT = self.ovv([128, 8, 512], BF16)
        rK, rV, rwq, rwx, rwo = Res("K"), Res("V"), Res("wq"), Res("wx"), Res("wo")
        rtmpf, rhb, rhTt, rra, rrb, rQT, rOT = (Res(n) for n in ("tmpf", "hb", "hTt", "ra", "rb", "QT", "OT"))
        qrot, rqrot = hb, rhb
        QTBhi = wx.rearrange("p a b -> p (a b)").rearrange("p (h t) -> p h t", h=4)
        self.Pring = [(self.ovv([128, 512], BF16), Res("p0")), (self.ovv([128, 512], BF16), Res("p1")),
                      (hTt.rearrange("p a b -> p (a b)")[:, 0:512], rhTt)]
        self.m_tmpf, self.m_rtmpf = tmpf, rtmpf
        rinv = tmpf[:, 0:512]
        o1 = tmpf[:, 512:1024]
        sqf = ra.rearrange("p a b -> p (a b)")
        dsq = rb.rearrange("p a b -> p (a b)").bitcast(BF16)[:, 0:512]
        rsd = ra.rearrange("p a b -> p (a b)")
        W = self.winab_d
        kc_view = lambda c0, c1: W[:, c0:c1].rearrange("(kc p) f -> p kc f", p=128)
        for (d0, s0, s1) in ((0, 512, 640), (128, 1280, 1792), (640, 640, 768), (768, 1792, 2048)):
            P.dma("pool", "wq", lambda e, d0=d0, s0=s0, s1=s1: e.dma_start(out=wq[:, :, d0:d0 + (s1 - s0)], in_=kc_view(s0, s1)), writes=[rwq])
        P.dma("pool", "w2", lambda e: e.dma_start(out=wx[:, :, :], in_=kc_view(2048, 2304)), writes=[rwx])
        P.dma("pool", "wo", lambda e: e.dma_start(out=wo[:], in_=self.woutab_d.rearrange("(c p) d -> p c d", p=128)), writes=[rwo])
        P.op("pool", lambda e: e.memset(KTA[0][64:128, :], 0.0), writes=[rK])
        P.op("pool", lambda e: e.memset(KTA[1][0:64, :], 0.0), writes=[rK])
        P.op("pool", lambda e: e.memset(VA[:, :, 64:128], 1.0), writes=[rV])
        P.op("pool", lambda e: e.memset(QT[64:128, 4:8, :], 0.0), writes=[rQT])
        rk = self.rst[0]
        self.rstd_cols([(self.X[:, t, :], [self.rX[t]], hb[:], rhb) for t in range(NT)], 0, rk, 1.0 / D)
        self._cur_row = None

        def stA(t):
            row = 2 if t < 2 else b
            if row != self._cur_row:
                self.prep_pre(0, 1, row)
                self._cur_row = row
            self.prenorm_tile(t, self.st[:, t:t + 1], rk, tmpf[:], rtmpf, hb[:], rhb, hTt[:], rhTt)

        kvsets = [((self.PY[0][:, 0:512], self.rPY[0][0]), (self.PY[0][:, 512:1024], self.rPY[0][1]), (self.PY[1][:, 0:256], self.rPY[1][0])),
                  ((self.PA[:, :], self.rPA), (self.PB[:, :], self.rPB), (self.PC[:, 0:256], self.rPC))]

        def s1B(t):
            ks = kvsets[t % 2]
            for (ps_ap, rps), (wsrc, rw, c0, c1) in zip(ks, ((wq, rwq, 0, 512), (wq, rwq, 512, 1024), (wx, rwx, 0, 256))):
                for kc in range(8):
                    P.op("pe", lambda e: e.matmul(ps_ap, lhsT=hTt[:, kc, :], rhs=wsrc[:, kc, c0:c1], start=(kc == 0), stop=(kc == 7)),
                         reads=[rhTt, rw], writes=[rps])

        def s1C(t):
            (p0, r0), (p1, r1), (p2, r2) = kvsets[t % 2]
            self.head_norm(p0[:, 0:128], [r0], 2, self.kgt, sqf, rra, 20, self.rst[4])
            self.rope(p0[:, 0:512].rearrange("p (h d) -> p h d", h=8), [r0], t, 8,
                      qrot[:, 0:512].rearrange("p (h d) -> p h d", h=8), rqrot, ra, rb, rra, rrb)
            self.rope(p1[:, 0:128].rearrange("p (h d) -> p h d", h=2), [r1], t, 2,
                      qrot[:, 512:640].rearrange("p (h d) -> p h d", h=2), rqrot, ra, rb, rra, rrb)

        def s1D(t):
            (p0, r0), (p1, r1), (p2, r2) = kvsets[t % 2]
            for c in range(5):
                P.op("pe", lambda e: e.transpose(out=self.PT[:, c, :], in_=qrot[:, c * 128:(c + 1) * 128], identity=self.identb[:]),
                     reads=[rqrot, self.rConst], writes=[self.rPT])
            tc_ = slice(t * 128, (t + 1) * 128)
            P.op("act", lambda e: e.activation(out=KTA[0][0:64, tc_], in_=self.PT[0:64, 0, :], func=AF.Copy), reads=[self.rPT], writes=[rK])
            P.op("act", lambda e: e.activation(out=KTA[1][64:128, tc_], in_=self.PT[64:128, 0, :], func=AF.Copy), reads=[self.rPT], writes=[rK])
            P.op("act", lambda e: e.activation(out=KTB[:, :, tc_], in_=self.PT[:, 1:5, :], func=AF.Copy), reads=[self.rPT], writes=[rK])
            P.op("act", lambda e: e.activation(out=VA[:, t, 0:64], in_=p1[:, 128:192], func=AF.Copy), reads=[r1], writes=[rV])
            P.op("act", lambda e: e.activation(out=VA[:, t, 128:192], in_=p1[:, 192:256], func=AF.Copy), reads=[r1], writes=[rV])
            P.op("act", lambda e: e.activation(out=VB[:, t, 0:256], in_=p1[:, 256:512], func=AF.Copy), reads=[r1], writes=[rV])
            P.op("act", lambda e: e.activation(out=VB[:, t, 256:512], in_=p2[:, 0:256], func=AF.Copy), reads=[r2], writes=[rV])

        for th in self.skewed(NT, [stA, s1B, lambda t: (s1C(t), s1D(t))]):
            th()
        if SUB < 2:
            return
        for c in range(4):
            for g in range(2):
                h = g * 4 + c
                pos = c * 2 + g
                P.dma("pool", "wq", lambda e, h=h, pos=pos: e.dma_start(out=wq[:, :, pos * 64:(pos + 1) * 64], in_=kc_view(h * 64, (h + 1) * 64)), writes=[rwq])
        P.dma("pool", "wq", lambda e: e.dma_start(out=wq[:, :, 512:1024], in_=kc_view(768, 1280)), writes=[rwq])
        P.op("pool", lambda e: e.memset(QTBhi[0:64, :, :], 0.0), reads=[rwx], writes=[rwx])
        blocks = [[0, 1]] + [list(range(2 + 4 * q, 6 + 4 * q)) for q in range(4)]
        for blk in blocks:
            nq = len(blk) * 128
            kts = [0, 1] if blk[0] < 2 else list(range(NT))
            row = 2 if blk[0] < 2 else b
            qsets = [(self.PY[0], self.rPY[0]), (self.PY[1], self.rPY[1])]

            def qB(j):
                qp, rqp = qsets[j % 2]
                for half in range(2):
                    for kc in range(8):
                        P.op("pe", lambda e: e.matmul(qp[:, half * 512:(half + 1) * 512], lhsT=hTt[:, kc, :],
                                                      rhs=wq[:, kc, half * 512:(half + 1) * 512], start=(kc == 0), stop=(kc == 7)),
                             reads=[rhTt, rwq], writes=[rqp[half]])

            def qC(j):
                qp, rqp = qsets[j % 2]
                self.head_norm(qp[:, 0:512], [rqp[0]], 8, self.qgt, sqf, rra, 24, self.rst[5])
                self.rope(qp[:, :].rearrange("p (h d) -> p h d", h=16), [rqp[0], rqp[1]], blk[j], 16,
                          qrot[:, :].rearrange("p (h d) -> p h d", h=16), rqrot, ra, rb, rra, rrb)

            def qD(j):
                for c in range(8):
                    P.op("pe", lambda e: e.transpose(out=self.PT[:, c, :], in_=qrot[:, c * 128:(c + 1) * 128], identity=self.identb[:]),
                         reads=[rqrot, self.rConst], writes=[self.rPT])
                js = slice(j * 128, (j + 1) * 128)
                P.op("act", lambda e: e.activation(out=QT[:, 0:4, js], in_=self.PT[:, 0:4, :], func=AF.Copy), reads=[self.rPT], writes=[rQT])
                P.op("act", lambda e: e.activation(out=QT[0:64, 4:8, js], in_=self.PT[0:64, 4:8, :], func=AF.Copy), reads=[self.rPT], writes=[rQT])
                P.op("act", lambda e: e.activation(out=QTBhi[64:128, :, js], in_=self.PT[64:128, 4:8, :], func=AF.Copy), reads=[self.rPT], writes=[rwx])

            for th in self.skewed(len(blk), [lambda j: stA(blk[j]), qB, lambda j: (qC(j), qD(j))]):
                th()
            if SUB < 3:
                continue
            jobs = []
            for c in range(4):
                for g in range(2):
                    h = g * 4 + c
                    ps_, ch = (h % 2) * 64, h // 2
                    ob, rsb = (0, 64) if g == 0 else (64, 0)

                    def fin(O, rO, Rs, rR, ps_=ps_, ch=ch, ob=ob, rsb=rsb):
                        P.op("dve", lambda e: e.reciprocal(out=rinv[0:64, 0:nq], in_=O[rsb:rsb + 64, 0:nq]), reads=[rO], writes=[rtmpf])
                        P.op("dve", lambda e: e.tensor_tensor(out=OT[ps_:ps_ + 64, ch, 0:nq], in0=O[ob:ob + 64, 0:nq], in1=rinv[0:64, 0:nq], op=ALU.mult),
                             reads=[rO, rtmpf], writes=[rOT])
                    jobs.append(dict(kT=lambda kt, g=g: KTA[g][:, kt * 128:(kt + 1) * 128], q=QT[:, c, 0:nq],
                                     v=lambda kt, g=g: VA[:, kt, g * 64:g * 64 + 128], nq=nq, kts=kts, rs_sep=False,
                                     rk=[rK], rq=[rQT], rv=[rV], fin=fin))
            for hb_ in range(4):
                for cm in range(2):
                    def fin(O, rO, Rs, rR, hb_=hb_, cm=cm):
                        P.op("dve", lambda e: e.reciprocal(out=rinv[:, 0:nq], in_=Rs[:, 0:nq]), reads=[rR], writes=[rtmpf])
                        if cm == 0:
                            P.op("dve", lambda e: e.tensor_tensor(out=o1[:, 0:nq], in0=O[:, 0:nq], in1=rinv[:, 0:nq], op=ALU.mult), reads=[rO, rtmpf], writes=[rtmpf])
                            return
                        P.op("dve", lambda e: e.tensor_tensor(out=rinv[:, 0:nq], in0=O[:, 0:nq], in1=rinv[:, 0:nq], op=ALU.mult), reads=[rO, rtmpf], writes=[rtmpf])
                        P.op("dve", lambda e: e.scalar_tensor_tensor(out=o1[:, 0:nq], in0=rinv[:, 0:nq], scalar=self.lamt[:, 4:5], in1=o1[:, 0:nq],
                                                                      op0=ALU.mult, op1=ALU.add), reads=[rtmpf, self.rlam], writes=[rtmpf])
                        P.op("dve", lambda e: e.tensor_copy(out=OT[:, 4 + hb_, 0:nq], in_=o1[:, 0:nq]), reads=[rtmpf], writes=[rOT])
                    qsrc, rq_ = (QT[:, 4 + hb_, 0:nq], rQT) if cm == 0 else (QTBhi[:, hb_, 0:nq], rwx)
                    jobs.append(dict(kT=lambda kt, hb_=hb_: KTB[:, hb_, kt * 128:(kt + 1) * 128], q=qsrc,
                                     v=lambda kt, hb_=hb_: VB[:, kt, hb_ * 128:(hb_ + 1) * 128], nq=nq, kts=kts, rs_sep=True,
                                     rk=[rK], rq=[rq_], rv=[rV], fin=fin))
            self.attend(jobs)
            for hb_ in range(4):
                dT = OT[:, 4 + hb_, 0:nq]
                P.op("pool", lambda e: e.tensor_tensor(out=dsq[:, 0:nq], in0=dT, in1=dT, op=ALU.mult), reads=[rOT], writes=[rrb])
                ssd, rssd = self.PY[1][:, 512:1024], self.rPY[1][1]
                P.op("pe", lambda e: e.matmul(ssd[:, 0:nq], lhsT=self.onesb[:, :], rhs=dsq[:, 0:nq], start=True, stop=True),
                     reads=[rrb, self.rConst], writes=[rssd])
                P.op("dve", lambda e: e.tensor_scalar(out=rsd[:, 0:nq], in0=ssd[:, 0:nq], scalar1=1.0 / 128, scalar2=EPS, op0=ALU.mult, op1=ALU.add),
                     reads=[rssd], writes=[rra])
                P.op("act", lambda e: e.activation(out=rsd[:, 0:nq], in_=rsd[:, 0:nq], func=AF.Sqrt), reads=[rra], writes=[rra])
                P.op("dve", lambda e: e.reciprocal(out=rsd[:, 0:nq], in_=rsd[:, 0:nq]), reads=[rra], writes=[rra])
                P.op("dve", lambda e: e.scalar_tensor_tensor(out=dT, in0=dT, scalar=self.sgc[:, 0:1], in1=rsd[:, 0:nq],
                                                              op0=ALU.mult, op1=ALU.mult), reads=[rOT, rra, self.rConst], writes=[rOT])
            if SUB < 4:
                continue
            for th in self.skewed(len(blk), [lambda j: self.outproj_mm(j, OT, rOT, wo, rwo, 8, yk=(j + 1) % 2),
                                             lambda j: self.outproj_post(blk[j], 0, row, 40 + (j % 2), yk=(j + 1) % 2)]):
                th()
        self._post_state = None

    def mixer_c(self, b):
        P = self.P
        self.ov_off = 0
        self._post_state = None
        T = NT * 128
        skb = self.ovv([128, 16, 128], F32)
        maskt = self.ovv([128, 2, 512], BF16)
        KTC = [self.ovv([128, 2, T], BF16), self.ovv([128, 2, T], BF16)]
        VC = self.ovv([128, NT, 384], BF16)
        wb = self.ovv([128, 8, 1024], BF16)
        wo = self.ovv([128, 8, D], BF16)
        tmpf = self.ovv([128, D], F32)
        tmpf2 = tmpf
        hb = self.ovv([128, D], BF16)
        hTts = [(self.ovv([128, 8, 128], BF16), Res("hT0")), (self.ovv([128, 8, 128], BF16), Res("hT1"))]
        qrots = [(self.ovv([128, D], BF16), Res("qr0"))] * 2
        ra = self.ovv([128, 16, 32], F32)
        rb = self.ovv([128, 16, 32], F32)
        QTs = [(self.ovv([128, 4, 8, 128], BF16), Res("QT0")), (self.ovv([128, 4, 8, 128], BF16), Res("QT1"))]
        OT = self.ovv([128, 4, 8, 128], BF16)
        self.Pring = [(self.ovv([128, 512], BF16), Res("p0")), (self.ovv([128, 512], BF16), Res("p1"))]
        rinv = self.ovv([128, 512], F32)
        rtmpf2 = Res("rinv")
        rK, rV, rwb, rwo = Res("K"), Res("V"), Res("wb"), Res("wo")
        rtmpf, rhb, rra, rrb, rOT, rsk = (Res(n) for n in ("tmpf", "hb", "ra", "rb", "OT", "sk"))
        rtmpfb = rtmpf
        if not CT6:
            qrots = [(hb, rhb)] * 2
        self.m_tmpf, self.m_rtmpf = tmpf, rtmpf
        skbf = skb.rearrange("p h t -> p (h t)")
        PAb = self.PT
        W = self.winc_d
        kc_view = lambda c0, c1: W[:, c0:c1].rearrange("(kc p) f -> p kc f", p=128)
        P.dma("pool", "wq", lambda e: e.dma_start(out=wb[:, :, 0:512], in_=kc_view(1024, 1536)), writes=[rwb])
        P.dma("pool", "wo", lambda e: e.dma_start(out=wo[:], in_=self.woutc_d.rearrange("(c p) d -> p c d", p=128)), writes=[rwo])
        P.dma("sp", "ld5", lambda e: e.dma_start(out=maskt[:], in_=self.mask_d[:, :, :]), writes=[rsk])
        P.op("dve", lambda e: e.tensor_copy(out=skb[:], in_=self.skt[:, :].unsqueeze(2).to_broadcast([128, 16, 128])), reads=[self.rConst], writes=[rsk])
        P.op("pool", lambda e: e.memset(KTC[0][64:128, :, :], 0.0), writes=[rK])
        P.op("pool", lambda e: e.memset(KTC[1][0:64, :, :], 0.0), writes=[rK])
        P.op("pool", lambda e: e.memset(VC[:, :, 64:128], 1.0), writes=[rV])
        P.op("pool", lambda e: e.memset(VC[:, :, 256:320], 1.0), writes=[rV])
        rk = self.rst[0]
        self.rstd_cols([(self.X[:, t, :], [self.rX[t]], hb[:], rhb) for t in range(NT)], 0, rk, 1.0 / D)
        self._cur_row = None

        def stA(t, j):
            row = 2 if t < 2 else b
            if row != self._cur_row:
                self.prep_pre(1, 1, row)
                self._cur_row = row
            hTt, rhTt = hTts[(j % 2) * CT5]
            self.prenorm_tile(t, self.st[:, t:t + 1], rk, tmpf[:], rtmpf, hb[:], rhb, hTt[:], rhTt)

        def s1B(j):
            hTt, rhTt = hTts[(j % 2) * CT5]
            ps = self.PY[0][:, (j % 2) * CT4 * 512:(j % 2) * CT4 * 512 + 512]
            for kc in range(8):
                P.op("pe", lambda e: e.matmul(ps, lhsT=hTt[:, kc, :], rhs=wb[:, kc, 0:512], start=(kc == 0), stop=(kc == 7)),
                     reads=[rhTt, rwb], writes=[self.rPY[0][(j % 2) * CT4]])

        def s1C(j):
            t = j
            ps = self.PY[0][:, (j % 2) * CT4 * 512:(j % 2) * CT4 * 512 + 512]
            rps = self.rPY[0][(j % 2) * CT4]
            qrot, rqrot = qrots[j % 2]
            self.rope(ps[:, 0:256].rearrange("p (h d) -> p h d", h=4), [rps], t, 4,
                      qrot[:, 0:256].rearrange("p (h d) -> p h d", h=4), rqrot, ra, rb, rra, rrb)
            P.op("act", lambda e: e.activation(out=VC[:, t, 0:64], in_=ps[:, 256:320], func=AF.Copy), reads=[rps], writes=[rV])
            P.op("act", lambda e: e.activation(out=VC[:, t, 128:256], in_=ps[:, 320:448], func=AF.Copy), reads=[rps], writes=[rV])
            P.op("act", lambda e: e.activation(out=VC[:, t, 320:384], in_=ps[:, 448:512], func=AF.Copy), reads=[rps], writes=[rV])

        def s1D(j):
            t = j
            qrot, rqrot = qrots[j % 2]
            for c in range(2):
                P.op("pe", lambda e: e.transpose(out=PAb[:, c, :], in_=qrot[:, c * 128:(c + 1) * 128], identity=self.identb[:]),
                     reads=[rqrot, self.rConst], writes=[self.rPT])
            tc_ = slice(t * 128, (t + 1) * 128)
            P.op("act", lambda e: e.activation(out=KTC[0][0:64, :, tc_], in_=PAb[0:64, 0:2, :], func=AF.Copy), reads=[self.rPT], writes=[rK])
            P.op("act", lambda e: e.activation(out=KTC[1][64:128, :, tc_], in_=PAb[64:128, 0:2, :], func=AF.Copy), reads=[self.rPT], writes=[rK])

        if CT1:
            for th in self.skewed(NT, [lambda j: stA(j, j), s1B, s1C, s1D]):
                th()
        else:
            for j in range(NT):
                stA(j, j); s1B(j); s1C(j); s1D(j)
        for h in range(16):
            gp, e_, i_ = h // 8, (h % 8) // 4, h % 4
            pos = (gp * 4 + i_) * 2 + e_
            P.dma("pool", "wq", lambda e, h=h, pos=pos: e.dma_start(out=wb[:, :, pos * 64:(pos + 1) * 64], in_=kc_view(h * 64, (h + 1) * 64)), writes=[rwb])
        voff = (0, 64, 192, 256)
        blocks = [list(range(2 + 4 * q, 6 + 4 * q)) for q in range(4)]

        def qstages(n):
            QTn, rQTn = QTs[n % 2]
            blk = blocks[n]

            def qB(j):
                hTt, rhTt = hTts[(j % 2) * CT5]
                for half in range(2):
                    for kc in range(8):
                        P.op("pe", lambda e: e.matmul(self.PY[0][:, half * 512:(half + 1) * 512], lhsT=hTt[:, kc, :],
                                                      rhs=wb[:, kc, half * 512:(half + 1) * 512], start=(kc == 0), stop=(kc == 7)),
                             reads=[rhTt, rwb], writes=[self.rPY[0][half]])

            def qC(j):
                qrot, rqrot = qrots[j % 2]
                self.rope(self.PY[0][:, :].rearrange("p (h d) -> p h d", h=16), [self.rPY[0][0], self.rPY[0][1]], blk[j], 16,
                          qrot[:, :].rearrange("p (h d) -> p h d", h=16), rqrot, ra, rb, rra, rrb)

            def qD(j):
                qrot, rqrot = qrots[j % 2]
                for c in range(8):
                    P.op("pe", lambda e: e.transpose(out=self.PT[:, c, :], in_=qrot[:, c * 128:(c + 1) * 128], identity=self.identb[:]),
                         reads=[rqrot, self.rConst], writes=[self.rPT])
                P.op("act", lambda e: e.activation(out=QTn[:, j, :, :], in_=self.PT[:], func=AF.Copy), reads=[self.rPT], writes=[rQTn])
            if not CT7:
                return [lambda j=j, f=f: f(j) for j in range(4) for f in (lambda j: stA(blk[j], j), qB, qC, qD)]
            return self.skewed(4, [lambda j: stA(blk[j], j), qB, qC, qD])

        for th in qstages(0):
            th()
        for n, blk in enumerate(blocks):
            QTn, rQTn = QTs[n % 2]
            QTf = QTn.rearrange("p j c t -> p j (c t)")
            jobs = []
            for j, t in enumerate(blk):
                kts = [0, 1] + [k for k in (t - 1, t, t + 1) if 2 <= k < NT]

                def mask(kt, t=t):
                    if kt == t - 1 and kt >= 2:
                        return maskt[:, 0, :]
                    if kt == t + 1:
                        return maskt[:, 1, :]
                    return None
                for g in range(4):
                    gp, e_ = g // 2, g % 2
                    ob, rsb = (0, 64) if e_ == 0 else (64, 0)

                    def fin(O, rO, Rs, rR, g=g, ob=ob, rsb=rsb, j=j):
                        P.op("dve", lambda e: e.tensor_tensor(out=rinv[0:64, :], in0=O[rsb:rsb + 64, 0:512], in1=skbf[0:64, g * 512:(g + 1) * 512], op=ALU.add),
                             reads=[rO, rsk], writes=[rtmpf2])
                        P.op("dve", lambda e: e.reciprocal(out=rinv[0:64, :], in_=rinv[0:64, :]), reads=[rtmpf2], writes=[rtmpf2])
                        for i_ in range(4):
                            h = 4 * g + i_
                            P.op("dve", lambda e: e.tensor_tensor(out=OT[(h % 2) * 64:(h % 2) * 64 + 64, j, h // 2, :], in0=O[ob:ob + 64, i_ * 128:(i_ + 1) * 128],
                                                                   in1=rinv[0:64, i_ * 128:(i_ + 1) * 128], op=ALU.mult), reads=[rO, rtmpf2], writes=[rOT])
                    jobs.append(dict(kT=lambda kt, gp=gp, e_=e_: KTC[e_][:, gp, kt * 128:(kt + 1) * 128],
                                     q=QTf[:, j, gp * 512:(gp + 1) * 512], v=lambda kt, g=g: VC[:, kt, voff[g]:voff[g] + 128],
                                     nq=512, kts=kts, mask=mask, rs_sep=False, rk=[rK, rsk], rq=[rQTn], rv=[rV], fin=fin))
            bg = qstages(n + 1) if n + 1 < len(blocks) else None
            if not CT2 and bg:
                for th in bg:
                    th()
                bg = None
            self.attend(jobs, mode="c", bg=bg, bg_every=4)
            if CT3:
                for th in self.skewed(4, [lambda j: self.outproj_mm(0, OT[:, j, :, :], rOT, wo, rwo, 8, yk=(j + 1) % 2),
                                          lambda j: self.outproj_post(blk[j], 1, b, 40 + (j % 2), yk=(j + 1) % 2, tmp=tmpf2, rtmp=rtmpfb)]):
                    th()
            else:
                for j in range(4):
                    self.outproj_mm(0, OT[:, j, :, :], rOT, wo, rwo, 8, yk=1)
                    self.outproj_post(blk[j], 1, b, 40 + (j % 2), yk=1, tmp=tmpf2, rtmp=rtmpfb)
        self._post_state = None


_NC_CACHE = {}


def _consts():
    ident_bf = np.eye(128, dtype=np.float32).astype(ml_dtypes.bfloat16)
    ident_f = np.eye(128, dtype=np.float32)
    pos = np.arange(L)
    row = (pos // 64).astype(np.float32)
    col = (pos % 64).astype(np.float32)
    inv = (10000.0 ** (-np.arange(0, 32, 2, dtype=np.float32) / 32)).astype(np.float32)
    ang = np.concatenate([row[:, None] * inv, col[:, None] * inv], axis=-1).astype(np.float32)
    cos = np.cos(ang).astype(np.float32)
    sin = np.sin(ang).astype(np.float32)
    cs = np.zeros((128, NT, 64), np.float32)
    cs[:, 0:2, 0:32] = 1.0
    cs[:, 2:, 0:32] = cos.reshape(16, 128, 32).transpose(1, 0, 2)
    cs[:, 2:, 32:64] = sin.reshape(16, 128, 32).transpose(1, 0, 2)
    kk = np.arange(128)[:, None]
    qq = np.arange(128)[None, :]
    m0 = np.where(kk >= qq, 0.0, -30000.0).astype(np.float32)
    m1 = np.where(kk <= qq, 0.0, -30000.0).astype(np.float32)
    maskb = np.stack([np.tile(m0, (1, 4)), np.tile(m1, (1, 4))], axis=1).astype(ml_dtypes.bfloat16)
    return dict(ident_bf=ident_bf, ident_f=ident_f, cs=cs, maskb=maskb)


def kernel(x, c, ctx, c_ctx, w_mod, b_mod, g_pre, g_post, w_ffn_gate, w_ffn_up, w_ffn_down,
           w_in_ab, w_out_ab, q_gain_a, k_gain_a, lam_q1, lam_k1, lam_q2, lam_k2, sub_gain_b,
           w_in_c, w_out_c, sink_c):
    f = lambda a: np.ascontiguousarray(np.asarray(a, dtype=np.float32))
    x, c, ctx, c_ctx = f(x), f(c), f(ctx), f(c_ctx)
    if "nc" not in _NC_CACHE:
        _NC_CACHE["nc"] = Builder().build()
    nc = _NC_CACHE["nc"]
    cst = _consts()
    shared = dict(
        w_mod=f(w_mod), b_mod=f(b_mod), g_pre=f(g_pre), g_post=f(g_post),
        w_ffn_gate=f(w_ffn_gate), w_ffn_up=f(w_ffn_up), w_ffn_down=f(w_ffn_down),
        w_in_ab=f(w_in_ab)[0], w_out_ab=f(w_out_ab)[0], q_gain_a=f(q_gain_a), k_gain_a=f(k_gain_a),
        lam4=np.ascontiguousarray(np.concatenate([f(lam_q1), f(lam_k1), f(lam_q2), f(lam_k2)], axis=0)),
        sub_gain_b=np.ascontiguousarray(f(sub_gain_b).reshape(128, 1)),
        w_in_c=f(w_in_c)[0], w_out_c=f(w_out_c)[0], sink_c=f(sink_c), **cst)
    in_maps = []
    for k in range(8):
        m = dict(shared)
        m["x"] = x[NB * k:NB * (k + 1)]
        m["ctx"] = ctx[NB * k:NB * (k + 1)]
        m["cvec"] = np.ascontiguousarray(np.concatenate([c[NB * k:NB * (k + 1)], c_ctx[None, :]], axis=0))
        in_maps.append(m)
    res = run_bass_kernel_spmd(nc, in_maps, core_ids=list(range(8)))
    return np.concatenate([r["out"] for r in res.results], axis=0)
```

```python
import math
from contextlib import ExitStack
import numpy as np
import ml_dtypes
import concourse.bass as bass
import concourse.mybir as mybir
from concourse.bass_utils import run_bass_kernel_spmd

F32 = mybir.dt.float32
BF16 = mybir.dt.bfloat16
AF = mybir.ActivationFunctionType
ALU = mybir.AluOpType
AX = mybir.AxisListType

D = 1024
L = 2048
CT = 256
NT = 18
DFF = 2816
NJ = 22
EPS = 1e-6
NB = 2
STAGES = 99
SUB = 99
import os as _os
CT1 = int(_os.environ.get('CT1', '1'))
CT2 = int(_os.environ.get('CT2', '1'))
CT3 = int(_os.environ.get('CT3', '1'))
CT4 = int(_os.environ.get('CT4', '1'))
CT5 = int(_os.environ.get('CT5', '1'))
CT6 = int(_os.environ.get('CT6', '1'))
CT7 = int(_os.environ.get('CT7', '1'))


class Res:
    __slots__ = ("name", "w", "r", "excl")

    def __init__(self, name, excl=False):
        self.name = name
        self.w = None
        self.r = []
        self.excl = excl


class _Rec:
    def __getattr__(self, name):
        return lambda *a, **k: (name, a, k)


_REC = _Rec()


class Prog:
    CE = ("pe", "act", "dve", "pool")

    def __init__(self):
        self.q = {e: [] for e in ("pe", "act", "dve", "pool", "sp")}
        self.cnt = {e: 0 for e in self.CE}
        self.waited = {e: {} for e in self.q}
        self.dcnt = {}

    def _deps(self, eng, reads, writes):
        toks = []
        for r in reads:
            if r.w is not None:
                toks.append(r.w)
            if r.excl:
                for t in r.r:
                    if t[0] != eng:
                        toks.append(t)
        for w in writes:
            if w.w is not None:
                toks.append(w.w)
            for t in w.r:
                toks.append(t)
        need = {}
        for sk, v in toks:
            if sk == "pe" and eng == "pe":
                continue
            if self.waited[eng].get(sk, 0) >= v:
                continue
            need[sk] = max(need.get(sk, 0), v)
        for sk, v in need.items():
            self.waited[eng][sk] = v
            self.q[eng].append(("wait", sk, v))

    def op(self, eng, fn, reads=(), writes=()):
        self._deps(eng, reads, writes)
        self.cnt[eng] += 1
        tok = (eng, self.cnt[eng])
        self.q[eng].append(("op", fn(_REC)))
        for r in reads:
            r.r.append(tok)
        for w in writes:
            w.w = tok
            w.r = []
        return tok

    def dma(self, queue, semkey, fn, reads=(), writes=()):
        self._deps(queue, reads, writes)
        self.dcnt[semkey] = self.dcnt.get(semkey, 0) + 16
        tok = (semkey, self.dcnt[semkey])
        self.q[queue].append(("dma", fn(_REC), semkey))
        for r in reads:
            r.r.append(tok)
        for w in writes:
            w.w = tok
            w.r = []
        return tok

    def barrier(self):
        toks = [(e, c) for e, c in self.cnt.items() if c > 0] + [(k, v) for k, v in self.dcnt.items()]
        for eng in self.q:
            for sk, v in toks:
                if sk == eng:
                    continue
                if self.waited[eng].get(sk, 0) >= v:
                    continue
                self.waited[eng][sk] = v
                self.q[eng].append(("wait", sk, v))

    def emit(self, sems, block):
        def replay(key, eng):
            for it in self.q[key]:
                if it[0] == "wait":
                    eng.wait_ge(sems[it[1]], it[2])
                elif it[0] == "op":
                    nm, a, k = it[1]
                    getattr(eng, nm)(*a, **k).then_inc(sems[key], 1)
                else:
                    nm, a, k = it[1]
                    getattr(eng, nm)(*a, **k).then_inc(sems[it[2]], 16)

        @block.tensor
        def _(e):
            replay("pe", e)

        @block.scalar
        def _(e):
            replay("act", e)

        @block.vector
        def _(e):
            replay("dve", e)

        @block.gpsimd
        def _(e):
            replay("pool", e)

        @block.sync
        def _(e):
            replay("sp", e)


class Builder:
    def __init__(self):
        self.nc = bass.Bass("TRN2", target_bir_lowering=False)
        self.P = Prog()
        self.es = ExitStack()
        self.dsems = []

    def din(self, name, shape, dt=F32):
        return self.nc.dram_tensor(name, list(shape), dt, kind="ExternalInput").ap()

    def sb(self, name, shape, dt):
        return self.es.enter_context(self.nc.sbuf_tensor("s_" + name, list(shape), dt))

    def ps(self, name, shape, dt):
        return self.es.enter_context(self.nc.psum_tensor("p_" + name, list(shape), dt))

    def dsem(self, name):
        self.dsems.append(name)
        return name

    def carve(self, nbytes):
        off = self.ov_off
        assert off % 4 == 0
        self.ov_off += (nbytes + 3) // 4 * 4
        assert self.ov_off <= self.ov_bytes, (self.ov_off, self.ov_bytes)
        return off

    def ovv(self, shape, dt):
        esz = 2 if dt == BF16 else 4
        n = 1
        for s in shape[1:]:
            n *= s
        off = self.carve(n * esz)
        v = self.OV[0:shape[0], off // 2: off // 2 + n * esz // 2]
        if dt != BF16:
            v = v.bitcast(dt)
        if len(shape) == 3:
            v = v.rearrange("p (a b) -> p a b", a=shape[1])
        elif len(shape) == 4:
            v = v.rearrange("p (a b c) -> p a b c", a=shape[1], b=shape[2])
        return v

    @staticmethod
    def bcast(row_ap, n):
        ln = row_ap.shape[-1]
        return bass.AP(row_ap.tensor, row_ap.offset, [[0, n], [1, ln]])

    def build(self):
        nc, P = self.nc, self.P
        self.x_d = self.din("x", [NB, L, D])
        self.ctx_d = self.din("ctx", [NB, CT, D])
        self.cvec_d = self.din("cvec", [3, D])
        self.wmod_d = self.din("w_mod", [2, D, 9 * D])
        self.bmod_d = self.din("b_mod", [2, 9 * D])
        self.gpre_d = self.din("g_pre", [2, 3, D])
        self.gpost_d = self.din("g_post", [2, 3, D])
        self.wg_d = self.din("w_ffn_gate", [2, 2, D, DFF])
        self.wu_d = self.din("w_ffn_up", [2, 2, D, DFF])
        self.wd_d = self.din("w_ffn_down", [2, 2, DFF, D])
        self.winab_d = self.din("w_in_ab", [D, 2304])
        self.woutab_d = self.din("w_out_ab", [D, D])
        self.qg_d = self.din("q_gain_a", [1, 64])
        self.kg_d = self.din("k_gain_a", [1, 64])
        self.lam_d = self.din("lam4", [4, 64])
        self.sg_d = self.din("sub_gain_b", [128, 1])
        self.winc_d = self.din("w_in_c", [D, 1536])
        self.woutc_d = self.din("w_out_c", [D, D])
        self.sink_d = self.din("sink_c", [1, 16])
        self.identb_d = self.din("ident_bf", [128, 128], BF16)
        self.identf_d = self.din("ident_f", [128, 128], F32)
        self.cs_d = self.din("cs", [128, NT, 64])
        self.mask_d = self.din("maskb", [128, 2, 512], BF16)
        self.out_d = nc.dram_tensor("out", [NB, L, D], F32, kind="ExternalOutput").ap()
        self.modd = nc.dram_tensor("modd", [2, 3, 9 * D], F32).ap()

        self.X = self.sb("X", [128, NT, D], F32)
        self.rX = [Res(f"x{t}") for t in range(NT)]
        self.identb = self.sb("identb", [128, 128], BF16)
        self.onesb = self.sb("onesb", [128, 128], BF16)
        self.cs = self.sb("cs", [128, NT, 64], F32)
        self.rConst = Res("const")
        self.gm = self.sb("gm", [128, D], F32)
        self.sh = self.sb("sh", [128, D], F32)
        self.gg = self.sb("gg", [128, D], F32)
        self.rgm, self.rsh, self.rgg = Res("gm"), Res("sh"), Res("gg")
        self.st = self.sb("st", [128, 64], F32)
        self.rst = [Res(f"st{k}") for k in range(8)]
        self.lamt = self.sb("lamt", [128, 8], F32)
        self.rlam = Res("lam")
        self.sgc = self.sb("sgc", [128, 1], F32)
        self.qgt = self.sb("qgt", [128, 64], F32)
        self.kgt = self.sb("kgt", [128, 64], F32)
        self.skt = self.sb("skt", [128, 16], F32)
        self.ov_bytes = 116096 + 4096 + 512
        self.OV = self.sb("OV", [128, self.ov_bytes // 2], BF16)
        self.ov_off = 0

        self.PY = [self.ps("py0", [128, 1024], F32), self.ps("py1", [128, 1024], F32)]
        self.rPY = [[Res("py0a", True), Res("py0b", True)], [Res("py1a", True), Res("py1b", True)]]
        self.PT = self.ps("pt", [128, 8, 128], BF16)
        self.rPT = Res("pt", True)
        self.PA = self.ps("pa", [128, 512], F32)
        self.PB = self.ps("pb", [128, 512], F32)
        self.PC = self.ps("pc", [128, 512], F32)
        self.rPA, self.rPB, self.rPC = Res("pa", True), Res("pb", True), Res("pc", True)

        for k in ("ld0", "ld1", "ld2", "ld3", "ld4", "ld5", "st2", "st3", "bc0", "bc1", "bc2", "bc3", "w0", "w1", "w2", "w3", "wd", "wq", "wo",
                  "st0", "st1", "ms0", "ms1"):
            self.dsem(k)

        self.consts()
        self.stage_mod()
        P.barrier()
        for b in range(NB):
            self.load_x(b)
            if STAGES >= 1:
                self.ffn(0, 0, 0, b, list(range(NT)))
            P.barrier()
            if STAGES >= 2:
                self.mixer_ab(b)
                P.barrier()
            if STAGES >= 3:
                self.ffn(0, 1, 2, b, list(range(NT)))
                P.barrier()
            if STAGES >= 4:
                self.ffn(1, 0, 0, b, list(range(NT)))
                P.barrier()
            if STAGES >= 5:
                self.mixer_c(b)
                P.barrier()
            if STAGES >= 6:
                self.ffn(1, 1, 2, b, list(range(2, NT)))
                P.barrier()
            self.store_x(b)
        P.barrier()

        sems = {}
        for k in ("pe", "act", "dve", "pool"):
            sems[k] = self.es.enter_context(nc.semaphore(k))
        for k in self.dsems:
            sems[k] = self.es.enter_context(nc.semaphore(k))
        block = self.es.enter_context(nc.Block())
        P.emit(sems, block)
        self.es.close()
        return nc

    def consts(self):
        P = self.P
        rc = self.rConst
        P.dma("sp", "ld5", lambda e: e.dma_start(out=self.identb[:], in_=self.identb_d[:, :]), writes=[rc])
        P.dma("sp", "ld5", lambda e: e.dma_start(out=self.cs[:], in_=self.cs_d[:, :, :]), writes=[rc])
        P.dma("sp", "ld5", lambda e: e.dma_start(out=self.sgc[:], in_=self.sg_d[:, :]), writes=[rc])
        P.dma("sp", "ld5", lambda e: e.dma_start(out=self.qgt[:], in_=self.bcast(self.qg_d[0:1, :], 128)), writes=[rc])
        P.dma("sp", "ld5", lambda e: e.dma_start(out=self.kgt[:], in_=self.bcast(self.kg_d[0:1, :], 128)), writes=[rc])
        P.dma("sp", "ld5", lambda e: e.dma_start(out=self.skt[:], in_=self.bcast(self.sink_d[0:1, :], 128)), writes=[rc])
        P.op("dve", lambda e: e.memset(self.onesb[:], 1.0), writes=[rc])

    def stage_mod(self):
        P = self.P
        self.ov_off = 0
        NR = 6
        identf = self.ovv([128, 128], F32)
        cv = self.ovv([128, D], F32)
        sc = self.ovv([128, D], F32)
        scT = self.ovv([128, 8, 4], F32)
        wm = [self.ovv([128, 8, 512], F32) for _ in range(NR)]
        bm = [self.ovv([128, 512], F32) for _ in range(2)]
        ms = [self.ovv([128, 512], F32) for _ in range(2)]
        rcv, rsc, rscT, rid = Res("cv"), Res("sc"), Res("scT"), Res("identf")
        rwm = [Res(f"wm{k}") for k in range(NR)]
        rbm = [Res("bm0"), Res("bm1")]
        rms = [Res("ms0"), Res("ms1")]
        for k in range(NR):
            self.dsem(f"wm{k}")
        ptf = self.PT[:].rearrange("p a b -> p (a b)").bitcast(F32).rearrange("p (a b) -> p a b", a=8)
        P.dma("sp", "ld1", lambda e: e.dma_start(out=cv[0:3, :], in_=self.cvec_d[:, :]), writes=[rcv])
        P.dma("sp", "ld3", lambda e: e.dma_start(out=identf[:, :], in_=self.identf_d[:, :]), writes=[rid])
        slabs = [(i, s_) for i in range(2) for s_ in range(18)]

        def issue(n):
            i, s_ = slabs[n]
            k = n % NR
            wsrc = self.wmod_d[i, :, s_ * 512:(s_ + 1) * 512].rearrange("(kc p) f -> p kc f", p=128)
            P.dma("sp", f"wm{k}", lambda e: e.dma_start(out=wm[k][:], in_=wsrc), writes=[rwm[k]])
        for n in range(NR - 1):
            issue(n)
        P.op("act", lambda e: e.activation(out=sc[0:3, :], in_=cv[0:3, :], func=AF.Silu), reads=[rcv], writes=[rsc])
        for kc in range(8):
            P.op("pe", lambda e, kc=kc: e.transpose(out=ptf[:, kc, 0:3], in_=sc[0:3, kc * 128:(kc + 1) * 128],
                                                   identity=identf[0:3, 0:3]),
                 reads=[rsc, rid], writes=[self.rPT])
        P.op("dve", lambda e: e.tensor_copy(out=scT[:, :, 0:3], in_=ptf[:, :, 0:3]), reads=[self.rPT], writes=[rscT])
        for n, (i, s_) in enumerate(slabs):
            if n + NR - 1 < len(slabs):
                issue(n + NR - 1)
            k = n % 2
            kw = n % NR
            bsrc = self.bcast(self.bmod_d[i:i + 1, s_ * 512:(s_ + 1) * 512], 3)
            P.dma("sp", f"bc{k}", lambda e, k=k, bsrc=bsrc: e.dma_start(out=bm[k][0:3, :], in_=bsrc), writes=[rbm[k]])
            mp = self.PA if k == 0 else self.PB
            rmp = self.rPA if k == 0 else self.rPB
            for kc in range(8):
                P.op("pe", lambda e, kc=kc: e.matmul(mp[0:3, :], lhsT=scT[:, kc, 0:3], rhs=wm[kw][:, kc, :], start=(kc == 0), stop=(kc == 7)),
                     reads=[rscT, rwm[kw]], writes=[rmp])
            P.op("dve", lambda e: e.tensor_tensor(out=ms[k][0:3, :], in0=mp[0:3, :], in1=bm[k][0:3, :], op=ALU.add),
                 reads=[rmp, rbm[k]], writes=[rms[k]])
            dst = self.modd[i, :, s_ * 512:(s_ + 1) * 512]
            P.dma("sp", f"ms{k}", lambda e: e.dma_start(out=dst, in_=ms[k][0:3, :]), reads=[rms[k]])
        lt = self.ovv([128, 4, 64], F32)
        rlt = Res("lt")
        for r in range(4):
            P.dma("sp", "ld2", lambda e, r=r: e.dma_start(out=lt[:, r, :], in_=self.bcast(self.lam_d[r:r + 1, :], 128)), writes=[rlt])
        P.op("dve", lambda e: e.tensor_tensor(out=lt[:, 0, :], in0=lt[:, 0, :], in1=lt[:, 1, :], op=ALU.mult), reads=[rlt], writes=[rlt])
        P.op("dve", lambda e: e.tensor_tensor(out=lt[:, 2, :], in0=lt[:, 2, :], in1=lt[:, 3, :], op=ALU.mult), reads=[rlt], writes=[rlt])
        P.op("dve", lambda e: e.tensor_reduce(out=self.lamt[:, 0:1], in_=lt[:, 0, :], axis=AX.X, op=ALU.add), reads=[rlt], writes=[self.rlam])
        P.op("dve", lambda e: e.tensor_reduce(out=self.lamt[:, 1:2], in_=lt[:, 2, :], axis=AX.X, op=ALU.add), reads=[rlt], writes=[self.rlam])
        P.op("act", lambda e: e.activation(out=self.lamt[:, 2:4], in_=self.lamt[:, 0:2], func=AF.Exp), reads=[self.rlam], writes=[self.rlam])
        lam_init = 0.8 - 0.6 * math.exp(-0.3 * 0)
        P.op("dve", lambda e: e.scalar_tensor_tensor(out=self.lamt[:, 4:5], in0=self.lamt[:, 3:4], scalar=-lam_init, in1=self.lamt[:, 2:3],
                                                      op0=ALU.add, op1=ALU.subtract), reads=[self.rlam], writes=[self.rlam])
        P.op("dve", lambda e: e.tensor_scalar(out=self.sgc[:], in0=self.sgc[:], scalar1=1.0 - lam_init, scalar2=None, op0=ALU.mult),
             reads=[self.rConst], writes=[self.rConst])
        P.op("act", lambda e: e.activation(out=self.skt[:], in_=self.skt[:], func=AF.Exp), reads=[self.rConst], writes=[self.rConst])

    def load_x(self, b):
        P = self.P
        P.dma("sp", "ld4", lambda e: e.dma_start(out=self.X[:, 0:2, :], in_=self.ctx_d[b].rearrange("(t p) d -> p t d", p=128)),
              writes=self.rX[0:2])
        for q in range(4):
            t0 = 2 + 4 * q
            P.dma("sp", f"ld{q}", lambda e, q=q, t0=t0: e.dma_start(
                out=self.X[:, t0:t0 + 4, :], in_=self.x_d[b, q * 512:(q + 1) * 512, :].rearrange("(t p) d -> p t d", p=128)),
                writes=self.rX[t0:t0 + 4])

    def store_x(self, b):
        P = self.P
        for q in range(4):
            t0 = 2 + 4 * q
            P.dma("sp", f"st{q}", lambda e, q=q, t0=t0: e.dma_start(
                out=self.out_d[b, q * 512:(q + 1) * 512, :].rearrange("(t p) d -> p t d", p=128), in_=self.X[:, t0:t0 + 4, :]),
                reads=self.rX[t0:t0 + 4])

    def bc_row(self, dst, rdst, src_row, sem):
        self.P.dma("sp", sem, lambda e: e.dma_start(out=dst[:, :], in_=self.bcast(src_row, 128)), writes=[rdst])

    def prep_pre(self, i, slot, r):
        P = self.P
        base = slot * 3 * D
        self.bc_row(self.gm, self.rgm, self.modd[i, r:r + 1, base + D: base + 2 * D], "bc0")
        self.bc_row(self.m_tmpf, self.m_rtmpf, self.gpre_d[i, slot:slot + 1, :], "bc1")
        self.bc_row(self.sh, self.rsh, self.modd[i, r:r + 1, base: base + D], "bc2")
        P.op("dve", lambda e: e.scalar_tensor_tensor(out=self.gm[:], in0=self.gm[:], scalar=1.0, in1=self.m_tmpf[:], op0=ALU.add, op1=ALU.mult),
             reads=[self.rgm, self.m_rtmpf], writes=[self.rgm])

    def prep_post(self, i, slot, r, res_w):
        P = self.P
        base = slot * 3 * D
        self.bc_row(self.gg, self.rgg, self.modd[i, r:r + 1, base + 2 * D: base + 3 * D], "bc3")
        self.bc_row(self.m_tmpf, self.m_rtmpf, self.gpost_d[i, slot:slot + 1, :], "bc1")
        P.op("dve", lambda e: e.scalar_tensor_tensor(out=self.gg[:], in0=self.gg[:], scalar=float(res_w), in1=self.m_tmpf[:], op0=ALU.mult, op1=ALU.mult),
             reads=[self.rgg, self.m_rtmpf], writes=[self.rgg])

    def rstd_cols(self, srcs, col0, rk, inv_n):
        P = self.P
        n = len(srcs)
        for j, (ap, rr, junk, rjunk) in enumerate(srcs):
            P.op("act", lambda e, ap=ap, j=j, junk=junk: e.activation(out=junk, in_=ap, func=AF.Square, accum_out=self.st[:, col0 + j:col0 + j + 1]),
                 reads=rr, writes=[rjunk, rk])
        P.op("dve", lambda e: e.tensor_scalar(out=self.st[:, col0:col0 + n], in0=self.st[:, col0:col0 + n], scalar1=float(inv_n), scalar2=EPS,
                                               op0=ALU.mult, op1=ALU.add), reads=[rk], writes=[rk])
        P.op("act", lambda e: e.activation(out=self.st[:, col0:col0 + n], in_=self.st[:, col0:col0 + n], func=AF.Sqrt), reads=[rk], writes=[rk])
        P.op("dve", lambda e: e.reciprocal(out=self.st[:, col0:col0 + n], in_=self.st[:, col0:col0 + n]), reads=[rk], writes=[rk])

    def prenorm_tile(self, t, rcol, rk, tmpf, rtmpf, hb, rhb, dstT, rdstT, part="all"):
        P = self.P
        if part in ("all", "dve"):
            P.op("dve", lambda e: e.scalar_tensor_tensor(out=tmpf, in0=self.X[:, t, :], scalar=rcol, in1=self.gm[:], op0=ALU.mult, op1=ALU.mult),
                 reads=[self.rX[t], rk, self.rgm], writes=[rtmpf])
            P.op("dve", lambda e: e.tensor_tensor(out=hb, in0=tmpf, in1=self.sh[:], op=ALU.add), reads=[rtmpf, self.rsh], writes=[rhb])
        if part == "dve":
            return
        for kc in range(8):
            P.op("pe", lambda e, kc=kc: e.transpose(out=self.PT[:, kc, :], in_=hb[:, kc * 128:(kc + 1) * 128], identity=self.identb[:]),
                 reads=[rhb, self.rConst], writes=[self.rPT])
        P.op("act", lambda e: e.activation(out=dstT, in_=self.PT[:], func=AF.Copy), reads=[self.rPT], writes=[rdstT])

    def postnorm_tile(self, t, y_ap, ry, tmpf, rtmpf, col, rk):
        P = self.P
        junk = tmpf.bitcast(BF16)[:, 0:1024]
        self.rstd_cols([(y_ap, ry, junk, rtmpf)], col, rk, 1.0 / D)
        P.op("dve", lambda e: e.scalar_tensor_tensor(out=tmpf, in0=y_ap, scalar=self.st[:, col:col + 1], in1=self.gg[:], op0=ALU.mult, op1=ALU.mult),
             reads=ry + [rk, self.rgg], writes=[rtmpf])
        P.op("dve", lambda e: e.tensor_tensor(out=self.X[:, t, :], in0=self.X[:, t, :], in1=tmpf, op=ALU.add), reads=[rtmpf, self.rX[t]], writes=[self.rX[t]])

    def ffn(self, i, widx, slot, b, tiles):
        P = self.P
        self.ov_off = 0
        TP = 6
        hT = self.ovv([128, 8, TP * 128], BF16)
        aT = self.ovv([128, NJ, TP * 128], BF16)
        wdt = self.ovv([128, NJ, D], BF16)
        ring = [(self.ovv([128, 8, 256], BF16), self.ovv([128, 8, 256], BF16)) for _ in range(2)]
        tmpf = [self.ovv([128, D], F32), self.ovv([128, D], F32)]
        hb = [self.ovv([128, D], BF16)] * 2
        sgb = [self.ovv([128, 512], F32)] * 2
        rhT = [Res(f"hT{j}") for j in range(TP)]
        raT = [Res(f"aT{j}") for j in range(NJ)]
        rwd = Res("wd")
        rring = [Res("ring0"), Res("ring1")]
        rtmpf = [Res("tmpf0"), Res("tmpf1")]
        rhb = [Res("hb0")] * 2
        rsgb = [Res("sgb0")] * 2
        self.m_tmpf, self.m_rtmpf = tmpf[0], rtmpf[0]
        for q in range(4):
            j0, j1 = q * 6, min(NJ, q * 6 + 6)
            src = self.wd_d[i, widx, j0 * 128:j1 * 128, :].rearrange("(j p) d -> p j d", p=128)
            P.dma("pool", "wd", lambda e, j0=j0, j1=j1, src=src: e.dma_start(out=wdt[:, j0:j1, :], in_=src), writes=[rwd])
        passes = [tiles[k:k + TP] for k in range(0, len(tiles), TP)]
        self._cur_row = None
        slab_ctr = [0]

        def issue_slab(sidx, s_):
            k = sidx % 2
            gsrc = self.wg_d[i, widx, :, s_ * 256:(s_ + 1) * 256].rearrange("(kc p) f -> p kc f", p=128)
            usrc = self.wu_d[i, widx, :, s_ * 256:(s_ + 1) * 256].rearrange("(kc p) f -> p kc f", p=128)
            P.dma("pool", f"w{k}", lambda e: e.dma_start(out=ring[k][0][:], in_=gsrc), writes=[rring[k]])
            P.dma("pool", f"w{k}", lambda e: e.dma_start(out=ring[k][1][:], in_=usrc), writes=[rring[k]])

        def p1_stats(ptiles, pi):
            col0 = 0 if pi % 2 == 0 else 16
            rk = self.rst[0] if pi % 2 == 0 else self.rst[6]
            srcs = [(self.X[:, t, :], [self.rX[t]], hb[0], rhb[0]) for t in ptiles]
            self.rstd_cols(srcs, col0, rk, 1.0 / D)
            return col0, rk

        def p1_tile(j, t, col0, rk):
            row = 2 if t < 2 else b
            if row != self._cur_row:
                self.prep_pre(i, slot, row)
                self._cur_row = row
            self.prenorm_tile(t, self.st[:, col0 + j:col0 + j + 1], rk, tmpf[0], rtmpf[0], hb[0], rhb[0],
                              hT[:, :, j * 128:(j + 1) * 128], rhT[j])

        c0_, rk_ = p1_stats(passes[0], 0)
        for j, t in enumerate(passes[0]):
            p1_tile(j, t, c0_, rk_)
        for pi, ptiles in enumerate(passes):
            npt = len(ptiles)
            T = npt * 128
            nblk = 2 if T > 512 else 1
            bs = T // nblk
            issue_slab(slab_ctr[0], 0)
            for s_ in range(11):
                if s_ + 1 < 11:
                    issue_slab(slab_ctr[0] + 1, s_ + 1)
                k = slab_ctr[0] % 2
                for jj in range(2):
                    j = s_ * 2 + jj
                    for nb in range(nblk):
                        pk = (j * nblk + nb) % 2
                        g_ps = self.PY[pk][:, 0:bs]
                        u_ps = self.PY[pk][:, 512:512 + bs]
                        tl = list(range(nb * bs // 128, (nb + 1) * bs // 128))
                        rh = [rhT[q] for q in tl]
                        for kc in range(8):
                            P.op("pe", lambda e: e.matmul(g_ps, lhsT=ring[k][0][:, kc, jj * 128:(jj + 1) * 128], rhs=hT[:, kc, nb * bs:(nb + 1) * bs],
                                                          start=(kc == 0), stop=(kc == 7)), reads=rh + [rring[k]], writes=[self.rPY[pk][0]])
                        for kc in range(8):
                            P.op("pe", lambda e: e.matmul(u_ps, lhsT=ring[k][1][:, kc, jj * 128:(jj + 1) * 128], rhs=hT[:, kc, nb * bs:(nb + 1) * bs],
                                                          start=(kc == 0), stop=(kc == 7)), reads=rh + [rring[k]], writes=[self.rPY[pk][1]])
                        P.op("act", lambda e: e.activation(out=sgb[pk][:, 0:bs], in_=g_ps, func=AF.Silu), reads=[self.rPY[pk][0]], writes=[rsgb[pk]])
                        P.op("dve", lambda e: e.tensor_tensor(out=aT[:, j, nb * bs:(nb + 1) * bs], in0=sgb[pk][:, 0:bs], in1=u_ps, op=ALU.mult),
                             reads=[rsgb[pk], self.rPY[pk][1]], writes=[raT[j]])
                slab_ctr[0] += 1
            nxt = passes[pi + 1] if pi + 1 < len(passes) else None
            if nxt is not None:
                c0n, rkn = p1_stats(nxt, pi + 1)
            for j, t in enumerate(ptiles):
                row = 2 if t < 2 else b
                pk = j % 2
                y = self.PY[pk]
                for half in range(2):
                    for jf in range(NJ):
                        P.op("pe", lambda e: e.matmul(y[:, half * 512:(half + 1) * 512], lhsT=aT[:, jf, j * 128:(j + 1) * 128],
                                                      rhs=wdt[:, jf, half * 512:(half + 1) * 512], start=(jf == 0), stop=(jf == NJ - 1)),
                             reads=[raT[jf], rwd], writes=[self.rPY[pk][half]])
                if nxt is not None and j < len(nxt):
                    p1_tile(j, nxt[j], c0n, rkn)
                self._post_row(i, slot, row, 0.5)
                self.postnorm_tile(t, y[:, :], [self.rPY[pk][0], self.rPY[pk][1]], tmpf[1], rtmpf[1], 8 + pk, self.rst[1 + pk])
        self._post_state = None

    _post_state = None

    def _post_row(self, i, slot, row, res_w):
        key = (i, slot, row)
        if self._post_state != key:
            self.prep_post(i, slot, row, res_w)
            self._post_state = key

    def rsqrt_inplace(self, ap, res, inv_n):
        P = self.P
        P.op("dve", lambda e: e.tensor_scalar(out=ap, in0=ap, scalar1=float(inv_n), scalar2=EPS, op0=ALU.mult, op1=ALU.add), reads=[res], writes=[res])
        P.op("act", lambda e: e.activation(out=ap, in_=ap, func=AF.Sqrt), reads=[res], writes=[res])
        P.op("dve", lambda e: e.reciprocal(out=ap, in_=ap), reads=[res], writes=[res])

    def rope(self, src3, rsrc, t, nh, dst3, rdst, ra, rb, rra, rrb):
        P = self.P
        cos = self.cs[:, t, 0:32].unsqueeze(1).to_broadcast([128, nh, 32])
        sin = self.cs[:, t, 32:64].unsqueeze(1).to_broadcast([128, nh, 32])
        x1, x2 = src3[:, :, 0:32], src3[:, :, 32:64]
        a, b_ = ra[:, 0:nh, :], rb[:, 0:nh, :]
        P.op("dve", lambda e: e.tensor_tensor(out=a, in0=x1, in1=cos, op=ALU.mult), reads=rsrc + [self.rConst], writes=[rra])
        P.op("dve", lambda e: e.tensor_tensor(out=b_, in0=x2, in1=sin, op=ALU.mult), reads=rsrc + [self.rConst], writes=[rrb])
        P.op("pool", lambda e: e.tensor_tensor(out=dst3[:, :, 0:32], in0=a, in1=b_, op=ALU.subtract), reads=[rra, rrb], writes=[rdst])
        P.op("dve", lambda e: e.tensor_tensor(out=a, in0=x2, in1=cos, op=ALU.mult), reads=rsrc + [self.rConst], writes=[rra])
        P.op("dve", lambda e: e.tensor_tensor(out=b_, in0=x1, in1=sin, op=ALU.mult), reads=rsrc + [self.rConst], writes=[rrb])
        P.op("pool", lambda e: e.tensor_tensor(out=dst3[:, :, 32:64], in0=a, in1=b_, op=ALU.add), reads=[rra, rrb], writes=[rdst])

    def head_norm(self, ps2, rps, nh, gain_t, sq, rsq, col0, rk):
        P = self.P
        v3 = ps2.rearrange("p (h d) -> p h d", h=nh)
        sq2 = sq[:, 0:nh * 64]
        P.op("act", lambda e: e.activation(out=sq2, in_=ps2, func=AF.Square), reads=rps, writes=[rsq])
        P.op("dve", lambda e: e.tensor_reduce(out=self.st[:, col0:col0 + nh], in_=sq2.rearrange("p (h d) -> p h d", h=nh), axis=AX.X, op=ALU.add),
             reads=[rsq], writes=[rk])
        self.rsqrt_inplace(self.st[:, col0:col0 + nh], rk, 1.0 / 64)
        rbc = self.st[:, col0:col0 + nh].unsqueeze(2).to_broadcast([128, nh, 64])
        gbc = gain_t[:, :].unsqueeze(1).to_broadcast([128, nh, 64])
        P.op("dve", lambda e: e.tensor_tensor(out=v3, in0=v3, in1=rbc, op=ALU.mult), reads=rps + [rk], writes=rps)
        P.op("dve", lambda e: e.tensor_tensor(out=v3, in0=v3, in1=gbc, op=ALU.mult), reads=rps + [self.rConst], writes=rps)

    def attend(self, jobs, mode="ab", bg=None, bg_every=4):
        P = self.P
        steps = []
        for ji, jb in enumerate(jobs):
            for ki, kt in enumerate(jb["kts"]):
                steps.append((ji, ki, kt))
        ptf = self.PT[:].rearrange("p a b -> p (a b)").bitcast(F32)
        if mode == "ab":
            Sb = [(self.PA, self.rPA), (self.PB, self.rPB), (ptf, self.rPT)]
            Ob = [(self.PC, self.rPC), (self.PY[0][:, 0:512], self.rPY[0][0])]
            Rb = [(self.PY[1][:, 0:512], self.rPY[1][0]), (self.PY[0][:, 512:1024], self.rPY[0][1])]
        else:
            Sb = [(self.PA, self.rPA), (self.PB, self.rPB)]
            Ob = [(self.PC, self.rPC), (self.PY[1][:, 0:512], self.rPY[1][0])]
            Rb = [(None, None), (None, None)]
        NS = len(Sb)
        LA = NS - 1

        def emit_S(si):
            ji, ki, kt = steps[si]
            jb = jobs[ji]
            S, rS = Sb[si % NS]
            nq = jb["nq"]
            mk = jb["mask"](kt) if jb.get("mask") else None
            P.op("pe", lambda e: e.matmul(S[:, 0:nq], lhsT=jb["kT"](kt), rhs=jb["q"], start=True, stop=(mk is None)),
                 reads=jb["rk"] + jb["rq"], writes=[rS])
            if mk is not None:
                P.op("pe", lambda e: e.matmul(S[:, 0:nq], lhsT=self.identb[:], rhs=mk, start=False, stop=True),
                     reads=[self.rConst], writes=[rS])
            Pt, rPt = self.Pring[si % NS]
            P.op("act", lambda e: e.activation(out=Pt[:, 0:nq], in_=S[:, 0:nq], func=AF.Exp, scale=0.125), reads=[rS], writes=[rPt])

        for si in range(min(LA, len(steps))):
            emit_S(si)
        bg = list(bg) if bg else []
        for si, (ji, ki, kt) in enumerate(steps):
            if bg and si % bg_every == 2:
                bg.pop(0)()
            if si + LA < len(steps):
                emit_S(si + LA)
            jb = jobs[ji]
            nq = jb["nq"]
            O, rO = Ob[ji % 2]
            Rs, rR = Rb[ji % 2]
            Pt, rPt = self.Pring[si % NS]
            first, last = ki == 0, ki == len(jb["kts"]) - 1
            P.op("pe", lambda e: e.matmul(O[:, 0:nq], lhsT=jb["v"](kt), rhs=Pt[:, 0:nq], start=first, stop=last),
                 reads=[rPt] + jb["rv"], writes=[rO])
            if jb["rs_sep"]:
                P.op("pe", lambda e: e.matmul(Rs[:, 0:nq], lhsT=self.onesb[:, :], rhs=Pt[:, 0:nq], start=first, stop=last),
                     reads=[rPt, self.rConst], writes=[rR])
            if last:
                jb["fin"](O, rO, Rs, rR)
        while bg:
            bg.pop(0)()

    @staticmethod
    def skewed(n, stages):
        ns = len(stages)
        out = []
        for s_ in range(n + ns - 1):
            for k in reversed(range(ns)):
                j = s_ - k
                if 0 <= j < n:
                    out.append(lambda k=k, j=j: stages[k](j))
        return out

    def outproj_mm(self, j, OT, rOT, wo, rwo, nchunk, yk=1):
        P = self.P
        y = self.PY[yk]
        for half in range(2):
            for c in range(nchunk):
                P.op("pe", lambda e: e.matmul(y[:, half * 512:(half + 1) * 512], lhsT=OT[:, c, j * 128:(j + 1) * 128],
                                              rhs=wo[:, c, half * 512:(half + 1) * 512], start=(c == 0), stop=(c == nchunk - 1)),
                     reads=[rOT, rwo], writes=[self.rPY[yk][half]])

    def outproj_post(self, t, i, row, col, yk=1, tmp=None, rtmp=None):
        y = self.PY[yk]
        self._post_row(i, 1, row, 1.0)
        self.postnorm_tile(t, y[:, :], [self.rPY[yk][0], self.rPY[yk][1]], tmp if tmp is not None else self.m_tmpf,
                           rtmp if rtmp is not None else self.m_rtmpf, col, self.rst[3])

    def outproj_tile(self, t, j, OT, rOT, wo, rwo, nchunk, i, row, col):
        self.outproj_mm(j, OT, rOT, wo, rwo, nchunk)
        self.outproj_post(t, i, row, col)

    def mixer_ab(self, b):
        P = self.P
        self.ov_off = 0
        self._post_state = None
        T = NT * 128
        KTA = [self.ovv([128, T], BF16), self.ovv([128, T], BF16)]
        KTB = self.ovv([128, 4, T], BF16)
        VA = self.ovv([128, NT, 192], BF16)
        VB = self.ovv([128, NT, 512], BF16)
        wq = self.ovv([128, 8, 1024], BF16)
        wx = self.ovv([128, 8, 256], BF16)
        wo = self.ovv([128, 8, D], BF16)
        tmpf = self.ovv([128, D], F32)
        hb = self.ovv([128, D], BF16)
        hTt = self.ovv([128, 8, 128], BF16)
        ra = self.ovv([128, 16, 32], F32)
        rb = self.ovv([128, 16, 32], F32)
        QT = self.ovv([128, 8, 512], BF16)
        OT = self.ovv([128, 8, 512], BF16)
        rK, rV, rwq, rwx, rwo = Res("K"), Res("V"), Res("wq"), Res("wx"), Res("wo")
        rtmpf, rhb, rhTt, rra, rrb, rQT, rOT = (Res(n) for n in ("tmpf", "hb", "hTt", "ra", "rb", "QT", "OT"))
        qrot, rqrot = hb, rhb
        QTBhi = wx.rearrange("p a b -> p (a b)").rearrange("p (h t) -> p h t", h=4)
        self.Pring = [(self.ovv([128, 512], BF16), Res("p0")), (self.ovv([128, 512], BF16), Res("p1")),
                      (hTt.rearrange("p a b -> p (a b)")[:, 0:512], rhTt)]
        self.m_tmpf, self.m_rtmpf = tmpf, rtmpf
        rinv = tmpf[:, 0:512]
        o1 = tmpf[:, 512:1024]
        sqf = ra.rearrange("p a b -> p (a b)")
        dsq = rb.rearrange("p a b -> p (a b)").bitcast(BF16)[:, 0:512]
        rsd = ra.rearrange("p a b -> p (a b)")
        W = self.winab_d
        kc_view = lambda c0, c1: W[:, c0:c1].rearrange("(kc p) f -> p kc f", p=128)
        for (d0, s0, s1) in ((0, 512, 640), (128, 1280, 1792), (640, 640, 768), (768, 1792, 2048)):
            P.dma("pool", "wq", lambda e, d0=d0, s0=s0, s1=s1: e.dma_start(out=wq[:, :, d0:d0 + (s1 - s0)], in_=kc_view(s0, s1)), writes=[rwq])
        P.dma("pool", "w2", lambda e: e.dma_start(out=wx[:, :, :], in_=kc_view(2048, 2304)), writes=[rwx])
        P.dma("pool", "wo", lambda e: e.dma_start(out=wo[:], in_=self.woutab_d.rearrange("(c p) d -> p c d", p=128)), writes=[rwo])
        P.op("pool", lambda e: e.memset(KTA[0][64:128, :], 0.0), writes=[rK])
        P.op("pool", lambda e: e.memset(KTA[1][0:64, :], 0.0), writes=[rK])
        P.op("pool", lambda e: e.memset(VA[:, :, 64:128], 1.0), writes=[rV])
        P.op("pool", lambda e: e.memset(QT[64:128, 4:8, :], 0.0), writes=[rQT])
        rk = self.rst[0]
        self.rstd_cols([(self.X[:, t, :], [self.rX[t]], hb[:], rhb) for t in range(NT)], 0, rk, 1.0 / D)
        self._cur_row = None

        def stA(t):
            row = 2 if t < 2 else b
            if row != self._cur_row:
                self.prep_pre(0, 1, row)
                self._cur_row = row
            self.prenorm_tile(t, self.st[:, t:t + 1], rk, tmpf[:], rtmpf, hb[:], rhb, hTt[:], rhTt)

        kvsets = [((self.PY[0][:, 0:512], self.rPY[0][0]), (self.PY[0][:, 512:1024], self.rPY[0][1]), (self.PY[1][:, 0:256], self.rPY[1][0])),
                  ((self.PA[:, :], self.rPA), (self.PB[:, :], self.rPB), (self.PC[:, 0:256], self.rPC))]

        def s1B(t):
            ks = kvsets[t % 2]
            for (ps_ap, rps), (wsrc, rw, c0, c1) in zip(ks, ((wq, rwq, 0, 512), (wq, rwq, 512, 1024), (wx, rwx, 0, 256))):
                for kc in range(8):
                    P.op("pe", lambda e: e.matmul(ps_ap, lhsT=hTt[:, kc, :], rhs=wsrc[:, kc, c0:c1], start=(kc == 0), stop=(kc == 7)),
                         reads=[rhTt, rw], writes=[rps])

        def s1C(t):
            (p0, r0), (p1, r1), (p2, r2) = kvsets[t % 2]
            self.head_norm(p0[:, 0:128], [r0], 2, self.kgt, sqf, rra, 20, self.rst[4])
            self.rope(p0[:, 0:512].rearrange("p (h d) -> p h d", h=8), [r0], t, 8,
                      qrot[:, 0:512].rearrange("p (h d) -> p h d", h=8), rqrot, ra, rb, rra, rrb)
            self.rope(p1[:, 0:128].rearrange("p (h d) -> p h d", h=2), [r1], t, 2,
                      qrot[:, 512:640].rearrange("p (h d) -> p h d", h=2), rqrot, ra, rb, rra, rrb)

        def s1D(t):
            (p0, r0), (p1, r1), (p2, r2) = kvsets[t % 2]
            for c in range(5):
                P.op("pe", lambda e: e.transpose(out=self.PT[:, c, :], in_=qrot[:, c * 128:(c + 1) * 128], identity=self.identb[:]),
                     reads=[rqrot, self.rConst], writes=[self.rPT])
            tc_ = slice(t * 128, (t + 1) * 128)
            P.op("act", lambda e: e.activation(out=KTA[0][0:64, tc_], in_=self.PT[0:64, 0, :], func=AF.Copy), reads=[self.rPT], writes=[rK])
            P.op("act", lambda e: e.activation(out=KTA[1][64:128, tc_], in_=self.PT[64:128, 0, :], func=AF.Copy), reads=[self.rPT], writes=[rK])
            P.op("act", lambda e: e.activation(out=KTB[:, :, tc_], in_=self.PT[:, 1:5, :], func=AF.Copy), reads=[self.rPT], writes=[rK])
            P.op("act", lambda e: e.activation(out=VA[:, t, 0:64], in_=p1[:, 128:192], func=AF.Copy), reads=[r1], writes=[rV])
            P.op("act", lambda e: e.activation(out=VA[:, t, 128:192], in_=p1[:, 192:256], func=AF.Copy), reads=[r1], writes=[rV])
            P.op("act", lambda e: e.activation(out=VB[:, t, 0:256], in_=p1[:, 256:512], func=AF.Copy), reads=[r1], writes=[rV])
            P.op("act", lambda e: e.activation(out=VB[:, t, 256:512], in_=p2[:, 0:256], func=AF.Copy), reads=[r2], writes=[rV])

        for th in self.skewed(NT, [stA, s1B, lambda t: (s1C(t), s1D(t))]):
            th()
        if SUB < 2:
            return
        for c in range(4):
            for g in range(2):
                h = g * 4 + c
                pos = c * 2 + g
                P.dma("pool", "wq", lambda e, h=h, pos=pos: e.dma_start(out=wq[:, :, pos * 64:(pos + 1) * 64], in_=kc_view(h * 64, (h + 1) * 64)), writes=[rwq])
        P.dma("pool", "wq", lambda e: e.dma_start(out=wq[:, :, 512:1024], in_=kc_view(768, 1280)), writes=[rwq])
        P.op("pool", lambda e: e.memset(QTBhi[0:64, :, :], 0.0), reads=[rwx], writes=[rwx])
        blocks = [[0, 1]] + [list(range(2 + 4 * q, 6 + 4 * q)) for q in range(4)]
        for blk in blocks:
            nq = len(blk) * 128
            kts = [0, 1] if blk[0] < 2 else list(range(NT))
            row = 2 if blk[0] < 2 else b
            qsets = [(self.PY[0], self.rPY[0]), (self.PY[1], self.rPY[1])]

            def qB(j):
                qp, rqp = qsets[j % 2]
                for half in range(2):
                    for kc in range(8):
                        P.op("pe", lambda e: e.matmul(qp[:, half * 512:(half + 1) * 512], lhsT=hTt[:, kc, :],
                                                      rhs=wq[:, kc, half * 512:(half + 1) * 512], start=(kc == 0), stop=(kc == 7)),
                             reads=[rhTt, rwq], writes=[rqp[half]])

            def qC(j):
                qp, rqp = qsets[j % 2]
                self.head_norm(qp[:, 0:512], [rqp[0]], 8, self.qgt, sqf, rra, 24, self.rst[5])
                self.rope(qp[:, :].rearrange("p (h d) -> p h d", h=16), [rqp[0], rqp[1]], blk[j], 16,
                          qrot[:, :].rearrange("p (h d) -> p h d", h=16), rqrot, ra, rb, rra, rrb)

            def qD(j):
                for c in range(8):
                    P.op("pe", lambda e: e.transpose(out=self.PT[:, c, :], in_=qrot[:, c * 128:(c + 1) * 128], identity=self.identb[:]),
                         reads=[rqrot, self.rConst], writes=[self.rPT])
                js = slice(j * 128, (j + 1) * 128)
                P.op("act", lambda e: e.activation(out=QT[:, 0:4, js], in_=self.PT[:, 0:4, :], func=AF.Copy), reads=[self.rPT], writes=[rQT])
                P.op("act", lambda e: e.activation(out=QT[0:64, 4:8, js], in_=self.PT[0:64, 4:8, :], func=AF.Copy), reads=[self.rPT], writes=[rQT])
                P.op("act", lambda e: e.activation(out=QTBhi[64:128, :, js], in_=self.PT[64:128, 4:8, :], func=AF.Copy), reads=[self.rPT], writes=[rwx])

            for th in self.skewed(len(blk), [lambda j: stA(blk[j]), qB, lambda j: (qC(j), qD(j))]):
                th()
            if SUB < 3:
                continue
            jobs = []
            for c in range(4):
                for g in range(2):
                    h = g * 4 + c
                    ps_, ch = (h % 2) * 64, h // 2
                    ob, rsb = (0, 64) if g == 0 else (64, 0)

                    def fin(O, rO, Rs, rR, ps_=ps_, ch=ch, ob=ob, rsb=rsb):
                        P.op("dve", lambda e: e.reciprocal(out=rinv[0:64, 0:nq], in_=O[rsb:rsb + 64, 0:nq]), reads=[rO], writes=[rtmpf])
                        P.op("dve", lambda e: e.tensor_tensor(out=OT[ps_:ps_ + 64, ch, 0:nq], in0=O[ob:ob + 64, 0:nq], in1=rinv[0:64, 0:nq], op=ALU.mult),
                             reads=[rO, rtmpf], writes=[rOT])
                    jobs.append(dict(kT=lambda kt, g=g: KTA[g][:, kt * 128:(kt + 1) * 128], q=QT[:, c, 0:nq],
                                     v=lambda kt, g=g: VA[:, kt, g * 64:g * 64 + 128], nq=nq, kts=kts, rs_sep=False,
                                     rk=[rK], rq=[rQT], rv=[rV], fin=fin))
            for hb_ in range(4):
                for cm in range(2):
                    def fin(O, rO, Rs, rR, hb_=hb_, cm=cm):
                        P.op("dve", lambda e: e.reciprocal(out=rinv[:, 0:nq], in_=Rs[:, 0:nq]), reads=[rR], writes=[rtmpf])
                        if cm == 0:
                            P.op("dve", lambda e: e.tensor_tensor(out=o1[:, 0:nq], in0=O[:, 0:nq], in1=rinv[:, 0:nq], op=ALU.mult), reads=[rO, rtmpf], writes=[rtmpf])
                            return
                        P.op("dve", lambda e: e.tensor_tensor(out=rinv[:, 0:nq], in0=O[:, 0:nq], in1=rinv[:, 0:nq], op=ALU.mult), reads=[rO, rtmpf], writes=[rtmpf])
                        P.op("dve", lambda e: e.scalar_tensor_tensor(out=o1[:, 0:nq], in0=rinv[:, 0:nq], scalar=self.lamt[:, 4:5], in1=o1[:, 0:nq],
                                                                      op0=ALU.mult, op1=ALU.add), reads=[rtmpf, self.rlam], writes=[rtmpf])
                        P.op("pool", lambda e: e.tensor_tensor(out=dsq[:, 0:nq], in0=o1[:, 0:nq], in1=o1[:, 0:nq], op=ALU.mult), reads=[rtmpf], writes=[rrb])
                        ssd, rssd = self.PY[1][:, 512:1024], self.rPY[1][1]
                        P.op("pe", lambda e: e.matmul(ssd[:, 0:nq], lhsT=self.onesb[:, :], rhs=dsq[:, 0:nq], start=True, stop=True),
                             reads=[rrb, self.rConst], writes=[rssd])
                        P.op("dve", lambda e: e.tensor_scalar(out=rsd[:, 0:nq], in0=ssd[:, 0:nq], scalar1=1.0 / 128, scalar2=EPS, op0=ALU.mult, op1=ALU.add),
                             reads=[rssd], writes=[rra])
                        P.op("act", lambda e: e.activation(out=rsd[:, 0:nq], in_=rsd[:, 0:nq], func=AF.Sqrt), reads=[rra], writes=[rra])
                        P.op("dve", lambda e: e.reciprocal(out=rsd[:, 0:nq], in_=rsd[:, 0:nq]), reads=[rra], writes=[rra])
                        P.op("dve", lambda e: e.scalar_tensor_tensor(out=OT[:, 4 + hb_, 0:nq], in0=o1[:, 0:nq], scalar=self.sgc[:, 0:1], in1=rsd[:, 0:nq],
                                                                      op0=ALU.mult, op1=ALU.mult), reads=[rtmpf, rra, self.rConst], writes=[rOT])
                    qsrc, rq_ = (QT[:, 4 + hb_, 0:nq], rQT) if cm == 0 else (QTBhi[:, hb_, 0:nq], rwx)
                    jobs.append(dict(kT=lambda kt, hb_=hb_: KTB[:, hb_, kt * 128:(kt + 1) * 128], q=qsrc,
                                     v=lambda kt, hb_=hb_: VB[:, kt, hb_ * 128:(hb_ + 1) * 128], nq=nq, kts=kts, rs_sep=True,
                                     rk=[rK], rq=[rq_], rv=[rV], fin=fin))
            self.attend(jobs)
            if SUB < 4:
                continue
            for th in self.skewed(len(blk), [lambda j: self.outproj_mm(j, OT, rOT, wo, rwo, 8, yk=(j + 1) % 2),
                                             lambda j: self.outproj_post(blk[j], 0, row, 40 + (j % 2), yk=(j + 1) % 2)]):
                th()
        self._post_state = None

    def mixer_c(self, b):
        P = self.P
        self.ov_off = 0
        self._post_state = None
        T = NT * 128
        skb = self.ovv([128, 16, 128], F32)
        maskt = self.ovv([128, 2, 512], BF16)
        KTC = [self.ovv([128, 2, T], BF16), self.ovv([128, 2, T], BF16)]
        VC = self.ovv([128, NT, 384], BF16)
        wb = self.ovv([128, 8, 1024], BF16)
        wo = self.ovv([128, 8, D], BF16)
        tmpf = self.ovv([128, D], F32)
        tmpf2 = tmpf
        hb = self.ovv([128, D], BF16)
        hTts = [(self.ovv([128, 8, 128], BF16), Res("hT0")), (self.ovv([128, 8, 128], BF16), Res("hT1"))]
        qrots = [(self.ovv([128, D], BF16), Res("qr0"))] * 2
        ra = self.ovv([128, 16, 32], F32)
        rb = self.ovv([128, 16, 32], F32)
        QTs = [(self.ovv([128, 4, 8, 128], BF16), Res("QT0")), (self.ovv([128, 4, 8, 128], BF16), Res("QT1"))]
        OT = self.ovv([128, 4, 8, 128], BF16)
        self.Pring = [(self.ovv([128, 512], BF16), Res("p0")), (self.ovv([128, 512], BF16), Res("p1"))]
        rinv = self.ovv([128, 512], F32)
        rtmpf2 = Res("rinv")
        rK, rV, rwb, rwo = Res("K"), Res("V"), Res("wb"), Res("wo")
        rtmpf, rhb, rra, rrb, rOT, rsk = (Res(n) for n in ("tmpf", "hb", "ra", "rb", "OT", "sk"))
        rtmpfb = rtmpf
        if not CT6:
            qrots = [(hb, rhb)] * 2
        self.m_tmpf, self.m_rtmpf = tmpf, rtmpf
        skbf = skb.rearrange("p h t -> p (h t)")
        PAb = self.PT
        W = self.winc_d
        kc_view = lambda c0, c1: W[:, c0:c1].rearrange("(kc p) f -> p kc f", p=128)
        P.dma("pool", "wq", lambda e: e.dma_start(out=wb[:, :, 0:512], in_=kc_view(1024, 1536)), writes=[rwb])
        P.dma("pool", "wo", lambda e: e.dma_start(out=wo[:], in_=self.woutc_d.rearrange("(c p) d -> p c d", p=128)), writes=[rwo])
        P.dma("sp", "ld5", lambda e: e.dma_start(out=maskt[:], in_=self.mask_d[:, :, :]), writes=[rsk])
        P.op("dve", lambda e: e.tensor_copy(out=skb[:], in_=self.skt[:, :].unsqueeze(2).to_broadcast([128, 16, 128])), reads=[self.rConst], writes=[rsk])
        P.op("pool", lambda e: e.memset(KTC[0][64:128, :, :], 0.0), writes=[rK])
        P.op("pool", lambda e: e.memset(KTC[1][0:64, :, :], 0.0), writes=[rK])
        P.op("pool", lambda e: e.memset(VC[:, :, 64:128], 1.0), writes=[rV])
        P.op("pool", lambda e: e.memset(VC[:, :, 256:320], 1.0), writes=[rV])
        rk = self.rst[0]
        self.rstd_cols([(self.X[:, t, :], [self.rX[t]], hb[:], rhb) for t in range(NT)], 0, rk, 1.0 / D)
        self._cur_row = None

        def stA(t, j, part="all"):
            row = 2 if t < 2 else b
            if row != self._cur_row:
                self.prep_pre(1, 1, row)
                self._cur_row = row
            hTt, rhTt = hTts[(j % 2) * CT5]
            self.prenorm_tile(t, self.st[:, t:t + 1], rk, tmpf[:], rtmpf, hb[:], rhb, hTt[:], rhTt, part=part)

        def s1B(j):
            hTt, rhTt = hTts[(j % 2) * CT5]
            ps = self.PY[0][:, (j % 2) * CT4 * 512:(j % 2) * CT4 * 512 + 512]
            for kc in range(8):
                P.op("pe", lambda e: e.matmul(ps, lhsT=hTt[:, kc, :], rhs=wb[:, kc, 0:512], start=(kc == 0), stop=(kc == 7)),
                     reads=[rhTt, rwb], writes=[self.rPY[0][(j % 2) * CT4]])

        def s1C(j):
            t = j
            ps = self.PY[0][:, (j % 2) * CT4 * 512:(j % 2) * CT4 * 512 + 512]
            rps = self.rPY[0][(j % 2) * CT4]
            qrot, rqrot = qrots[j % 2]
            self.rope(ps[:, 0:256].rearrange("p (h d) -> p h d", h=4), [rps], t, 4,
                      qrot[:, 0:256].rearrange("p (h d) -> p h d", h=4), rqrot, ra, rb, rra, rrb)
            P.op("act", lambda e: e.activation(out=VC[:, t, 0:64], in_=ps[:, 256:320], func=AF.Copy), reads=[rps], writes=[rV])
            P.op("act", lambda e: e.activation(out=VC[:, t, 128:256], in_=ps[:, 320:448], func=AF.Copy), reads=[rps], writes=[rV])
            P.op("act", lambda e: e.activation(out=VC[:, t, 320:384], in_=ps[:, 448:512], func=AF.Copy), reads=[rps], writes=[rV])

        def s1D(j):
            t = j
            qrot, rqrot = qrots[j % 2]
            for c in range(2):
                P.op("pe", lambda e: e.transpose(out=PAb[:, c, :], in_=qrot[:, c * 128:(c + 1) * 128], identity=self.identb[:]),
                     reads=[rqrot, self.rConst], writes=[self.rPT])
            tc_ = slice(t * 128, (t + 1) * 128)
            P.op("act", lambda e: e.activation(out=KTC[0][0:64, :, tc_], in_=PAb[0:64, 0:2, :], func=AF.Copy), reads=[self.rPT], writes=[rK])
            P.op("act", lambda e: e.activation(out=KTC[1][64:128, :, tc_], in_=PAb[64:128, 0:2, :], func=AF.Copy), reads=[self.rPT], writes=[rK])

        if CT1:
            for th in self.skewed(NT, [lambda j: stA(j, j), s1B, s1C, s1D]):
                th()
        else:
            for j in range(NT):
                stA(j, j); s1B(j); s1C(j); s1D(j)
        for h in range(16):
            gp, e_, i_ = h // 8, (h % 8) // 4, h % 4
            pos = (gp * 4 + i_) * 2 + e_
            P.dma("pool", "wq", lambda e, h=h, pos=pos: e.dma_start(out=wb[:, :, pos * 64:(pos + 1) * 64], in_=kc_view(h * 64, (h + 1) * 64)), writes=[rwb])
        voff = (0, 64, 192, 256)
        blocks = [list(range(2 + 4 * q, 6 + 4 * q)) for q in range(4)]

        def qstages(n):
            QTn, rQTn = QTs[n % 2]
            blk = blocks[n]

            def qB(j):
                hTt, rhTt = hTts[(j % 2) * CT5]
                for half in range(2):
                    for kc in range(8):
                        P.op("pe", lambda e: e.matmul(self.PY[0][:, half * 512:(half + 1) * 512], lhsT=hTt[:, kc, :],
                                                      rhs=wb[:, kc, half * 512:(half + 1) * 512], start=(kc == 0), stop=(kc == 7)),
                             reads=[rhTt, rwb], writes=[self.rPY[0][half]])

            def qC(j):
                qrot, rqrot = qrots[j % 2]
                self.rope(self.PY[0][:, :].rearrange("p (h d) -> p h d", h=16), [self.rPY[0][0], self.rPY[0][1]], blk[j], 16,
                          qrot[:, :].rearrange("p (h d) -> p h d", h=16), rqrot, ra, rb, rra, rrb)

            def qD(j):
                qrot, rqrot = qrots[j % 2]
                for c in range(8):
                    P.op("pe", lambda e: e.transpose(out=self.PT[:, c, :], in_=qrot[:, c * 128:(c + 1) * 128], identity=self.identb[:]),
                         reads=[rqrot, self.rConst], writes=[self.rPT])
                P.op("act", lambda e: e.activation(out=QTn[:, j, :, :], in_=self.PT[:], func=AF.Copy), reads=[self.rPT], writes=[rQTn])
            if not CT7:
                return [lambda j=j, f=f: f(j) for j in range(4) for f in (lambda j: stA(blk[j], j), qB, qC, qD)]
            return self.skewed(4, [lambda j: stA(blk[j], j, "dve"), lambda j: stA(blk[j], j, "pe"), qB, qC, qD])

        for th in qstages(0):
            th()
        for n, blk in enumerate(blocks):
            QTn, rQTn = QTs[n % 2]
            QTf = QTn.rearrange("p j c t -> p j (c t)")
            jobs = []
            for j, t in enumerate(blk):
                kts = [0, 1] + [k for k in (t - 1, t, t + 1) if 2 <= k < NT]

                def mask(kt, t=t):
                    if kt == t - 1 and kt >= 2:
                        return maskt[:, 0, :]
                    if kt == t + 1:
                        return maskt[:, 1, :]
                    return None
                for g in range(4):
                    gp, e_ = g // 2, g % 2
                    ob, rsb = (0, 64) if e_ == 0 else (64, 0)

                    def fin(O, rO, Rs, rR, g=g, ob=ob, rsb=rsb, j=j):
                        P.op("dve", lambda e: e.tensor_tensor(out=rinv[0:64, :], in0=O[rsb:rsb + 64, 0:512], in1=skbf[0:64, g * 512:(g + 1) * 512], op=ALU.add),
                             reads=[rO, rsk], writes=[rtmpf2])
                        P.op("dve", lambda e: e.reciprocal(out=rinv[0:64, :], in_=rinv[0:64, :]), reads=[rtmpf2], writes=[rtmpf2])
                        for i_ in range(4):
                            h = 4 * g + i_
                            P.op("dve", lambda e: e.tensor_tensor(out=OT[(h % 2) * 64:(h % 2) * 64 + 64, j, h // 2, :], in0=O[ob:ob + 64, i_ * 128:(i_ + 1) * 128],
                                                                   in1=rinv[0:64, i_ * 128:(i_ + 1) * 128], op=ALU.mult), reads=[rO, rtmpf2], writes=[rOT])
                    jobs.append(dict(kT=lambda kt, gp=gp, e_=e_: KTC[e_][:, gp, kt * 128:(kt + 1) * 128],
                                     q=QTf[:, j, gp * 512:(gp + 1) * 512], v=lambda kt, g=g: VC[:, kt, voff[g]:voff[g] + 128],
                                     nq=512, kts=kts, mask=mask, rs_sep=False, rk=[rK, rsk], rq=[rQTn], rv=[rV], fin=fin))
            bg = qstages(n + 1) if n + 1 < len(blocks) else None
            if not CT2 and bg:
                for th in bg:
                    th()
                bg = None
            self.attend(jobs, mode="c", bg=bg, bg_every=4)
            if CT3:
                for th in self.skewed(4, [lambda j: self.outproj_mm(0, OT[:, j, :, :], rOT, wo, rwo, 8, yk=(j + 1) % 2),
                                          lambda j: self.outproj_post(blk[j], 1, b, 40 + (j % 2), yk=(j + 1) % 2, tmp=tmpf2, rtmp=rtmpfb)]):
                    th()
            else:
                for j in range(4):
                    self.outproj_mm(0, OT[:, j, :, :], rOT, wo, rwo, 8, yk=1)
                    self.outproj_post(blk[j], 1, b, 40 + (j % 2), yk=1, tmp=tmpf2, rtmp=rtmpfb)
        self._post_state = None


_NC_CACHE = {}


def _consts():
    ident_bf = np.eye(128, dtype=np.float32).astype(ml_dtypes.bfloat16)
    ident_f = np.eye(128, dtype=np.float32)
    pos = np.arange(L)
    row = (pos // 64).astype(np.float32)
    col = (pos % 64).astype(np.float32)
    inv = (10000.0 ** (-np.arange(0, 32, 2, dtype=np.float32) / 32)).astype(np.float32)
    ang = np.concatenate([row[:, None] * inv, col[:, None] * inv], axis=-1).astype(np.float32)
    cos = np.cos(ang).astype(np.float32)
    sin = np.sin(ang).astype(np.float32)
    cs = np.zeros((128, NT, 64), np.float32)
    cs[:, 0:2, 0:32] = 1.0
    cs[:, 2:, 0:32] = cos.reshape(16, 128, 32).transpose(1, 0, 2)
    cs[:, 2:, 32:64] = sin.reshape(16, 128, 32).transpose(1, 0, 2)
    kk = np.arange(128)[:, None]
    qq = np.arange(128)[None, :]
    m0 = np.where(kk >= qq, 0.0, -30000.0).astype(np.float32)
    m1 = np.where(kk <= qq, 0.0, -30000.0).astype(np.float32)
    maskb = np.stack([np.tile(m0, (1, 4)), np.tile(m1, (1, 4))], axis=1).astype(ml_dtypes.bfloat16)
    return dict(ident_bf=ident_bf, ident_f=ident_f, cs=cs, maskb=maskb)


def kernel(x, c, ctx, c_ctx, w_mod, b_mod, g_pre, g_post, w_ffn_gate, w_ffn_up, w_ffn_down,
           w_in_ab, w_out_ab, q_gain_a, k_gain_a, lam_q1, lam_k1, lam_q2, lam_k2, sub_gain_b,
           w_in_c, w_out_c, sink_c):
    f = lambda a: np.ascontiguousarray(np.asarray(a, dtype=np.float32))
    x, c, ctx, c_ctx = f(x), f(c), f(ctx), f(c_ctx)
    if "nc" not in _NC_CACHE:
        _NC_CACHE["nc"] = Builder().build()
    nc = _NC_CACHE["nc"]
    cst = _consts()
    shared = dict(
        w_mod=f(w_mod), b_mod=f(b_mod), g_pre=f(g_pre), g_post=f(g_post),
        w_ffn_gate=f(w_ffn_gate), w_ffn_up=f(w_ffn_up), w_ffn_down=f(w_ffn_down),
        w_in_ab=f(w_in_ab)[0], w_out_ab=f(w_out_ab)[0], q_gain_a=f(q_gain_a), k_gain_a=f(k_gain_a),
        lam4=np.ascontiguousarray(np.concatenate([f(lam_q1), f(lam_k1), f(lam_q2), f(lam_k2)], axis=0)),
        sub_gain_b=np.ascontiguousarray(f(sub_gain_b).reshape(128, 1)),
        w_in_c=f(w_in_c)[0], w_out_c=f(w_out_c)[0], sink_c=f(sink_c), **cst)
    in_maps = []
    for k in range(8):
        m = dict(shared)
        m["x"] = x[NB * k:NB * (k + 1)]
        m["ctx"] = ctx[NB * k:NB * (k + 1)]
        m["cvec"] = np.ascontiguousarray(np.concatenate([c[NB * k:NB * (k + 1)], c_ctx[None, :]], axis=0))
        in_maps.append(m)
    res = run_bass_kernel_spmd(nc, in_maps, core_ids=list(range(8)))
    return np.concatenate([r["out"] for r in res.results], axis=0)
```

```python
import math
from contextlib import ExitStack
import numpy as np
import ml_dtypes
import concourse.bass as bass
import concourse.mybir as mybir
from concourse.bass_utils import run_bass_kernel_spmd

F32 = mybir.dt.float32
BF16 = mybir.dt.bfloat16
AF = mybir.ActivationFunctionType
ALU = mybir.AluOpType
AX = mybir.AxisListType

D = 1024
L = 2048
CT = 256
NT = 18
DFF = 2816
NJ = 22
EPS = 1e-6
NB = 2
STAGES = 99
SUB = 99
import os as _os
CT1 = int(_os.environ.get('CT1', '1'))
CT2 = int(_os.environ.get('CT2', '1'))
CT3 = int(_os.environ.get('CT3', '1'))
CT4 = int(_os.environ.get('CT4', '1'))
CT5 = int(_os.environ.get('CT5', '1'))
CT6 = int(_os.environ.get('CT6', '1'))
CT7 = int(_os.environ.get('CT7', '1'))


class Res:
    __slots__ = ("name", "w", "r", "excl")

    def __init__(self, name, excl=False):
        self.name = name
        self.w = None
        self.r = []
        self.excl = excl


class _Rec:
    def __getattr__(self, name):
        return lambda *a, **k: (name, a, k)


_REC = _Rec()


class Prog:
    CE = ("pe", "act", "dve", "pool")

    def __init__(self):
        self.q = {e: [] for e in ("pe", "act", "dve", "pool", "sp")}
        self.cnt = {e: 0 for e in self.CE}
        self.waited = {e: {} for e in self.q}
        self.dcnt = {}

    def _deps(self, eng, reads, writes):
        toks = []
        for r in reads:
            if r.w is not None:
                toks.append(r.w)
            if r.excl:
                for t in r.r:
                    if t[0] != eng:
                        toks.append(t)
        for w in writes:
            if w.w is not None:
                toks.append(w.w)
            for t in w.r:
                toks.append(t)
        need = {}
        for sk, v in toks:
            if sk == "pe" and eng == "pe":
                continue
            if self.waited[eng].get(sk, 0) >= v:
                continue
            need[sk] = max(need.get(sk, 0), v)
        for sk, v in need.items():
            self.waited[eng][sk] = v
            self.q[eng].append(("wait", sk, v))

    def op(self, eng, fn, reads=(), writes=()):
        self._deps(eng, reads, writes)
        self.cnt[eng] += 1
        tok = (eng, self.cnt[eng])
        self.q[eng].append(("op", fn(_REC)))
        for r in reads:
            r.r.append(tok)
        for w in writes:
            w.w = tok
            w.r = []
        return tok

    def dma(self, queue, semkey, fn, reads=(), writes=()):
        self._deps(queue, reads, writes)
        self.dcnt[semkey] = self.dcnt.get(semkey, 0) + 16
        tok = (semkey, self.dcnt[semkey])
        self.q[queue].append(("dma", fn(_REC), semkey))
        for r in reads:
            r.r.append(tok)
        for w in writes:
            w.w = tok
            w.r = []
        return tok

    def barrier(self):
        toks = [(e, c) for e, c in self.cnt.items() if c > 0] + [(k, v) for k, v in self.dcnt.items()]
        for eng in self.q:
            for sk, v in toks:
                if sk == eng:
                    continue
                if self.waited[eng].get(sk, 0) >= v:
                    continue
                self.waited[eng][sk] = v
                self.q[eng].append(("wait", sk, v))

    def emit(self, sems, block):
        def replay(key, eng):
            for it in self.q[key]:
                if it[0] == "wait":
                    eng.wait_ge(sems[it[1]], it[2])
                elif it[0] == "op":
                    nm, a, k = it[1]
                    getattr(eng, nm)(*a, **k).then_inc(sems[key], 1)
                else:
                    nm, a, k = it[1]
                    getattr(eng, nm)(*a, **k).then_inc(sems[it[2]], 16)

        @block.tensor
        def _(e):
            replay("pe", e)

        @block.scalar
        def _(e):
            replay("act", e)

        @block.vector
        def _(e):
            replay("dve", e)

        @block.gpsimd
        def _(e):
            replay("pool", e)

        @block.sync
        def _(e):
            replay("sp", e)


class Builder:
    def __init__(self):
        self.nc = bass.Bass("TRN2", target_bir_lowering=False)
        self.P = Prog()
        self.es = ExitStack()
        self.dsems = []

    def din(self, name, shape, dt=F32):
        return self.nc.dram_tensor(name, list(shape), dt, kind="ExternalInput").ap()

    def sb(self, name, shape, dt):
        return self.es.enter_context(self.nc.sbuf_tensor("s_" + name, list(shape), dt))

    def ps(self, name, shape, dt):
        return self.es.enter_context(self.nc.psum_tensor("p_" + name, list(shape), dt))

    def dsem(self, name):
        self.dsems.append(name)
        return name

    def carve(self, nbytes):
        off = self.ov_off
        assert off % 4 == 0
        self.ov_off += (nbytes + 3) // 4 * 4
        assert self.ov_off <= self.ov_bytes, (self.ov_off, self.ov_bytes)
        return off

    def ovv(self, shape, dt):
        esz = 2 if dt == BF16 else 4
        n = 1
        for s in shape[1:]:
            n *= s
        off = self.carve(n * esz)
        v = self.OV[0:shape[0], off // 2: off // 2 + n * esz // 2]
        if dt != BF16:
            v = v.bitcast(dt)
        if len(shape) == 3:
            v = v.rearrange("p (a b) -> p a b", a=shape[1])
        elif len(shape) == 4:
            v = v.rearrange("p (a b c) -> p a b c", a=shape[1], b=shape[2])
        return v

    @staticmethod
    def bcast(row_ap, n):
        ln = row_ap.shape[-1]
        return bass.AP(row_ap.tensor, row_ap.offset, [[0, n], [1, ln]])

    def build(self):
        nc, P = self.nc, self.P
        self.x_d = self.din("x", [NB, L, D])
        self.ctx_d = self.din("ctx", [NB, CT, D])
        self.cvec_d = self.din("cvec", [3, D])
        self.wmod_d = self.din("w_mod", [2, D, 9 * D])
        self.bmod_d = self.din("b_mod", [2, 9 * D])
        self.gpre_d = self.din("g_pre", [2, 3, D])
        self.gpost_d = self.din("g_post", [2, 3, D])
        self.wg_d = self.din("w_ffn_gate", [2, 2, D, DFF])
        self.wu_d = self.din("w_ffn_up", [2, 2, D, DFF])
        self.wd_d = self.din("w_ffn_down", [2, 2, DFF, D])
        self.winab_d = self.din("w_in_ab", [D, 2304])
        self.woutab_d = self.din("w_out_ab", [D, D])
        self.qg_d = self.din("q_gain_a", [1, 64])
        self.kg_d = self.din("k_gain_a", [1, 64])
        self.lam_d = self.din("lam4", [4, 64])
        self.sg_d = self.din("sub_gain_b", [128, 1])
        self.winc_d = self.din("w_in_c", [D, 1536])
        self.woutc_d = self.din("w_out_c", [D, D])
        self.sink_d = self.din("sink_c", [1, 16])
        self.identb_d = self.din("ident_bf", [128, 128], BF16)
        self.identf_d = self.din("ident_f", [128, 128], F32)
        self.cs_d = self.din("cs", [128, NT, 64])
        self.mask_d = self.din("maskb", [128, 2, 512], BF16)
        self.out_d = nc.dram_tensor("out", [NB, L, D], F32, kind="ExternalOutput").ap()
        self.modd = nc.dram_tensor("modd", [2, 3, 9 * D], F32).ap()

        self.X = self.sb("X", [128, NT, D], F32)
        self.rX = [Res(f"x{t}") for t in range(NT)]
        self.identb = self.sb("identb", [128, 128], BF16)
        self.onesb = self.sb("onesb", [128, 128], BF16)
        self.cs = self.sb("cs", [128, NT, 64], F32)
        self.rConst = Res("const")
        self.gm = self.sb("gm", [128, D], F32)
        self.sh = self.sb("sh", [128, D], F32)
        self.gg = self.sb("gg", [128, D], F32)
        self.rgm, self.rsh, self.rgg = Res("gm"), Res("sh"), Res("gg")
        self.st = self.sb("st", [128, 64], F32)
        self.rst = [Res(f"st{k}") for k in range(8)]
        self.lamt = self.sb("lamt", [128, 8], F32)
        self.rlam = Res("lam")
        self.sgc = self.sb("sgc", [128, 1], F32)
        self.qgt = self.sb("qgt", [128, 64], F32)
        self.kgt = self.sb("kgt", [128, 64], F32)
        self.skt = self.sb("skt", [128, 16], F32)
        self.ov_bytes = 116096 + 4096 + 512
        self.OV = self.sb("OV", [128, self.ov_bytes // 2], BF16)
        self.ov_off = 0

        self.PY = [self.ps("py0", [128, 1024], F32), self.ps("py1", [128, 1024], F32)]
        self.rPY = [[Res("py0a", True), Res("py0b", True)], [Res("py1a", True), Res("py1b", True)]]
        self.PT = self.ps("pt", [128, 8, 128], BF16)
        self.rPT = Res("pt", True)
        self.PA = self.ps("pa", [128, 512], F32)
        self.PB = self.ps("pb", [128, 512], F32)
        self.PC = self.ps("pc", [128, 512], F32)
        self.rPA, self.rPB, self.rPC = Res("pa", True), Res("pb", True), Res("pc", True)

        for k in ("ld0", "ld1", "ld2", "ld3", "ld4", "ld5", "st2", "st3", "bc0", "bc1", "bc2", "bc3", "w0", "w1", "w2", "w3", "wd", "wq", "wo",
                  "st0", "st1", "ms0", "ms1"):
            self.dsem(k)

        self.consts()
        self.stage_mod()
        P.barrier()
        for b in range(NB):
            self.load_x(b)
            if STAGES >= 1:
                self.ffn(0, 0, 0, b, list(range(NT)))
            P.barrier()
            if STAGES >= 2:
                self.mixer_ab(b)
                P.barrier()
            if STAGES >= 3:
                self.ffn(0, 1, 2, b, list(range(NT)))
                P.barrier()
            if STAGES >= 4:
                self.ffn(1, 0, 0, b, list(range(NT)))
                P.barrier()
            if STAGES >= 5:
                self.mixer_c(b)
                P.barrier()
            if STAGES >= 6:
                self.ffn(1, 1, 2, b, list(range(2, NT)))
                P.barrier()
            self.store_x(b)
        P.barrier()

        sems = {}
        for k in ("pe", "act", "dve", "pool"):
            sems[k] = self.es.enter_context(nc.semaphore(k))
        for k in self.dsems:
            sems[k] = self.es.enter_context(nc.semaphore(k))
        block = self.es.enter_context(nc.Block())
        P.emit(sems, block)
        self.es.close()
        return nc

    def consts(self):
        P = self.P
        rc = self.rConst
        P.dma("sp", "ld5", lambda e: e.dma_start(out=self.identb[:], in_=self.identb_d[:, :]), writes=[rc])
        P.dma("sp", "ld5", lambda e: e.dma_start(out=self.cs[:], in_=self.cs_d[:, :, :]), writes=[rc])
        P.dma("sp", "ld5", lambda e: e.dma_start(out=self.sgc[:], in_=self.sg_d[:, :]), writes=[rc])
        P.dma("sp", "ld5", lambda e: e.dma_start(out=self.qgt[:], in_=self.bcast(self.qg_d[0:1, :], 128)), writes=[rc])
        P.dma("sp", "ld5", lambda e: e.dma_start(out=self.kgt[:], in_=self.bcast(self.kg_d[0:1, :], 128)), writes=[rc])
        P.dma("sp", "ld5", lambda e: e.dma_start(out=self.skt[:], in_=self.bcast(self.sink_d[0:1, :], 128)), writes=[rc])
        P.op("dve", lambda e: e.memset(self.onesb[:], 1.0), writes=[rc])

    def stage_mod(self):
        P = self.P
        self.ov_off = 0
        NR = 6
        identf = self.ovv([128, 128], F32)
        cv = self.ovv([128, D], F32)
        sc = self.ovv([128, D], F32)
        scT = self.ovv([128, 8, 4], F32)
        wm = [self.ovv([128, 8, 512], F32) for _ in range(NR)]
        bm = [self.ovv([128, 512], F32) for _ in range(2)]
        ms = [self.ovv([128, 512], F32) for _ in range(2)]
        rcv, rsc, rscT, rid = Res("cv"), Res("sc"), Res("scT"), Res("identf")
        rwm = [Res(f"wm{k}") for k in range(NR)]
        rbm = [Res("bm0"), Res("bm1")]
        rms = [Res("ms0"), Res("ms1")]
        for k in range(NR):
            self.dsem(f"wm{k}")
        ptf = self.PT[:].rearrange("p a b -> p (a b)").bitcast(F32).rearrange("p (a b) -> p a b", a=8)
        P.dma("sp", "ld1", lambda e: e.dma_start(out=cv[0:3, :], in_=self.cvec_d[:, :]), writes=[rcv])
        P.dma("sp", "ld3", lambda e: e.dma_start(out=identf[:, :], in_=self.identf_d[:, :]), writes=[rid])
        slabs = [(i, s_) for i in range(2) for s_ in range(18)]

        def issue(n):
            i, s_ = slabs[n]
            k = n % NR
            wsrc = self.wmod_d[i, :, s_ * 512:(s_ + 1) * 512].rearrange("(kc p) f -> p kc f", p=128)
            P.dma("sp", f"wm{k}", lambda e: e.dma_start(out=wm[k][:], in_=wsrc), writes=[rwm[k]])
        for n in range(NR - 1):
            issue(n)
        P.op("act", lambda e: e.activation(out=sc[0:3, :], in_=cv[0:3, :], func=AF.Silu), reads=[rcv], writes=[rsc])
        for kc in range(8):
            P.op("pe", lambda e, kc=kc: e.transpose(out=ptf[:, kc, 0:3], in_=sc[0:3, kc * 128:(kc + 1) * 128],
                                                   identity=identf[0:3, 0:3]),
                 reads=[rsc, rid], writes=[self.rPT])
        P.op("dve", lambda e: e.tensor_copy(out=scT[:, :, 0:3], in_=ptf[:, :, 0:3]), reads=[self.rPT], writes=[rscT])
        for n, (i, s_) in enumerate(slabs):
            if n + NR - 1 < len(slabs):
                issue(n + NR - 1)
            k = n % 2
            kw = n % NR
            bsrc = self.bcast(self.bmod_d[i:i + 1, s_ * 512:(s_ + 1) * 512], 3)
            P.dma("sp", f"bc{k}", lambda e, k=k, bsrc=bsrc: e.dma_start(out=bm[k][0:3, :], in_=bsrc), writes=[rbm[k]])
            mp = self.PA if k == 0 else self.PB
            rmp = self.rPA if k == 0 else self.rPB
            for kc in range(8):
                P.op("pe", lambda e, kc=kc: e.matmul(mp[0:3, :], lhsT=scT[:, kc, 0:3], rhs=wm[kw][:, kc, :], start=(kc == 0), stop=(kc == 7)),
                     reads=[rscT, rwm[kw]], writes=[rmp])
            P.op("dve", lambda e: e.tensor_tensor(out=ms[k][0:3, :], in0=mp[0:3, :], in1=bm[k][0:3, :], op=ALU.add),
                 reads=[rmp, rbm[k]], writes=[rms[k]])
            dst = self.modd[i, :, s_ * 512:(s_ + 1) * 512]
            P.dma("sp", f"ms{k}", lambda e: e.dma_start(out=dst, in_=ms[k][0:3, :]), reads=[rms[k]])
        lt = self.ovv([128, 4, 64], F32)
        rlt = Res("lt")
        for r in range(4):
            P.dma("sp", "ld2", lambda e, r=r: e.dma_start(out=lt[:, r, :], in_=self.bcast(self.lam_d[r:r + 1, :], 128)), writes=[rlt])
        P.op("dve", lambda e: e.tensor_tensor(out=lt[:, 0, :], in0=lt[:, 0, :], in1=lt[:, 1, :], op=ALU.mult), reads=[rlt], writes=[rlt])
        P.op("dve", lambda e: e.tensor_tensor(out=lt[:, 2, :], in0=lt[:, 2, :], in1=lt[:, 3, :], op=ALU.mult), reads=[rlt], writes=[rlt])
        P.op("dve", lambda e: e.tensor_reduce(out=self.lamt[:, 0:1], in_=lt[:, 0, :], axis=AX.X, op=ALU.add), reads=[rlt], writes=[self.rlam])
        P.op("dve", lambda e: e.tensor_reduce(out=self.lamt[:, 1:2], in_=lt[:, 2, :], axis=AX.X, op=ALU.add), reads=[rlt], writes=[self.rlam])
        P.op("act", lambda e: e.activation(out=self.lamt[:, 2:4], in_=self.lamt[:, 0:2], func=AF.Exp), reads=[self.rlam], writes=[self.rlam])
        lam_init = 0.8 - 0.6 * math.exp(-0.3 * 0)
        P.op("dve", lambda e: e.scalar_tensor_tensor(out=self.lamt[:, 4:5], in0=self.lamt[:, 3:4], scalar=-lam_init, in1=self.lamt[:, 2:3],
                                                      op0=ALU.add, op1=ALU.subtract), reads=[self.rlam], writes=[self.rlam])
        P.op("dve", lambda e: e.tensor_scalar(out=self.sgc[:], in0=self.sgc[:], scalar1=1.0 - lam_init, scalar2=None, op0=ALU.mult),
             reads=[self.rConst], writes=[self.rConst])
        P.op("act", lambda e: e.activation(out=self.skt[:], in_=self.skt[:], func=AF.Exp), reads=[self.rConst], writes=[self.rConst])

    def load_x(self, b):
        P = self.P
        P.dma("sp", "ld4", lambda e: e.dma_start(out=self.X[:, 0:2, :], in_=self.ctx_d[b].rearrange("(t p) d -> p t d", p=128)),
              writes=self.rX[0:2])
        for q in range(4):
            t0 = 2 + 4 * q
            P.dma("sp", f"ld{q}", lambda e, q=q, t0=t0: e.dma_start(
                out=self.X[:, t0:t0 + 4, :], in_=self.x_d[b, q * 512:(q + 1) * 512, :].rearrange("(t p) d -> p t d", p=128)),
                writes=self.rX[t0:t0 + 4])

    def store_x(self, b):
        P = self.P
        for q in range(4):
            t0 = 2 + 4 * q
            P.dma("sp", f"st{q}", lambda e, q=q, t0=t0: e.dma_start(
                out=self.out_d[b, q * 512:(q + 1) * 512, :].rearrange("(t p) d -> p t d", p=128), in_=self.X[:, t0:t0 + 4, :]),
                reads=self.rX[t0:t0 + 4])

    def bc_row(self, dst, rdst, src_row, sem):
        self.P.dma("sp", sem, lambda e: e.dma_start(out=dst[:, :], in_=self.bcast(src_row, 128)), writes=[rdst])

    def prep_pre(self, i, slot, r):
        P = self.P
        base = slot * 3 * D
        self.bc_row(self.gm, self.rgm, self.modd[i, r:r + 1, base + D: base + 2 * D], "bc0")
        self.bc_row(self.m_tmpf, self.m_rtmpf, self.gpre_d[i, slot:slot + 1, :], "bc1")
        self.bc_row(self.sh, self.rsh, self.modd[i, r:r + 1, base: base + D], "bc2")
        P.op("dve", lambda e: e.scalar_tensor_tensor(out=self.gm[:], in0=self.gm[:], scalar=1.0, in1=self.m_tmpf[:], op0=ALU.add, op1=ALU.mult),
             reads=[self.rgm, self.m_rtmpf], writes=[self.rgm])

    def prep_post(self, i, slot, r, res_w):
        P = self.P
        base = slot * 3 * D
        self.bc_row(self.gg, self.rgg, self.modd[i, r:r + 1, base + 2 * D: base + 3 * D], "bc3")
        self.bc_row(self.m_tmpf, self.m_rtmpf, self.gpost_d[i, slot:slot + 1, :], "bc1")
        P.op("dve", lambda e: e.scalar_tensor_tensor(out=self.gg[:], in0=self.gg[:], scalar=float(res_w), in1=self.m_tmpf[:], op0=ALU.mult, op1=ALU.mult),
             reads=[self.rgg, self.m_rtmpf], writes=[self.rgg])

    def rstd_cols(self, srcs, col0, rk, inv_n):
        P = self.P
        n = len(srcs)
        for j, (ap, rr, junk, rjunk) in enumerate(srcs):
            P.op("act", lambda e, ap=ap, j=j, junk=junk: e.activation(out=junk, in_=ap, func=AF.Square, accum_out=self.st[:, col0 + j:col0 + j + 1]),
                 reads=rr, writes=[rjunk, rk])
        P.op("dve", lambda e: e.tensor_scalar(out=self.st[:, col0:col0 + n], in0=self.st[:, col0:col0 + n], scalar1=float(inv_n), scalar2=EPS,
                                               op0=ALU.mult, op1=ALU.add), reads=[rk], writes=[rk])
        P.op("act", lambda e: e.activation(out=self.st[:, col0:col0 + n], in_=self.st[:, col0:col0 + n], func=AF.Sqrt), reads=[rk], writes=[rk])
        P.op("dve", lambda e: e.reciprocal(out=self.st[:, col0:col0 + n], in_=self.st[:, col0:col0 + n]), reads=[rk], writes=[rk])

    def prenorm_tile(self, t, rcol, rk, tmpf, rtmpf, hb, rhb, dstT, rdstT, part="all"):
        P = self.P
        if part in ("all", "dve"):
            P.op("dve", lambda e: e.scalar_tensor_tensor(out=tmpf, in0=self.X[:, t, :], scalar=rcol, in1=self.gm[:], op0=ALU.mult, op1=ALU.mult),
                 reads=[self.rX[t], rk, self.rgm], writes=[rtmpf])
            P.op("dve", lambda e: e.tensor_tensor(out=hb, in0=tmpf, in1=self.sh[:], op=ALU.add), reads=[rtmpf, self.rsh], writes=[rhb])
        if part == "dve":
            return
        for kc in range(8):
            P.op("pe", lambda e, kc=kc: e.transpose(out=self.PT[:, kc, :], in_=hb[:, kc * 128:(kc + 1) * 128], identity=self.identb[:]),
                 reads=[rhb, self.rConst], writes=[self.rPT])
        P.op("act", lambda e: e.activation(out=dstT, in_=self.PT[:], func=AF.Copy), reads=[self.rPT], writes=[rdstT])

    def postnorm_tile(self, t, y_ap, ry, tmpf, rtmpf, col, rk):
        P = self.P
        junk = tmpf.bitcast(BF16)[:, 0:1024]
        self.rstd_cols([(y_ap, ry, junk, rtmpf)], col, rk, 1.0 / D)
        P.op("dve", lambda e: e.scalar_tensor_tensor(out=tmpf, in0=y_ap, scalar=self.st[:, col:col + 1], in1=self.gg[:], op0=ALU.mult, op1=ALU.mult),
             reads=ry + [rk, self.rgg], writes=[rtmpf])
        P.op("dve", lambda e: e.tensor_tensor(out=self.X[:, t, :], in0=self.X[:, t, :], in1=tmpf, op=ALU.add), reads=[rtmpf, self.rX[t]], writes=[self.rX[t]])

    def ffn(self, i, widx, slot, b, tiles):
        P = self.P
        self.ov_off = 0
        TP = 6
        hT = self.ovv([128, 8, TP * 128], BF16)
        aT = self.ovv([128, NJ, TP * 128], BF16)
        wdt = self.ovv([128, NJ, D], BF16)
        ring = [(self.ovv([128, 8, 256], BF16), self.ovv([128, 8, 256], BF16)) for _ in range(2)]
        tmpf = [self.ovv([128, D], F32), self.ovv([128, D], F32)]
        hb = [self.ovv([128, D], BF16)] * 2
        sgb = [self.ovv([128, 512], F32)] * 2
        rhT = [Res(f"hT{j}") for j in range(TP)]
        raT = [Res(f"aT{j}") for j in range(NJ)]
        rwd = Res("wd")
        rring = [Res("ring0"), Res("ring1")]
        rtmpf = [Res("tmpf0"), Res("tmpf1")]
        rhb = [Res("hb0")] * 2
        rsgb = [Res("sgb0")] * 2
        self.m_tmpf, self.m_rtmpf = tmpf[0], rtmpf[0]
        for q in range(4):
            j0, j1 = q * 6, min(NJ, q * 6 + 6)
            src = self.wd_d[i, widx, j0 * 128:j1 * 128, :].rearrange("(j p) d -> p j d", p=128)
            P.dma("pool", "wd", lambda e, j0=j0, j1=j1, src=src: e.dma_start(out=wdt[:, j0:j1, :], in_=src), writes=[rwd])
        passes = [tiles[k:k + TP] for k in range(0, len(tiles), TP)]
        self._cur_row = None
        slab_ctr = [0]

        def issue_slab(sidx, s_):
            k = sidx % 2
            gsrc = self.wg_d[i, widx, :, s_ * 256:(s_ + 1) * 256].rearrange("(kc p) f -> p kc f", p=128)
            usrc = self.wu_d[i, widx, :, s_ * 256:(s_ + 1) * 256].rearrange("(kc p) f -> p kc f", p=128)
            P.dma("pool", f"w{k}", lambda e: e.dma_start(out=ring[k][0][:], in_=gsrc), writes=[rring[k]])
            P.dma("pool", f"w{k}", lambda e: e.dma_start(out=ring[k][1][:], in_=usrc), writes=[rring[k]])

        def p1_stats(ptiles, pi):
            col0 = 0 if pi % 2 == 0 else 16
            rk = self.rst[0] if pi % 2 == 0 else self.rst[6]
            srcs = [(self.X[:, t, :], [self.rX[t]], hb[0], rhb[0]) for t in ptiles]
            self.rstd_cols(srcs, col0, rk, 1.0 / D)
            return col0, rk

        def p1_tile(j, t, col0, rk):
            row = 2 if t < 2 else b
            if row != self._cur_row:
                self.prep_pre(i, slot, row)
                self._cur_row = row
            self.prenorm_tile(t, self.st[:, col0 + j:col0 + j + 1], rk, tmpf[0], rtmpf[0], hb[0], rhb[0],
                              hT[:, :, j * 128:(j + 1) * 128], rhT[j])

        c0_, rk_ = p1_stats(passes[0], 0)
        for j, t in enumerate(passes[0]):
            p1_tile(j, t, c0_, rk_)
        for pi, ptiles in enumerate(passes):
            npt = len(ptiles)
            T = npt * 128
            nblk = 2 if T > 512 else 1
            bs = T // nblk
            issue_slab(slab_ctr[0], 0)
            for s_ in range(11):
                if s_ + 1 < 11:
                    issue_slab(slab_ctr[0] + 1, s_ + 1)
                k = slab_ctr[0] % 2
                for jj in range(2):
                    j = s_ * 2 + jj
                    for nb in range(nblk):
                        pk = (j * nblk + nb) % 2
                        g_ps = self.PY[pk][:, 0:bs]
                        u_ps = self.PY[pk][:, 512:512 + bs]
                        tl = list(range(nb * bs // 128, (nb + 1) * bs // 128))
                        rh = [rhT[q] for q in tl]
                        for kc in range(8):
                            P.op("pe", lambda e: e.matmul(g_ps, lhsT=ring[k][0][:, kc, jj * 128:(jj + 1) * 128], rhs=hT[:, kc, nb * bs:(nb + 1) * bs],
                                                          start=(kc == 0), stop=(kc == 7)), reads=rh + [rring[k]], writes=[self.rPY[pk][0]])
                        for kc in range(8):
                            P.op("pe", lambda e: e.matmul(u_ps, lhsT=ring[k][1][:, kc, jj * 128:(jj + 1) * 128], rhs=hT[:, kc, nb * bs:(nb + 1) * bs],
                                                          start=(kc == 0), stop=(kc == 7)), reads=rh + [rring[k]], writes=[self.rPY[pk][1]])
                        P.op("act", lambda e: e.activation(out=sgb[pk][:, 0:bs], in_=g_ps, func=AF.Silu), reads=[self.rPY[pk][0]], writes=[rsgb[pk]])
                        P.op("dve", lambda e: e.tensor_tensor(out=aT[:, j, nb * bs:(nb + 1) * bs], in0=sgb[pk][:, 0:bs], in1=u_ps, op=ALU.mult),
                             reads=[rsgb[pk], self.rPY[pk][1]], writes=[raT[j]])
                slab_ctr[0] += 1
            nxt = passes[pi + 1] if pi + 1 < len(passes) else None
            if nxt is not None:
                c0n, rkn = p1_stats(nxt, pi + 1)
            for j, t in enumerate(ptiles):
                row = 2 if t < 2 else b
                pk = j % 2
                y = self.PY[pk]
                for half in range(2):
                    for jf in range(NJ):
                        P.op("pe", lambda e: e.matmul(y[:, half * 512:(half + 1) * 512], lhsT=aT[:, jf, j * 128:(j + 1) * 128],
                                                      rhs=wdt[:, jf, half * 512:(half + 1) * 512], start=(jf == 0), stop=(jf == NJ - 1)),
                             reads=[raT[jf], rwd], writes=[self.rPY[pk][half]])
                if nxt is not None and j < len(nxt):
                    p1_tile(j, nxt[j], c0n, rkn)
                self._post_row(i, slot, row, 0.5)
                self.postnorm_tile(t, y[:, :], [self.rPY[pk][0], self.rPY[pk][1]], tmpf[1], rtmpf[1], 8 + pk, self.rst[1 + pk])
        self._post_state = None

    _post_state = None

    def _post_row(self, i, slot, row, res_w):
        key = (i, slot, row)
        if self._post_state != key:
            self.prep_post(i, slot, row, res_w)
            self._post_state = key

    def rsqrt_inplace(self, ap, res, inv_n):
        P = self.P
        P.op("dve", lambda e: e.tensor_scalar(out=ap, in0=ap, scalar1=float(inv_n), scalar2=EPS, op0=ALU.mult, op1=ALU.add), reads=[res], writes=[res])
        P.op("act", lambda e: e.activation(out=ap, in_=ap, func=AF.Sqrt), reads=[res], writes=[res])
        P.op("dve", lambda e: e.reciprocal(out=ap, in_=ap), reads=[res], writes=[res])

    def rope(self, src3, rsrc, t, nh, dst3, rdst, ra, rb, rra, rrb):
        P = self.P
        cos = self.cs[:, t, 0:32].unsqueeze(1).to_broadcast([128, nh, 32])
        sin = self.cs[:, t, 32:64].unsqueeze(1).to_broadcast([128, nh, 32])
        x1, x2 = src3[:, :, 0:32], src3[:, :, 32:64]
        a, b_ = ra[:, 0:nh, :], rb[:, 0:nh, :]
        P.op("dve", lambda e: e.tensor_tensor(out=a, in0=x1, in1=cos, op=ALU.mult), reads=rsrc + [self.rConst], writes=[rra])
        P.op("dve", lambda e: e.tensor_tensor(out=b_, in0=x2, in1=sin, op=ALU.mult), reads=rsrc + [self.rConst], writes=[rrb])
        P.op("pool", lambda e: e.tensor_tensor(out=dst3[:, :, 0:32], in0=a, in1=b_, op=ALU.subtract), reads=[rra, rrb], writes=[rdst])
        P.op("dve", lambda e: e.tensor_tensor(out=a, in0=x2, in1=cos, op=ALU.mult), reads=rsrc + [self.rConst], writes=[rra])
        P.op("dve", lambda e: e.tensor_tensor(out=b_, in0=x1, in1=sin, op=ALU.mult), reads=rsrc + [self.rConst], writes=[rrb])
        P.op("pool", lambda e: e.tensor_tensor(out=dst3[:, :, 32:64], in0=a, in1=b_, op=ALU.add), reads=[rra, rrb], writes=[rdst])

    def head_norm(self, ps2, rps, nh, gain_t, sq, rsq, col0, rk):
        P = self.P
        v3 = ps2.rearrange("p (h d) -> p h d", h=nh)
        sq2 = sq[:, 0:nh * 64]
        P.op("act", lambda e: e.activation(out=sq2, in_=ps2, func=AF.Square), reads=rps, writes=[rsq])
        P.op("dve", lambda e: e.tensor_reduce(out=self.st[:, col0:col0 + nh], in_=sq2.rearrange("p (h d) -> p h d", h=nh), axis=AX.X, op=ALU.add),
             reads=[rsq], writes=[rk])
        self.rsqrt_inplace(self.st[:, col0:col0 + nh], rk, 1.0 / 64)
        rbc = self.st[:, col0:col0 + nh].unsqueeze(2).to_broadcast([128, nh, 64])
        gbc = gain_t[:, :].unsqueeze(1).to_broadcast([128, nh, 64])
        P.op("dve", lambda e: e.tensor_tensor(out=v3, in0=v3, in1=rbc, op=ALU.mult), reads=rps + [rk], writes=rps)
        P.op("dve", lambda e: e.tensor_tensor(out=v3, in0=v3, in1=gbc, op=ALU.mult), reads=rps + [self.rConst], writes=rps)

    def attend(self, jobs, mode="ab", bg=None, bg_every=4):
        P = self.P
        steps = []
        for ji, jb in enumerate(jobs):
            for ki, kt in enumerate(jb["kts"]):
                steps.append((ji, ki, kt))
        ptf = self.PT[:].rearrange("p a b -> p (a b)").bitcast(F32)
        if mode == "ab":
            Sb = [(self.PA, self.rPA), (self.PB, self.rPB), (ptf, self.rPT)]
            Ob = [(self.PC, self.rPC), (self.PY[0][:, 0:512], self.rPY[0][0])]
            Rb = [(self.PY[1][:, 0:512], self.rPY[1][0]), (self.PY[0][:, 512:1024], self.rPY[0][1])]
        else:
            Sb = [(self.PA, self.rPA), (self.PB, self.rPB)]
            Ob = [(self.PC, self.rPC), (self.PY[1][:, 0:512], self.rPY[1][0])]
            Rb = [(None, None), (None, None)]
        NS = len(Sb)
        LA = NS - 1

        def emit_S(si):
            ji, ki, kt = steps[si]
            jb = jobs[ji]
            S, rS = Sb[si % NS]
            nq = jb["nq"]
            mk = jb["mask"](kt) if jb.get("mask") else None
            P.op("pe", lambda e: e.matmul(S[:, 0:nq], lhsT=jb["kT"](kt), rhs=jb["q"], start=True, stop=(mk is None)),
                 reads=jb["rk"] + jb["rq"], writes=[rS])
            if mk is not None:
                P.op("pe", lambda e: e.matmul(S[:, 0:nq], lhsT=self.identb[:], rhs=mk, start=False, stop=True),
                     reads=[self.rConst], writes=[rS])
            Pt, rPt = self.Pring[si % NS]
            P.op("act", lambda e: e.activation(out=Pt[:, 0:nq], in_=S[:, 0:nq], func=AF.Exp, scale=0.125), reads=[rS], writes=[rPt])

        for si in range(min(LA, len(steps))):
            emit_S(si)
        bg = list(bg) if bg else []
        for si, (ji, ki, kt) in enumerate(steps):
            if bg and si % bg_every == 2:
                bg.pop(0)()
            if si + LA < len(steps):
                emit_S(si + LA)
            jb = jobs[ji]
            nq = jb["nq"]
            O, rO = Ob[ji % 2]
            Rs, rR = Rb[ji % 2]
            Pt, rPt = self.Pring[si % NS]
            first, last = ki == 0, ki == len(jb["kts"]) - 1
            P.op("pe", lambda e: e.matmul(O[:, 0:nq], lhsT=jb["v"](kt), rhs=Pt[:, 0:nq], start=first, stop=last),
                 reads=[rPt] + jb["rv"], writes=[rO])
            if jb["rs_sep"]:
                P.op("pe", lambda e: e.matmul(Rs[:, 0:nq], lhsT=self.onesb[:, :], rhs=Pt[:, 0:nq], start=first, stop=last),
                     reads=[rPt, self.rConst], writes=[rR])
            if last:
                jb["fin"](O, rO, Rs, rR)
        while bg:
            bg.pop(0)()

    @staticmethod
    def skewed(n, stages):
        ns = len(stages)
        out = []
        for s_ in range(n + ns - 1):
            for k in reversed(range(ns)):
                j = s_ - k
                if 0 <= j < n:
                    out.append(lambda k=k, j=j: stages[k](j))
        return out

    def outproj_mm(self, j, OT, rOT, wo, rwo, nchunk, yk=1):
        P = self.P
        y = self.PY[yk]
        for half in range(2):
            for c in range(nchunk):
                P.op("pe", lambda e: e.matmul(y[:, half * 512:(half + 1) * 512], lhsT=OT[:, c, j * 128:(j + 1) * 128],
                                              rhs=wo[:, c, half * 512:(half + 1) * 512], start=(c == 0), stop=(c == nchunk - 1)),
                     reads=[rOT, rwo], writes=[self.rPY[yk][half]])

    def outproj_post(self, t, i, row, col, yk=1, tmp=None, rtmp=None):
        y = self.PY[yk]
        self._post_row(i, 1, row, 1.0)
        self.postnorm_tile(t, y[:, :], [self.rPY[yk][0], self.rPY[yk][1]], tmp if tmp is not None else self.m_tmpf,
                           rtmp if rtmp is not None else self.m_rtmpf, col, self.rst[3])

    def outproj_tile(self, t, j, OT, rOT, wo, rwo, nchunk, i, row, col):
        self.outproj_mm(j, OT, rOT, wo, rwo, nchunk)
        self.outproj_post(t, i, row, col)

    def mixer_ab(self, b):
        P = self.P
        self.ov_off = 0
        self._post_state = None
        T = NT * 128
        KTA = [self.ovv([128, T], BF16), self.ovv([128, T], BF16)]
        KTB = self.ovv([128, 4, T], BF16)
        VA = self.ovv([128, NT, 192], BF16)
        VB = self.ovv([128, NT, 512], BF16)
        wq = self.ovv([128, 8, 1024], BF16)
        wx = self.ovv([128, 8, 256], BF16)
        wo = self.ovv([128, 8, D], BF16)
        tmpf = self.ovv([128, D], F32)
        hb = self.ovv([128, D], BF16)
        hTt = self.ovv([128, 8, 128], BF16)
        ra = self.ovv([128, 16, 32], F32)
        rb = self.ovv([128, 16, 32], F32)
        QT = self.ovv([128, 8, 512], BF16)
        OT = self.ovv([128, 8, 512], BF16)
        rK, rV, rwq, rwx, rwo = Res("K"), Res("V"), Res("wq"), Res("wx"), Res("wo")
        rtmpf, rhb, rhTt, rra, rrb, rQT, rOT = (Res(n) for n in ("tmpf", "hb", "hTt", "ra", "rb", "QT", "OT"))
        qrot, rqrot = hb, rhb
        QTBhi = wx.rearrange("p a b -> p (a b)").rearrange("p (h t) -> p h t", h=4)
        self.Pring = [(self.ovv([128, 512], BF16), Res("p0")), (self.ovv([128, 512], BF16), Res("p1")),
                      (hTt.rearrange("p a b -> p (a b)")[:, 0:512], rhTt)]
        self.m_tmpf, self.m_rtmpf = tmpf, rtmpf
        rinv = tmpf[:, 0:512]
        o1 = tmpf[:, 512:1024]
        sqf = ra.rearrange("p a b -> p (a b)")
        dsq = rb.rearrange("p a b -> p (a b)").bitcast(BF16)[:, 0:512]
        rsd = ra.rearrange("p a b -> p (a b)")
        W = self.winab_d
        kc_view = lambda c0, c1: W[:, c0:c1].rearrange("(kc p) f -> p kc f", p=128)
        for (d0, s0, s1) in ((0, 512, 640), (128, 1280, 1792), (640, 640, 768), (768, 1792, 2048)):
            P.dma("pool", "wq", lambda e, d0=d0, s0=s0, s1=s1: e.dma_start(out=wq[:, :, d0:d0 + (s1 - s0)], in_=kc_view(s0, s1)), writes=[rwq])
        P.dma("pool", "w2", lambda e: e.dma_start(out=wx[:, :, :], in_=kc_view(2048, 2304)), writes=[rwx])
        P.dma("pool", "wo", lambda e: e.dma_start(out=wo[:], in_=self.woutab_d.rearrange("(c p) d -> p c d", p=128)), writes=[rwo])
        P.op("pool", lambda e: e.memset(KTA[0][64:128, :], 0.0), writes=[rK])
        P.op("pool", lambda e: e.memset(KTA[1][0:64, :], 0.0), writes=[rK])
        P.op("pool", lambda e: e.memset(VA[:, :, 64:128], 1.0), writes=[rV])
        P.op("pool", lambda e: e.memset(QT[64:128, 4:8, :], 0.0), writes=[rQT])
        rk = self.rst[0]
        self.rstd_cols([(self.X[:, t, :], [self.rX[t]], hb[:], rhb) for t in range(NT)], 0, rk, 1.0 / D)
        self._cur_row = None

        def stA(t, part="all"):
            row = 2 if t < 2 else b
            if part != "pe" and row != self._cur_row:
                self.prep_pre(0, 1, row)
                self._cur_row = row
            self.prenorm_tile(t, self.st[:, t:t + 1], rk, tmpf[:], rtmpf, hb[:], rhb, hTt[:], rhTt, part=part)

        def pipe4(n, a_dve, a_pe, bst, cd):
            out = []
            for s_ in range(n + 3):
                for fn, j in ((bst, s_ - 2), (a_pe, s_ - 1), (cd, s_ - 3), (a_dve, s_)):
                    if 0 <= j < n:
                        out.append(lambda fn=fn, j=j: fn(j))
            return out

        kvsets = [((self.PY[0][:, 0:512], self.rPY[0][0]), (self.PY[0][:, 512:1024], self.rPY[0][1]), (self.PY[1][:, 0:256], self.rPY[1][0])),
                  ((self.PA[:, :], self.rPA), (self.PB[:, :], self.rPB), (self.PC[:, 0:256], self.rPC))]

        def s1B(t):
            ks = kvsets[t % 2]
            for (ps_ap, rps), (wsrc, rw, c0, c1) in zip(ks, ((wq, rwq, 0, 512), (wq, rwq, 512, 1024), (wx, rwx, 0, 256))):
                for kc in range(8):
                    P.op("pe", lambda e: e.matmul(ps_ap, lhsT=hTt[:, kc, :], rhs=wsrc[:, kc, c0:c1], start=(kc == 0), stop=(kc == 7)),
                         reads=[rhTt, rw], writes=[rps])

        def s1C(t):
            (p0, r0), (p1, r1), (p2, r2) = kvsets[t % 2]
            self.head_norm(p0[:, 0:128], [r0], 2, self.kgt, sqf, rra, 20, self.rst[4])
            self.rope(p0[:, 0:512].rearrange("p (h d) -> p h d", h=8), [r0], t, 8,
                      qrot[:, 0:512].rearrange("p (h d) -> p h d", h=8), rqrot, ra, rb, rra, rrb)
            self.rope(p1[:, 0:128].rearrange("p (h d) -> p h d", h=2), [r1], t, 2,
                      qrot[:, 512:640].rearrange("p (h d) -> p h d", h=2), rqrot, ra, rb, rra, rrb)

        def s1D(t):
            (p0, r0), (p1, r1), (p2, r2) = kvsets[t % 2]
            for c in range(5):
                P.op("pe", lambda e: e.transpose(out=self.PT[:, c, :], in_=qrot[:, c * 128:(c + 1) * 128], identity=self.identb[:]),
                     reads=[rqrot, self.rConst], writes=[self.rPT])
            tc_ = slice(t * 128, (t + 1) * 128)
            P.op("act", lambda e: e.activation(out=KTA[0][0:64, tc_], in_=self.PT[0:64, 0, :], func=AF.Copy), reads=[self.rPT], writes=[rK])
            P.op("act", lambda e: e.activation(out=KTA[1][64:128, tc_], in_=self.PT[64:128, 0, :], func=AF.Copy), reads=[self.rPT], writes=[rK])
            P.op("act", lambda e: e.activation(out=KTB[:, :, tc_], in_=self.PT[:, 1:5, :], func=AF.Copy), reads=[self.rPT], writes=[rK])
            P.op("act", lambda e: e.activation(out=VA[:, t, 0:64], in_=p1[:, 128:192], func=AF.Copy), reads=[r1], writes=[rV])
            P.op("act", lambda e: e.activation(out=VA[:, t, 128:192], in_=p1[:, 192:256], func=AF.Copy), reads=[r1], writes=[rV])
            P.op("act", lambda e: e.activation(out=VB[:, t, 0:256], in_=p1[:, 256:512], func=AF.Copy), reads=[r1], writes=[rV])
            P.op("act", lambda e: e.activation(out=VB[:, t, 256:512], in_=p2[:, 0:256], func=AF.Copy), reads=[r2], writes=[rV])

        for th in pipe4(NT, lambda t: stA(t, "dve"), lambda t: stA(t, "pe"), s1B, lambda t: (s1C(t), s1D(t))):
            th()
        if SUB < 2:
            return
        for c in range(4):
            for g in range(2):
                h = g * 4 + c
                pos = c * 2 + g
                P.dma("pool", "wq", lambda e, h=h, pos=pos: e.dma_start(out=wq[:, :, pos * 64:(pos + 1) * 64], in_=kc_view(h * 64, (h + 1) * 64)), writes=[rwq])
        P.dma("pool", "wq", lambda e: e.dma_start(out=wq[:, :, 512:1024], in_=kc_view(768, 1280)), writes=[rwq])
        P.op("pool", lambda e: e.memset(QTBhi[0:64, :, :], 0.0), reads=[rwx], writes=[rwx])
        blocks = [[0, 1]] + [list(range(2 + 4 * q, 6 + 4 * q)) for q in range(4)]
        for blk in blocks:
            nq = len(blk) * 128
            kts = [0, 1] if blk[0] < 2 else list(range(NT))
            row = 2 if blk[0] < 2 else b
            qsets = [(self.PY[0], self.rPY[0]), (self.PY[1], self.rPY[1])]

            def qB(j):
                qp, rqp = qsets[j % 2]
                for half in range(2):
                    for kc in range(8):
                        P.op("pe", lambda e: e.matmul(qp[:, half * 512:(half + 1) * 512], lhsT=hTt[:, kc, :],
                                                      rhs=wq[:, kc, half * 512:(half + 1) * 512], start=(kc == 0), stop=(kc == 7)),
                             reads=[rhTt, rwq], writes=[rqp[half]])

            def qC(j):
                qp, rqp = qsets[j % 2]
                self.head_norm(qp[:, 0:512], [rqp[0]], 8, self.qgt, sqf, rra, 24, self.rst[5])
                self.rope(qp[:, :].rearrange("p (h d) -> p h d", h=16), [rqp[0], rqp[1]], blk[j], 16,
                          qrot[:, :].rearrange("p (h d) -> p h d", h=16), rqrot, ra, rb, rra, rrb)

            def qD(j):
                for c in range(8):
                    P.op("pe", lambda e: e.transpose(out=self.PT[:, c, :], in_=qrot[:, c * 128:(c + 1) * 128], identity=self.identb[:]),
                         reads=[rqrot, self.rConst], writes=[self.rPT])
                js = slice(j * 128, (j + 1) * 128)
                P.op("act", lambda e: e.activation(out=QT[:, 0:4, js], in_=self.PT[:, 0:4, :], func=AF.Copy), reads=[self.rPT], writes=[rQT])
                P.op("act", lambda e: e.activation(out=QT[0:64, 4:8, js], in_=self.PT[0:64, 4:8, :], func=AF.Copy), reads=[self.rPT], writes=[rQT])
                P.op("act", lambda e: e.activation(out=QTBhi[64:128, :, js], in_=self.PT[64:128, 4:8, :], func=AF.Copy), reads=[self.rPT], writes=[rwx])

            for th in pipe4(len(blk), lambda j: stA(blk[j], "dve"), lambda j: stA(blk[j], "pe"), qB, lambda j: (qC(j), qD(j))):
                th()
            if SUB < 3:
                continue
            jobs = []
            for c in range(4):
                for g in range(2):
                    h = g * 4 + c
                    ps_, ch = (h % 2) * 64, h // 2
                    ob, rsb = (0, 64) if g == 0 else (64, 0)

                    def fin(O, rO, Rs, rR, ps_=ps_, ch=ch, ob=ob, rsb=rsb):
                        P.op("dve", lambda e: e.reciprocal(out=rinv[0:64, 0:nq], in_=O[rsb:rsb + 64, 0:nq]), reads=[rO], writes=[rtmpf])
                        P.op("dve", lambda e: e.tensor_tensor(out=OT[ps_:ps_ + 64, ch, 0:nq], in0=O[ob:ob + 64, 0:nq], in1=rinv[0:64, 0:nq], op=ALU.mult),
                             reads=[rO, rtmpf], writes=[rOT])
                    jobs.append(dict(kT=lambda kt, g=g: KTA[g][:, kt * 128:(kt + 1) * 128], q=QT[:, c, 0:nq],
                                     v=lambda kt, g=g: VA[:, kt, g * 64:g * 64 + 128], nq=nq, kts=kts, rs_sep=False,
                                     rk=[rK], rq=[rQT], rv=[rV], fin=fin))
            for hb_ in range(4):
                for cm in range(2):
                    def fin(O, rO, Rs, rR, hb_=hb_, cm=cm):
                        P.op("dve", lambda e: e.reciprocal(out=rinv[:, 0:nq], in_=Rs[:, 0:nq]), reads=[rR], writes=[rtmpf])
                        if cm == 0:
                            P.op("dve", lambda e: e.tensor_tensor(out=o1[:, 0:nq], in0=O[:, 0:nq], in1=rinv[:, 0:nq], op=ALU.mult), reads=[rO, rtmpf], writes=[rtmpf])
                            return
                        P.op("dve", lambda e: e.tensor_tensor(out=rinv[:, 0:nq], in0=O[:, 0:nq], in1=rinv[:, 0:nq], op=ALU.mult), reads=[rO, rtmpf], writes=[rtmpf])
                        P.op("dve", lambda e: e.scalar_tensor_tensor(out=o1[:, 0:nq], in0=rinv[:, 0:nq], scalar=self.lamt[:, 4:5], in1=o1[:, 0:nq],
                                                                      op0=ALU.mult, op1=ALU.add), reads=[rtmpf, self.rlam], writes=[rtmpf])
                        P.op("pool", lambda e: e.tensor_tensor(out=dsq[:, 0:nq], in0=o1[:, 0:nq], in1=o1[:, 0:nq], op=ALU.mult), reads=[rtmpf], writes=[rrb])
                        ssd, rssd = self.PY[1][:, 512:1024], self.rPY[1][1]
                        P.op("pe", lambda e: e.matmul(ssd[:, 0:nq], lhsT=self.onesb[:, :], rhs=dsq[:, 0:nq], start=True, stop=True),
                             reads=[rrb, self.rConst], writes=[rssd])
                        P.op("dve", lambda e: e.tensor_scalar(out=rsd[:, 0:nq], in0=ssd[:, 0:nq], scalar1=1.0 / 128, scalar2=EPS, op0=ALU.mult, op1=ALU.add),
                             reads=[rssd], writes=[rra])
                        P.op("act", lambda e: e.activation(out=rsd[:, 0:nq], in_=rsd[:, 0:nq], func=AF.Sqrt), reads=[rra], writes=[rra])
                        P.op("dve", lambda e: e.reciprocal(out=rsd[:, 0:nq], in_=rsd[:, 0:nq]), reads=[rra], writes=[rra])
                        P.op("dve", lambda e: e.scalar_tensor_tensor(out=OT[:, 4 + hb_, 0:nq], in0=o1[:, 0:nq], scalar=self.sgc[:, 0:1], in1=rsd[:, 0:nq],
                                                                      op0=ALU.mult, op1=ALU.mult), reads=[rtmpf, rra, self.rConst], writes=[rOT])
                    qsrc, rq_ = (QT[:, 4 + hb_, 0:nq], rQT) if cm == 0 else (QTBhi[:, hb_, 0:nq], rwx)
                    jobs.append(dict(kT=lambda kt, hb_=hb_: KTB[:, hb_, kt * 128:(kt + 1) * 128], q=qsrc,
                                     v=lambda kt, hb_=hb_: VB[:, kt, hb_ * 128:(hb_ + 1) * 128], nq=nq, kts=kts, rs_sep=True,
                                     rk=[rK], rq=[rq_], rv=[rV], fin=fin))
            self.attend(jobs)
            if SUB < 4:
                continue
            for th in self.skewed(len(blk), [lambda j: self.outproj_mm(j, OT, rOT, wo, rwo, 8, yk=(j + 1) % 2),
                                             lambda j: self.outproj_post(blk[j], 0, row, 40 + (j % 2), yk=(j + 1) % 2)]):
                th()
        self._post_state = None

    def mixer_c(self, b):
        P = self.P
        self.ov_off = 0
        self._post_state = None
        T = NT * 128
        skb = self.ovv([128, 16, 128], F32)
        maskt = self.ovv([128, 2, 512], BF16)
        KTC = [self.ovv([128, 2, T], BF16), self.ovv([128, 2, T], BF16)]
        VC = self.ovv([128, NT, 384], BF16)
        wb = self.ovv([128, 8, 1024], BF16)
        wo = self.ovv([128, 8, D], BF16)
        tmpf = self.ovv([128, D], F32)
        tmpf2 = tmpf
        hb = self.ovv([128, D], BF16)
        hTts = [(self.ovv([128, 8, 128], BF16), Res("hT0")), (self.ovv([128, 8, 128], BF16), Res("hT1"))]
        qrots = [(self.ovv([128, D], BF16), Res("qr0"))] * 2
        ra = self.ovv([128, 16, 32], F32)
        rb = self.ovv([128, 16, 32], F32)
        QTs = [(self.ovv([128, 4, 8, 128], BF16), Res("QT0")), (self.ovv([128, 4, 8, 128], BF16), Res("QT1"))]
        OT = self.ovv([128, 4, 8, 128], BF16)
        self.Pring = [(self.ovv([128, 512], BF16), Res("p0")), (self.ovv([128, 512], BF16), Res("p1"))]
        rinv = self.ovv([128, 512], F32)
        rtmpf2 = Res("rinv")
        rK, rV, rwb, rwo = Res("K"), Res("V"), Res("wb"), Res("wo")
        rtmpf, rhb, rra, rrb, rOT, rsk = (Res(n) for n in ("tmpf", "hb", "ra", "rb", "OT", "sk"))
        rtmpfb = rtmpf
        if not CT6:
            qrots = [(hb, rhb)] * 2
        self.m_tmpf, self.m_rtmpf = tmpf, rtmpf
        skbf = skb.rearrange("p h t -> p (h t)")
        PAb = self.PT
        W = self.winc_d
        kc_view = lambda c0, c1: W[:, c0:c1].rearrange("(kc p) f -> p kc f", p=128)
        P.dma("pool", "wq", lambda e: e.dma_start(out=wb[:, :, 0:512], in_=kc_view(1024, 1536)), writes=[rwb])
        P.dma("pool", "wo", lambda e: e.dma_start(out=wo[:], in_=self.woutc_d.rearrange("(c p) d -> p c d", p=128)), writes=[rwo])
        P.dma("sp", "ld5", lambda e: e.dma_start(out=maskt[:], in_=self.mask_d[:, :, :]), writes=[rsk])
        P.op("dve", lambda e: e.tensor_copy(out=skb[:], in_=self.skt[:, :].unsqueeze(2).to_broadcast([128, 16, 128])), reads=[self.rConst], writes=[rsk])
        P.op("pool", lambda e: e.memset(KTC[0][64:128, :, :], 0.0), writes=[rK])
        P.op("pool", lambda e: e.memset(KTC[1][0:64, :, :], 0.0), writes=[rK])
        P.op("pool", lambda e: e.memset(VC[:, :, 64:128], 1.0), writes=[rV])
        P.op("pool", lambda e: e.memset(VC[:, :, 256:320], 1.0), writes=[rV])
        rk = self.rst[0]
        self.rstd_cols([(self.X[:, t, :], [self.rX[t]], hb[:], rhb) for t in range(NT)], 0, rk, 1.0 / D)
        self._cur_row = None

        def stA(t, j, part="all"):
            row = 2 if t < 2 else b
            if row != self._cur_row:
                self.prep_pre(1, 1, row)
                self._cur_row = row
            hTt, rhTt = hTts[(j % 2) * CT5]
            self.prenorm_tile(t, self.st[:, t:t + 1], rk, tmpf[:], rtmpf, hb[:], rhb, hTt[:], rhTt, part=part)

        def s1B(j):
            hTt, rhTt = hTts[(j % 2) * CT5]
            ps = self.PY[0][:, (j % 2) * CT4 * 512:(j % 2) * CT4 * 512 + 512]
            for kc in range(8):
                P.op("pe", lambda e: e.matmul(ps, lhsT=hTt[:, kc, :], rhs=wb[:, kc, 0:512], start=(kc == 0), stop=(kc == 7)),
                     reads=[rhTt, rwb], writes=[self.rPY[0][(j % 2) * CT4]])

        def s1C(j):
            t = j
            ps = self.PY[0][:, (j % 2) * CT4 * 512:(j % 2) * CT4 * 512 + 512]
            rps = self.rPY[0][(j % 2) * CT4]
            qrot, rqrot = qrots[j % 2]
            self.rope(ps[:, 0:256].rearrange("p (h d) -> p h d", h=4), [rps], t, 4,
                      qrot[:, 0:256].rearrange("p (h d) -> p h d", h=4), rqrot, ra, rb, rra, rrb)
            P.op("act", lambda e: e.activation(out=VC[:, t, 0:64], in_=ps[:, 256:320], func=AF.Copy), reads=[rps], writes=[rV])
            P.op("act", lambda e: e.activation(out=VC[:, t, 128:256], in_=ps[:, 320:448], func=AF.Copy), reads=[rps], writes=[rV])
            P.op("act", lambda e: e.activation(out=VC[:, t, 320:384], in_=ps[:, 448:512], func=AF.Copy), reads=[rps], writes=[rV])

        def s1D(j):
            t = j
            qrot, rqrot = qrots[j % 2]
            for c in range(2):
                P.op("pe", lambda e: e.transpose(out=PAb[:, c, :], in_=qrot[:, c * 128:(c + 1) * 128], identity=self.identb[:]),
                     reads=[rqrot, self.rConst], writes=[self.rPT])
            tc_ = slice(t * 128, (t + 1) * 128)
            P.op("act", lambda e: e.activation(out=KTC[0][0:64, :, tc_], in_=PAb[0:64, 0:2, :], func=AF.Copy), reads=[self.rPT], writes=[rK])
            P.op("act", lambda e: e.activation(out=KTC[1][64:128, :, tc_], in_=PAb[64:128, 0:2, :], func=AF.Copy), reads=[self.rPT], writes=[rK])

        if CT1:
            for th in self.skewed(NT, [lambda j: stA(j, j), s1B, s1C, s1D]):
                th()
        else:
            for j in range(NT):
                stA(j, j); s1B(j); s1C(j); s1D(j)
        for h in range(16):
            gp, e_, i_ = h // 8, (h % 8) // 4, h % 4
            pos = (gp * 4 + i_) * 2 + e_
            P.dma("pool", "wq", lambda e, h=h, pos=pos: e.dma_start(out=wb[:, :, pos * 64:(pos + 1) * 64], in_=kc_view(h * 64, (h + 1) * 64)), writes=[rwb])
        voff = (0, 64, 192, 256)
        blocks = [list(range(2 + 4 * q, 6 + 4 * q)) for q in range(4)]

        def qstages(n):
            QTn, rQTn = QTs[n % 2]
            blk = blocks[n]

            def qB(j):
                hTt, rhTt = hTts[(j % 2) * CT5]
                for half in range(2):
                    for kc in range(8):
                        P.op("pe", lambda e: e.matmul(self.PY[0][:, half * 512:(half + 1) * 512], lhsT=hTt[:, kc, :],
                                                      rhs=wb[:, kc, half * 512:(half + 1) * 512], start=(kc == 0), stop=(kc == 7)),
                             reads=[rhTt, rwb], writes=[self.rPY[0][half]])

            def qC(j):
                qrot, rqrot = qrots[j % 2]
                self.rope(self.PY[0][:, :].rearrange("p (h d) -> p h d", h=16), [self.rPY[0][0], self.rPY[0][1]], blk[j], 16,
                          qrot[:, :].rearrange("p (h d) -> p h d", h=16), rqrot, ra, rb, rra, rrb)

            def qD(j):
                qrot, rqrot = qrots[j % 2]
                for c in range(8):
                    P.op("pe", lambda e: e.transpose(out=self.PT[:, c, :], in_=qrot[:, c * 128:(c + 1) * 128], identity=self.identb[:]),
                         reads=[rqrot, self.rConst], writes=[self.rPT])
                P.op("act", lambda e: e.activation(out=QTn[:, j, :, :], in_=self.PT[:], func=AF.Copy), reads=[self.rPT], writes=[rQTn])
            if not CT7:
                return [lambda j=j, f=f: f(j) for j in range(4) for f in (lambda j: stA(blk[j], j), qB, qC, qD)]
            return self.skewed(4, [lambda j: stA(blk[j], j, "dve"), lambda j: stA(blk[j], j, "pe"), qB, qC, qD])

        for th in qstages(0):
            th()
        for n, blk in enumerate(blocks):
            QTn, rQTn = QTs[n % 2]
            QTf = QTn.rearrange("p j c t -> p j (c t)")
            jobs = []
            for j, t in enumerate(blk):
                kts = [0, 1] + [k for k in (t - 1, t, t + 1) if 2 <= k < NT]

                def mask(kt, t=t):
                    if kt == t - 1 and kt >= 2:
                        return maskt[:, 0, :]
                    if kt == t + 1:
                        return maskt[:, 1, :]
                    return None
                for g in range(4):
                    gp, e_ = g // 2, g % 2
                    ob, rsb = (0, 64) if e_ == 0 else (64, 0)

                    def fin(O, rO, Rs, rR, g=g, ob=ob, rsb=rsb, j=j):
                        P.op("dve", lambda e: e.tensor_tensor(out=rinv[0:64, :], in0=O[rsb:rsb + 64, 0:512], in1=skbf[0:64, g * 512:(g + 1) * 512], op=ALU.add),
                             reads=[rO, rsk], writes=[rtmpf2])
                        P.op("dve", lambda e: e.reciprocal(out=rinv[0:64, :], in_=rinv[0:64, :]), reads=[rtmpf2], writes=[rtmpf2])
                        for i_ in range(4):
                            h = 4 * g + i_
                            P.op("dve", lambda e: e.tensor_tensor(out=OT[(h % 2) * 64:(h % 2) * 64 + 64, j, h // 2, :], in0=O[ob:ob + 64, i_ * 128:(i_ + 1) * 128],
                                                                   in1=rinv[0:64, i_ * 128:(i_ + 1) * 128], op=ALU.mult), reads=[rO, rtmpf2], writes=[rOT])
                    jobs.append(dict(kT=lambda kt, gp=gp, e_=e_: KTC[e_][:, gp, kt * 128:(kt + 1) * 128],
                                     q=QTf[:, j, gp * 512:(gp + 1) * 512], v=lambda kt, g=g: VC[:, kt, voff[g]:voff[g] + 128],
                                     nq=512, kts=kts, mask=mask, rs_sep=False, rk=[rK, rsk], rq=[rQTn], rv=[rV], fin=fin))
            bg = qstages(n + 1) if n + 1 < len(blocks) else None
            if not CT2 and bg:
                for th in bg:
                    th()
                bg = None
            self.attend(jobs, mode="c", bg=bg, bg_every=4)
            if CT3:
                for th in self.skewed(4, [lambda j: self.outproj_mm(0, OT[:, j, :, :], rOT, wo, rwo, 8, yk=(j + 1) % 2),
                                          lambda j: self.outproj_post(blk[j], 1, b, 40 + (j % 2), yk=(j + 1) % 2, tmp=tmpf2, rtmp=rtmpfb)]):
                    th()
            else:
                for j in range(4):
                    self.outproj_mm(0, OT[:, j, :, :], rOT, wo, rwo, 8, yk=1)
                    self.outproj_post(blk[j], 1, b, 40 + (j % 2), yk=1, tmp=tmpf2, rtmp=rtmpfb)
        self._post_state = None


_NC_CACHE = {}


def _consts():
    ident_bf = np.eye(128, dtype=np.float32).astype(ml_dtypes.bfloat16)
    ident_f = np.eye(128, dtype=np.float32)
    pos = np.arange(L)
    row = (pos // 64).astype(np.float32)
    col = (pos % 64).astype(np.float32)
    inv = (10000.0 ** (-np.arange(0, 32, 2, dtype=np.float32) / 32)).astype(np.float32)
    ang = np.concatenate([row[:, None] * inv, col[:, None] * inv], axis=-1).astype(np.float32)
    cos = np.cos(ang).astype(np.float32)
    sin = np.sin(ang).astype(np.float32)
    cs = np.zeros((128, NT, 64), np.float32)
    cs[:, 0:2, 0:32] = 1.0
    cs[:, 2:, 0:32] = cos.reshape(16, 128, 32).transpose(1, 0, 2)
    cs[:, 2:, 32:64] = sin.reshape(16, 128, 32).transpose(1, 0, 2)
    kk = np.arange(128)[:, None]
    qq = np.arange(128)[None, :]
    m0 = np.where(kk >= qq, 0.0, -30000.0).astype(np.float32)
    m1 = np.where(kk <= qq, 0.0, -30000.0).astype(np.float32)
    maskb = np.stack([np.tile(m0, (1, 4)), np.tile(m1, (1, 4))], axis=1).astype(ml_dtypes.bfloat16)
    return dict(ident_bf=ident_bf, ident_f=ident_f, cs=cs, maskb=maskb)


def kernel(x, c, ctx, c_ctx, w_mod, b_mod, g_pre, g_post, w_ffn_gate, w_ffn_up, w_ffn_down,
           w_in_ab, w_out_ab, q_gain_a, k_gain_a, lam_q1, lam_k1, lam_q2, lam_k2, sub_gain_b,
           w_in_c, w_out_c, sink_c):
    f = lambda a: np.ascontiguousarray(np.asarray(a, dtype=np.float32))
    x, c, ctx, c_ctx = f(x), f(c), f(ctx), f(c_ctx)
    if "nc" not in _NC_CACHE:
        _NC_CACHE["nc"] = Builder().build()
    nc = _NC_CACHE["nc"]
    cst = _consts()
    shared = dict(
        w_mod=f(w_mod), b_mod=f(b_mod), g_pre=f(g_pre), g_post=f(g_post),
        w_ffn_gate=f(w_ffn_gate), w_ffn_up=f(w_ffn_up), w_ffn_down=f(w_ffn_down),
        w_in_ab=f(w_in_ab)[0], w_out_ab=f(w_out_ab)[0], q_gain_a=f(q_gain_a), k_gain_a=f(k_gain_a),
        lam4=np.ascontiguousarray(np.concatenate([f(lam_q1), f(lam_k1), f(lam_q2), f(lam_k2)], axis=0)),
        sub_gain_b=np.ascontiguousarray(f(sub_gain_b).reshape(128, 1)),
        w_in_c=f(w_in_c)[0], w_out_c=f(w_out_c)[0], sink_c=f(sink_c), **cst)
    in_maps = []
    for k in range(8):
        m = dict(shared)
        m["x"] = x[NB * k:NB * (k + 1)]
        m["ctx"] = ctx[NB * k:NB * (k + 1)]
        m["cvec"] = np.ascontiguousarray(np.concatenate([c[NB * k:NB * (k + 1)], c_ctx[None, :]], axis=0))
        in_maps.append(m)
    res = run_bass_kernel_spmd(nc, in_maps, core_ids=list(range(8)))
    return np.concatenate([r["out"] for r in res.results], axis=0)
```
